# Optimizing a Trainium2 kernel written in Bass

```python
import jax, jax.numpy as jnp
from jax import lax
import numpy as np

D_MODEL = 1024
BATCH = 16
SEQ = 256
DEPTH = 4
DEC_BATCH = 4
DEC_SEQ = 2048
PAST_LEN = 256

GRID_W = 64
N_HEADS = 8
N_KV_HEADS = 2
HEAD_DIM = 64
GQA_GROUP = N_HEADS // N_KV_HEADS
ATTN_WIDTH = N_HEADS * HEAD_DIM
KV_WIDTH = N_KV_HEADS * HEAD_DIM
POOL_WIDTH = D_MODEL - ATTN_WIDTH
POOL_WINDOWS = (2, 4, 8, 16)
N_POOL_GROUPS = len(POOL_WINDOWS)
POOL_GROUP_W = POOL_WIDTH // N_POOL_GROUPS
IN_WIDTH = ATTN_WIDTH + 2 * KV_WIDTH + ATTN_WIDTH + 2 * POOL_WIDTH
WINDOW = 128
BLOCK = 128
SPAN = BLOCK + 2 * WINDOW
AXIS_DIM = HEAD_DIM // 2
ROPE_THETA = 10000.0
EPS = 1e-6
NEG_INF = -1e30

kernel_name = "hybrid_dit_swa_pool_step"


def _rmsnorm(x, w):
    xf = x.astype(jnp.float32)
    r = lax.rsqrt(jnp.mean(xf * xf, axis=-1, keepdims=True) + EPS)
    return (xf * r).astype(x.dtype) * w


def _rope_2d(x, rows):
    row = jnp.repeat(jnp.arange(rows), GRID_W).astype(jnp.float32)
    col = jnp.tile(jnp.arange(GRID_W), rows).astype(jnp.float32)
    inv = ROPE_THETA ** (-(jnp.arange(0, AXIS_DIM, 2, dtype=jnp.float32) / AXIS_DIM))

    def rot(xa, pos):
        ang = pos[:, None] * inv[None, :]
        cos = jnp.cos(ang)[None, :, None, :].astype(xa.dtype)
        sin = jnp.sin(ang)[None, :, None, :].astype(xa.dtype)
        x1, x2 = xa[..., : AXIS_DIM // 2], xa[..., AXIS_DIM // 2:]
        return jnp.concatenate([x1 * cos - x2 * sin, x2 * cos + x1 * sin], axis=-1)

    return jnp.concatenate([rot(x[..., :AXIS_DIM], row), rot(x[..., AXIS_DIM:], col)], axis=-1)


def _attend(qb, k, v, mask, sink):
    s = jnp.einsum("bqkgd,bnkd->bkgqn", qb, k).astype(jnp.float32) * (HEAD_DIM ** -0.5)
    if mask is not None:
        s = jnp.where(mask[None, None, None], s, NEG_INF)
    sk = sink.astype(jnp.float32)[None, :, :, None, None]
    m = jnp.maximum(jnp.max(s, axis=-1, keepdims=True), sk)
    e = jnp.exp(s - m)
    p = e / (jnp.sum(e, axis=-1, keepdims=True) + jnp.exp(sk - m))
    return jnp.einsum("bkgqn,bnkd->bqkgd", p.astype(v.dtype), v)


def _context_attention(q, k, v, sink):
    B, L = q.shape[0], q.shape[1]
    nb = L // BLOCK
    qs = jnp.moveaxis(q.reshape(B, nb, BLOCK, N_KV_HEADS, GQA_GROUP, HEAD_DIM), 1, 0)
    sk = sink.reshape(N_KV_HEADS, GQA_GROUP)
    out = lax.map(lambda qb: _attend(qb, k, v, None, sk), qs)
    return jnp.moveaxis(out, 0, 1).reshape(B, L, ATTN_WIDTH)


def _latent_attention(q, k, v, kc, vc, sink):
    B, L = q.shape[0], q.shape[1]
    Lc = kc.shape[1]
    nb = L // BLOCK
    pad = ((0, 0), (WINDOW, WINDOW), (0, 0), (0, 0))
    kp, vp = jnp.pad(k, pad), jnp.pad(v, pad)
    qs = jnp.moveaxis(q.reshape(B, nb, BLOCK, N_KV_HEADS, GQA_GROUP, HEAD_DIM), 1, 0)
    sk = sink.reshape(N_KV_HEADS, GQA_GROUP)
    qi = jnp.arange(BLOCK)[:, None]
    rj = jnp.arange(SPAN)[None, :]
    ctx_mask = jnp.ones((BLOCK, Lc), dtype=bool)

    def one(args):
        b, qb = args
        start = b * BLOCK
        kb = lax.dynamic_slice_in_dim(kp, start, SPAN, axis=1)
        vb = lax.dynamic_slice_in_dim(vp, start, SPAN, axis=1)
        qpos = start + qi
        kpos = start - WINDOW + rj
        band = (jnp.abs(qpos - kpos) <= WINDOW) & (kpos >= 0) & (kpos < L)
        mask = jnp.concatenate([band, ctx_mask], axis=1)
        return _attend(qb, jnp.concatenate([kb, kc], axis=1),
                       jnp.concatenate([vb, vc], axis=1), mask, sk)

    out = lax.map(one, (jnp.arange(nb), qs))
    return jnp.moveaxis(out, 0, 1).reshape(B, L, ATTN_WIDTH)


def _pool_mixer(u, w_pool, pool_scale):
    B, L = u.shape[0], u.shape[1]
    t = jnp.arange(L)
    ug = u.reshape(B, L, N_POOL_GROUPS, POOL_GROUP_W)
    outs = []
    for g, w in enumerate(POOL_WINDOWS):
        xg = ug[:, :, g].astype(jnp.float32)
        cs = jnp.concatenate([jnp.zeros((B, 1, POOL_GROUP_W), jnp.float32),
                              lax.cumsum(xg, axis=1)], axis=1)
        lo = jnp.clip(t - w // 2, 0, L)
        hi = jnp.clip(t + w // 2, 0, L)
        mean = (cs[:, hi] - cs[:, lo]) / (hi - lo).astype(jnp.float32)[None, :, None]
        outs.append(mean - xg)
    pooled = jnp.stack(outs, axis=2).astype(u.dtype)
    y = jnp.einsum("blgc,gcd->blgd", pooled, w_pool).reshape(B, L, POOL_WIDTH)
    return y * pool_scale


def _modulated_projection(x, mod, norm_w, w_in):
    shift, scale, gate = jnp.split(mod, 3, axis=-1)
    h = _rmsnorm(x, norm_w) * (1.0 + scale) + shift
    p = h @ w_in
    o1 = ATTN_WIDTH
    o2 = o1 + KV_WIDTH
    o3 = o2 + KV_WIDTH
    o4 = o3 + ATTN_WIDTH
    o5 = o4 + POOL_WIDTH
    q, k, v, ga, u, gp = jnp.split(p, [o1, o2, o3, o4, o5], axis=-1)
    B, L = x.shape[0], x.shape[1]
    q = q.reshape(B, L, N_HEADS, HEAD_DIM)
    k = k.reshape(B, L, N_KV_HEADS, HEAD_DIM)
    v = v.reshape(B, L, N_KV_HEADS, HEAD_DIM)
    return q, k, v, ga, u, gp, gate


def _merge_output(x, attn_out, ga, pool_out, gp, gate, attn_norm, pool_norm, w_out):
    a = _rmsnorm(attn_out, attn_norm) * jax.nn.silu(ga)
    pl = _rmsnorm(pool_out, pool_norm) * jax.nn.silu(gp)
    y = jnp.concatenate([a, pl], axis=-1) @ w_out
    return x + gate * y


def setup_inputs(seed: int = 0) -> dict:
    key = jax.random.key(seed)
    ks = jax.random.split(key, 20)
    f32 = jnp.float32
    D = D_MODEL
    cache_shape = (DEC_BATCH, DEPTH, PAST_LEN, N_KV_HEADS, HEAD_DIM)
    return {
        "x_prompt": jax.random.normal(ks[0], (BATCH, SEQ, D), f32),
        "x_sample": jax.random.normal(ks[1], (DEC_BATCH, DEC_SEQ, D), f32),
        "cache_k": jax.random.normal(ks[2], cache_shape, f32),
        "cache_v": jax.random.normal(ks[3], cache_shape, f32),
        "c": jax.random.normal(ks[4], (DEC_BATCH, D), f32),
        "c_ctx": jax.random.normal(ks[5], (D,), f32),
        "norm_w": 1.0 + 0.02 * jax.random.normal(ks[6], (DEPTH, D), f32),
        "w_ada": 0.5 * D ** -0.5 * jax.random.normal(ks[7], (DEPTH, D, 3 * D), f32),
        "b_ada": 0.02 * jax.random.normal(ks[8], (DEPTH, 3 * D), f32),
        "w_in": D ** -0.5 * jax.random.normal(ks[9], (DEPTH, D, IN_WIDTH), f32),
        "sink": 0.5 * jax.random.normal(ks[10], (DEPTH, N_HEADS), f32),
        "attn_norm": 1.0 + 0.02 * jax.random.normal(ks[11], (DEPTH, ATTN_WIDTH), f32),
        "pool_norm": 1.0 + 0.02 * jax.random.normal(ks[12], (DEPTH, POOL_WIDTH), f32),
        "w_pool": POOL_GROUP_W ** -0.5 * jax.random.normal(
            ks[13], (DEPTH, N_POOL_GROUPS, POOL_GROUP_W, POOL_GROUP_W), f32),
        "pool_scale": 1.0 + 0.1 * jax.random.normal(ks[14], (DEPTH, POOL_WIDTH), f32),
        "w_out": D ** -0.5 * jax.random.normal(ks[15], (DEPTH, D, D), f32),
        "final_norm": 1.0 + 0.02 * jax.random.normal(ks[16], (D,), f32),
    }


def reference(x_prompt, x_sample, cache_k, cache_v, c, c_ctx, norm_w, w_ada, b_ada,
              w_in, sink, attn_norm, pool_norm, w_pool, pool_scale, w_out, final_norm):
    silu_ctx = jax.nn.silu(c_ctx)
    silu_c = jax.nn.silu(c)
    xp, xs = x_prompt, x_sample
    rows = xs.shape[1] // GRID_W
    new_k, new_v = [], []
    for l in range(DEPTH):
        mod_ctx = (silu_ctx @ w_ada[l] + b_ada[l])[None, None, :]
        mod_lat = (silu_c @ w_ada[l] + b_ada[l])[:, None, :]

        q, k, v, ga, u, gp, gate = _modulated_projection(xp, mod_ctx, norm_w[l], w_in[l])
        attn = _context_attention(q, k, v, sink[l])
        pool = _pool_mixer(u, w_pool[l], pool_scale[l])
        xp = _merge_output(xp, attn, ga, pool, gp, gate, attn_norm[l], pool_norm[l], w_out[l])
        new_k.append(k)
        new_v.append(v)

        q, k, v, ga, u, gp, gate = _modulated_projection(xs, mod_lat, norm_w[l], w_in[l])
        q = _rope_2d(q, rows)
        k = _rope_2d(k, rows)
        attn = _latent_attention(q, k, v, cache_k[:, l], cache_v[:, l], sink[l])
        pool = _pool_mixer(u, w_pool[l], pool_scale[l])
        xs = _merge_output(xs, attn, ga, pool, gp, gate, attn_norm[l], pool_norm[l], w_out[l])

    y_prompt = _rmsnorm(xp, final_norm)
    y_sample = _rmsnorm(xs, final_norm)
    k_state = jnp.stack(new_k, axis=1)
    v_state = jnp.stack(new_v, axis=1)
    return (y_prompt, y_sample, k_state, v_state)
```

```python
from contextlib import ExitStack
import numpy as np
import ml_dtypes
import concourse.bass as bass
import concourse.mybir as mybir
from concourse.bass_utils import run_bass_kernel_spmd

F32 = mybir.dt.float32
BF16 = mybir.dt.bfloat16
AF = mybir.ActivationFunctionType
ALU = mybir.AluOpType

D = 1024
DEPTH = 4
NCORES = 8
EPS = 1e-6
NTOK = 2048
NBLK = 16
KT_COLS = NTOK + 256
NRING = 2
DEBUG_DUMP = False
STOP = 99


class StopBuild(Exception):
    pass


def stage_gate(n):
    if n > STOP:
        raise StopBuild()

ENGS = ("pe", "act", "dve", "pool", "sp")


class Op:
    __slots__ = ("eng", "fn", "deps", "signal", "tick", "dma_sem", "dma_val")

    def __init__(self, eng, fn):
        self.eng = eng
        self.fn = fn
        self.deps = []
        self.signal = False
        self.tick = None
        self.dma_sem = None
        self.dma_val = None


class Prog:
    def __init__(self, nc):
        self.nc = nc
        self.ops = {e: [] for e in ENGS}
        self.res = {}
        self.dma_tot = {}

    def op(self, eng, fn, reads=(), writes=(), accum=False, dma_sem=None):
        o = Op(eng, fn)
        deps = []
        for r in reads:
            st = self.res.get(r)
            if st is not None and st[0] is not None:
                deps.append(("raw", st[0]))
            if st is not None and isinstance(r, tuple) and r[0] == "ps":
                for rd in st[1]:
                    if rd.eng != eng:
                        deps.append(("raw", rd))
        for w in writes:
            st = self.res.get(w)
            if st is not None:
                if st[0] is not None and not accum:
                    deps.append(("waw", st[0]))
                for rd in st[1]:
                    deps.append(("war", rd))
        seen = set()
        for kind, d in deps:
            if d is o or id(d) in seen:
                continue
            if d.eng == eng and d.dma_sem is None and dma_sem is None:
                if eng == "pe":
                    continue
            seen.add(id(d))
            o.deps.append(d)
        for r in reads:
            self.res.setdefault(r, [None, []])[1].append(o)
        for w in writes:
            st = self.res.setdefault(w, [None, []])
            st[0] = o
            st[1] = []
        if dma_sem is not None:
            o.dma_sem = dma_sem
            self.dma_tot[dma_sem] = self.dma_tot.get(dma_sem, 0) + 16
            o.dma_val = self.dma_tot[dma_sem]
        self.ops[eng].append(o)
        return o

    def emit(self, final_wait_ops=()):
        nc = self.nc
        for e in ENGS:
            for o in self.ops[e]:
                for d in o.deps:
                    d.signal = True
        for o in final_wait_ops:
            o.signal = True
        for e in ENGS:
            t = 0
            for o in self.ops[e]:
                if o.dma_sem is None and o.signal:
                    t += 1
                    o.tick = t
        with ExitStack() as es:
            sems = {}
            for e in ENGS:
                sems[e] = es.enter_context(nc.semaphore("s_" + e))
            for k in self.dma_tot:
                sems[("dma", k)] = es.enter_context(nc.semaphore("d_" + str(k)))
            block = es.enter_context(nc.Block())

            def run(eng_name, e):
                waited = {}
                for o in self.ops[eng_name]:
                    need = {}
                    for d in o.deps:
                        if d.dma_sem is not None:
                            key, val = ("dma", d.dma_sem), d.dma_val
                        else:
                            key, val = d.eng, d.tick
                        if need.get(key, 0) < val:
                            need[key] = val
                    for key, val in need.items():
                        if waited.get(key, 0) < val:
                            e.wait_ge(sems[key], val)
                            waited[key] = val
                    inst = o.fn(e)
                    if o.dma_sem is not None:
                        inst.then_inc(sems[("dma", o.dma_sem)], 16)
                    elif o.signal:
                        inst.then_inc(sems[eng_name], 1)
                if eng_name == "sp":
                    need = {}
                    for o in final_wait_ops:
                        if o.dma_sem is not None:
                            key, val = ("dma", o.dma_sem), o.dma_val
                        else:
                            key, val = o.eng, o.tick
                        need[key] = max(need.get(key, 0), val)
                    for key, val in need.items():
                        e.wait_ge(sems[key], val)

            @block.tensor
            def _(e):
                run("pe", e)

            @block.scalar
            def _(e):
                run("act", e)

            @block.vector
            def _(e):
                run("dve", e)

            @block.gpsimd
            def _(e):
                run("pool", e)

            @block.sync
            def _(e):
                run("sp", e)


def _aperm():
    idx = []
    for ac in range(4):
        g, j = ac // 2, ac % 2
        for h in (4 * g + j, 4 * g + 2 + j):
            idx += [h * 64 + d for d in range(64)]
    return np.array(idx)


def _qperm():
    idx = []
    for c in range(4):
        for h in (c, 4 + c):
            idx += [h * 64 + d for d in range(64)]
    return np.array(idx)


def _pool_op(L, w, n):
    M = np.zeros((n, n + 16), np.float64)
    for t in range(n):
        lo = min(max(t - w // 2, 0), L)
        hi = min(max(t + w // 2, 0), L)
        for s in range(lo, hi):
            if s < n + 16:
                M[t, s] += 1.0 / (hi - lo)
        M[t, t] -= 1.0
    return M


def _pool_tables(reverse):
    pd = np.zeros((128, 4, 4, 128), np.float32)
    pe = np.zeros((128, 4, 4, 16), np.float32)
    for g, w in enumerate((2, 4, 8, 16)):
        M = np.zeros((256, 256))
        for t in range(256):
            lo = min(max(t - w // 2, 0), 256)
            hi = min(max(t + w // 2, 0), 256)
            M[t, lo:hi] += 1.0 / (hi - lo)
            M[t, t] -= 1.0
        pd[:, 0, g, :] = M[0:128, 0:128].T
        pd[:, 1, g, :] = M[128:256, 128:256].T
        pe[:, 0, g, :] = M[128:144, 0:128].T
        pe[:, 1, g, :] = M[112:128, 128:256].T
        L = 2048
        Mg = np.zeros((L, L))
        for t in range(L):
            lo = min(max(t - w // 2, 0), L)
            hi = min(max(t + w // 2, 0), L)
            Mg[t, lo:hi] += 1.0 / (hi - lo)
            Mg[t, t] -= 1.0
        Ml = Mg[::-1, ::-1] if reverse else Mg
        pd[:, 2, g, :] = Ml[0:128, 0:128].T
        pd[:, 3, g, :] = Ml[128:256, 128:256].T
        pe[:, 2, g, :] = Ml[128:144, 0:128].T
        pe[:, 3, g, :] = Ml[240:256, 256:384].T
    return pd, pe


def _rope_tables(reverse):
    j = np.arange(1536)
    t = (2047 - j) if reverse else j
    row = (t // 64).astype(np.float32)
    col = (t % 64).astype(np.float32)
    inv = (10000.0 ** (-(np.arange(0, 32, 2, dtype=np.float32) / 32))).astype(np.float32)
    cos = np.zeros((128, 1536), np.float32)
    sin = np.zeros((128, 1536), np.float32)
    for p in range(128):
        d = p % 64
        pos = row if d < 32 else col
        ang = (pos * inv[d % 16]).astype(np.float32)
        cos[p] = np.cos(ang)
        sin[p] = np.sin(ang)
    return cos, sin


def _perm_matrix():
    pm = np.zeros((128, 128), np.float32)
    for jx in range(128):
        if jx % 32 < 16:
            pm[jx + 16, jx] = -1.0
        else:
            pm[jx - 16, jx] = 1.0
    return pm


def _masks():
    k = np.arange(128)[:, None]
    q = np.arange(128)[None, :]
    m = np.zeros((128, 2, 4, 128), np.float32)
    m[:, 0] = (k >= q)[:, None, :]
    m[:, 1] = (k <= q)[:, None, :]
    return m


def build_program():
    nc = bass.Bass("TRN2", target_bir_lowering=False)

    def din(name, shape, dt=F32):
        return nc.dram_tensor(name, list(shape), dt, kind="ExternalInput").ap()

    def dout(name, shape, dt=F32):
        return nc.dram_tensor(name, list(shape), dt, kind="ExternalOutput").ap()

    xp_d = din("xp", [512, D])
    xs_d = din("xs", [1536, D])
    ck_d = din("ck", [DEPTH, 256, 128])
    cv_d = din("cv", [DEPTH, 256, 128])
    wada_d = din("w_ada", [DEPTH, D, 3 * D])
    win_d = din("w_in", [DEPTH, D, 2304])
    wout_d = din("w_out", [DEPTH, D, D])
    wpool_d = din("w_pool", [DEPTH, 4, 128, 128])
    vecs_d = din("vecs", [256, 128])
    sink_d = din("sinkv", [1, 32])
    fn_d = din("fnorm", [1, D])
    identf_d = din("ident_f", [128, 128])
    identb_d = din("ident_b", [128, 128], BF16)
    perm_d = din("perm", [128, 128], BF16)
    cos_d = din("cosT", [128, 1536], BF16)
    sin_d = din("sinT", [128, 1536], BF16)
    mask_d = din("mask2", [128, 2, 128], BF16)
    pmd_d = din("pm_diag", [128, 4, 4, 128], BF16)
    pme_d = din("pm_edge", [128, 4, 4, 16], BF16)

    yp_d = dout("yp", [512, D])
    ys_d = dout("ys", [1024, D])
    nk_d = dout("nk", [DEPTH, 512, 128])
    nv_d = dout("nv", [DEPTH, 512, 128])
    if DEBUG_DUMP:
        dbg_d = dout("dbg", [DEPTH, 128, 8, NTOK])

    es = ExitStack()

    def sb(name, shape, dt=F32):
        return es.enter_context(nc.sbuf_tensor(name, list(shape), dt))

    psum = es.enter_context(nc.psum_tensor("psum", [128, 4096], F32))

    def bank(b, n=512, off=0):
        return psum[:, b * 512 + off: b * 512 + off + n]

    xT = sb("xT", [128, 8, NTOK])
    hT = sb("hT", [128, 8, NTOK], BF16)
    qT = sb("qT", [128, 4, NTOK], BF16)
    zT = sb("zT", [128, 4, NTOK], BF16)
    kT = sb("kT", [128, KT_COLS], BF16)
    NVA = 64 + 128 * 36
    vaug = sb("vaug", [128, NVA], BF16)
    ring = sb("ring", [128, NRING, 8, 512], BF16)
    AR = sb("arena", [128, 8192], BF16)
    zf = AR[:, 0:4096].bitcast(F32).rearrange("p (g t) -> p g t", g=4)
    stage = AR[:, 0:4096].bitcast(F32).rearrange("p (s t) -> p s t", s=2)
    pooledT = AR[:, 4096:6144].rearrange("p (g t) -> p g t", g=4)
    utm = AR[:, 6144:8192].rearrange("p (s t) -> p s t", s=4)
    PT = AR[:, 0:3072].rearrange("p (s t) -> p s t", s=3)
    atf = AR[:, 3072:5120].bitcast(F32).rearrange("p (a c t) -> p a c t", a=2, c=4)
    lnt = AR[:, 5120:7168].bitcast(F32).rearrange("p (a t) -> p a t", a=2)
    sqa = AR[:, 7168:7680].rearrange("p (c t) -> p c t", c=4)
    rb2 = AR[:, 7680:7936].bitcast(F32)
    vec_in = AR[:, 7168:7680].bitcast(F32).rearrange("p (s t) -> p s t", s=2)
    sq = sb("sq", [128, 4, 512], BF16)
    rb = sb("rb", [128, 512])
    tmpf = sb("tmpf", [128, 3, 512])
    q16 = sb("q16", [128, 2, 512], BF16)
    kvf = sb("kvf", [128, 256])
    ckt = sb("ckt", [128, 2, 128], BF16)
    cosT = sb("cosT_sb", [128, 1536], BF16)
    sinT = sb("sinT_sb", [128, 1536], BF16)
    mask2 = sb("mask2_sb", [128, 2, 128], BF16)
    pmd = sb("pmd_sb", [128, 4, 4, 128], BF16)
    pme = sb("pme_sb", [128, 4, 4, 16], BF16)
    wp = sb("wp_sb", [128, 2, 4, 128], BF16)
    identf = sb("identf_sb", [128, 128])
    identb = sb("identb_sb", [128, 128], BF16)
    onesb = sb("onesb", [128, 128], BF16)
    perm = sb("perm_sb", [128, 128], BF16)
    vec = sb("vec", [128, 256])
    esink = sb("esink", [128, 32])
    sT = sb("sT", [128, 8, 2], BF16)
    mods = sb("mods", [128, 2, 24, 2])
    gmul = sb("gmul", [128, 2, 8, 2])
    ssq = sb("ssq", [128, 4])
    selr = sb("selr", [1, 128], BF16)
    esb = sb("esb", [128, 2, 512], BF16)
    fnb = hT[:, 0, :].bitcast(F32)

    P = Prog(nc)
    cnt = {"dma": 0, "ring": 0, "ps_norm": 0, "evac": 0, "rope": 0}

    def MM(out, lhsT, rhs, start, stop, reads, writes, accum=False):
        return P.op("pe", lambda e: e.matmul(out, lhsT=lhsT, rhs=rhs, start=start, stop=stop),
                    reads=reads, writes=writes, accum=accum)

    def TR(out, in_, ident, reads, writes):
        return P.op("pe", lambda e: e.transpose(out=out, in_=in_, identity=ident), reads=reads, writes=writes)

    def ACT(out, in_, func, reads, writes, bias=None, scale=None, accum_out=None):
        kw = {}
        if bias is not None:
            kw["bias"] = bias
        if scale is not None:
            kw["scale"] = scale
        if accum_out is not None:
            kw["accum_out"] = accum_out
        return P.op("act", lambda e: e.activation(out=out, in_=in_, func=func, **kw), reads=reads, writes=writes)

    def ACOPY(out, in_, reads, writes):
        return P.op("act", lambda e: e.copy(out=out, in_=in_), reads=reads, writes=writes)

    def VCOPY(out, in_, reads, writes):
        return P.op("dve", lambda e: e.tensor_copy(out=out, in_=in_), reads=reads, writes=writes)

    def TT(out, in0, in1, op, reads, writes):
        return P.op("dve", lambda e: e.tensor_tensor(out=out, in0=in0, in1=in1, op=op), reads=reads, writes=writes)

    def STT(out, in0, scalar, in1, op0, op1, reads, writes):
        return P.op("dve", lambda e: e.scalar_tensor_tensor(out=out, in0=in0, scalar=scalar, in1=in1, op0=op0, op1=op1),
                    reads=reads, writes=writes)

    def TS(out, in0, scalar1, op0, reads, writes):
        return P.op("dve", lambda e: e.tensor_scalar(out=out, in0=in0, scalar1=scalar1, scalar2=None, op0=op0),
                    reads=reads, writes=writes)

    def DMA(eng, out, in_, reads, writes, sem):
        return P.op(eng, lambda e: e.dma_start(out=out, in_=in_), reads=reads, writes=writes, dma_sem=sem)

    def newsem(prefix):
        cnt["dma"] += 1
        return "%s%d" % (prefix, cnt["dma"])

    def c_bada(l):
        return vec[:, l * 24:(l + 1) * 24]

    def c_normw(l):
        return vec[:, 96 + l * 8: 96 + l * 8 + 8]

    def c_an(l, j):
        return vec[:, 128 + l * 4 + j: 128 + l * 4 + j + 1]

    def c_pn(l, j):
        return vec[:, 144 + l * 4 + j: 144 + l * 4 + j + 1]

    def c_psc(l, j):
        return vec[:, 160 + l * 4 + j: 160 + l * 4 + j + 1]

    def kx(cs, bs):
        return [("x", c, b) for c in cs for b in bs]

    def kh(cs, bs):
        return [("h", c, b) for c in cs for b in bs]

    def kq(cs, bs):
        return [("q", c, b) for c in cs for b in bs]

    def kz(cs, bs):
        return [("z", c, b) for c in cs for b in bs]

    def kk(bs):
        return [("k", b) for b in bs]

    def kv(bs):
        return [("v", b) for b in bs]

    def tok(b0, b1):
        return slice(b0 * 128, b1 * 128)

    DMA("sp", identf[:], identf_d, [], ["identf"], newsem("c"))
    DMA("sp", identb[:], identb_d, [], ["identb"], newsem("c"))
    DMA("sp", perm[:], perm_d, [], ["perm"], newsem("c"))
    DMA("sp", vec_in[:, 0, :], vecs_d[0:128, :], [], ["vec_in0"], newsem("c"))
    DMA("sp", vec_in[:, 1, :], vecs_d[128:256, :], [], ["vec_in1"], newsem("c"))
    DMA("sp", esink[:], sink_d.broadcast_to([128, 32]), [], ["esink"], newsem("c"))
    DMA("sp", cosT[:], cos_d, [], ["cos"], newsem("c"))
    DMA("sp", sinT[:], sin_d, [], ["sin"], newsem("c"))
    DMA("sp", mask2[:], mask_d, [], ["mask2"], newsem("c"))
    DMA("sp", pmd[:], pmd_d, [], ["pmd"], newsem("c"))
    DMA("sp", pme[:], pme_d, [], ["pme"], newsem("c"))
    P.op("dve", lambda e: e.memset(onesb[:], 1.0), writes=["onesb"])
    P.op("dve", lambda e: e.memset(selr[:, 0:64], 0.0), writes=["selr"])
    P.op("dve", lambda e: e.memset(selr[:, 64:128], 1.0), writes=["selr"])
    P.op("dve", lambda e: e.memset(vaug[:], 1.0), writes=kv(range(18)))

    for i in range(2):
        TR(bank(7, 128, i * 128), vec_in[:, i, :], identf[:], ["vec_in%d" % i, "identf"], [("ps", 7)])
    ACOPY(vec[:], bank(7, 256), [("ps", 7)], ["vec"])
    ACT(sT[:, :, 0], vec[:, 176:184], AF.Silu, ["vec"], ["sT"])
    ACT(sT[:, :, 1], vec[:, 184:192], AF.Silu, ["vec"], ["sT"])
    ACT(esink[:], esink[:], AF.Exp, ["esink"], ["esink"])

    def ring_load(src_ap, ncols, slot):
        src = src_ap.rearrange("(k p) c -> p k c", p=128)
        DMA("pool", ring[:, slot, :, 0:ncols], src, [], [("ring", slot)], "ring%d" % slot)
        return slot

    MODB = 3

    def emit_ada_granule(l, gi, slot):
        ring_load(wada_d[l][:, gi * 512:(gi + 1) * 512], 512, slot)
        pm = l % 2
        for jc in range(4):
            for k in range(8):
                MM(bank(7, 2, jc * 2), ring[:, slot, k, jc * 128:(jc + 1) * 128], sT[:, k, :], k == 0, k == 7,
                   [("ring", slot), "sT"], [("ps", 7)], accum=(k > 0))
        pv = bank(7, 8).rearrange("p (j v) -> p j v", v=2)
        for v in range(2):
            TT(mods[:, pm, gi * 4:(gi + 1) * 4, v], pv[:, :, v], vec[:, l * 24 + gi * 4: l * 24 + gi * 4 + 4], ALU.add,
               [("ps", 7), "vec"], [("mods", pm)])

    def emit_mods_finish(l):
        pm = l % 2
        for v in range(2):
            STT(gmul[:, pm, :, v], mods[:, pm, 8:16, v], 1.0, c_normw(l), ALU.add, ALU.mult,
                [("mods", pm), "vec"], [("gmul", pm)])

    if STOP >= 1:
        for gi in range(6):
            emit_ada_granule(0, gi, gi % 2)
        emit_mods_finish(0)

    for b in (range(NBLK) if STOP >= 2 else []):
        st = b % 2
        src = xp_d[b * 128:(b + 1) * 128, :] if b < 4 else xs_d[(b - 4) * 128:(b - 3) * 128, :]
        DMA("sp", stage[:, st, :], src, [], [("stage", st)], "xin%d" % st)
        pb = (b % 2) * 2
        for c in range(8):
            TR(bank(pb + c // 4, 128, (c % 4) * 128), stage[:, st, c * 128:(c + 1) * 128], identf[:],
               [("stage", st), "identf"], [("ps", pb + c // 4)])
        src_ps = psum[:, pb * 512: pb * 512 + 1024].rearrange("p (c t) -> p c t", c=8)
        if b % 2 == 0:
            ACOPY(xT[:, :, tok(b, b + 1)], src_ps, [("ps", pb), ("ps", pb + 1)], kx(range(8), [b]))
        else:
            VCOPY(xT[:, :, tok(b, b + 1)], src_ps, [("ps", pb), ("ps", pb + 1)], kx(range(8), [b]))

    def units_of(nsamp):
        us = [(0, 4)]
        b = 4
        while b < 4 + nsamp:
            us.append((b, min(b + 4, 4 + nsamp)))
            b += 4
        return us

    def rsqrt_from_stats(ps_ap, dst, scale_div, rd_keys, key):
        ACT(dst, ps_ap, AF.Ln, rd_keys, [key], bias=EPS, scale=1.0 / scale_div)
        ACT(dst, dst, AF.Exp, [key], [key], scale=-0.5)

    def rope_evac(ps_b, n, lt0, dst_ap, dst_keys):
        i = cnt["rope"] % 2
        rbk = 4 + (cnt["rope"] % 4)
        cnt["rope"] += 1
        ACOPY(q16[:, i, 0:n], bank(ps_b, n), [("ps", ps_b)], [("q16", i)])
        MM(bank(rbk, n), perm[:], q16[:, i, 0:n], True, True, [("q16", i), "perm"], [("ps", rbk)])
        TT(tmpf[:, i, 0:n], bank(ps_b, n), cosT[:, lt0:lt0 + n], ALU.mult, [("ps", ps_b), "cos"], [("tmpf", i)])
        TT(tmpf[:, 2, 0:n], bank(rbk, n), sinT[:, lt0:lt0 + n], ALU.mult, [("ps", rbk), "sin"], [("tmpf", 2)])
        TT(dst_ap, tmpf[:, i, 0:n], tmpf[:, 2, 0:n], ALU.add, [("tmpf", i), ("tmpf", 2)], dst_keys)

    def f_slab(slot, col0, units, kchunks_fn, banks, evac_fn):
        for ui, (b0, b1) in enumerate(units):
            n = (b1 - b0) * 128
            pb = banks[ui]
            for k in range(8):
                rhs_ap, rkeys = kchunks_fn(k, b0, b1)
                MM(bank(pb, n), ring[:, slot, k, col0:col0 + 128], rhs_ap, k == 0, k == 7,
                   [("ring", slot)] + rkeys, [("ps", pb)], accum=(k > 0))
        for ui, (b0, b1) in enumerate(units):
            evac_fn(ui, b0, b1, banks[ui])

    def h_chunk(k, b0, b1):
        return hT[:, k, tok(b0, b1)], kh([k], range(b0, b1))

    def a_chunk(k, b0, b1):
        if k < 4:
            return qT[:, k, tok(b0, b1)], kq([k], range(b0, b1))
        return zT[:, k - 4, tok(b0, b1)], kz([k - 4], range(b0, b1))

    def vsel(b0):
        return 0 if b0 < 4 else 1

    out_ops = []

    def emit_layer(l):
        pm = l % 2
        ni = 12 - l
        nq = 11 - l
        units_in = units_of(ni)
        units_q = units_of(nq)
        blocks_in = list(range(4)) + list(range(4, 4 + ni))
        blocks_q = list(range(4)) + list(range(4, 4 + nq))

        DMA("pool", wp[:, pm], wpool_d[l].rearrange("g c d -> c g d"), [], [("wp", pm)], "wp%d" % pm)

        stage_gate(3 + 10 * l)
        def norm_A(b0, b1, pb):
            n = (b1 - b0) * 128
            for hf in range(2):
                ACT(sq[:, :, 0:n], xT[:, hf * 4:hf * 4 + 4, tok(b0, b1)], AF.Square,
                    kx(range(hf * 4, hf * 4 + 4), range(b0, b1)), ["sq"])
                for c4 in range(4):
                    c = hf * 4 + c4
                    MM(bank(pb, n), onesb[:], sq[:, c4, 0:n], c == 0, c == 7, ["sq", "onesb"], [("ps", pb)], accum=(c > 0))

        def norm_B(b0, b1, pb):
            n = (b1 - b0) * 128
            v = vsel(b0)
            rsqrt_from_stats(bank(pb, n), rb[:, 0:n], float(D), [("ps", pb)], "rb")
            for c in range(8):
                i = c % 2
                STT(tmpf[:, i, 0:n], xT[:, c, tok(b0, b1)], gmul[:, pm, c, v:v + 1], rb[:, 0:n], ALU.mult, ALU.mult,
                    kx([c], range(b0, b1)) + [("gmul", pm), "rb"], [("tmpf", i)])
                if c % 2 == 0:
                    ACT(hT[:, c, tok(b0, b1)], tmpf[:, i, 0:n], AF.Identity, [("tmpf", i), ("mods", pm)],
                        kh([c], range(b0, b1)), bias=mods[:, pm, c, v:v + 1], scale=1.0)
                else:
                    TS(hT[:, c, tok(b0, b1)], tmpf[:, i, 0:n], mods[:, pm, c, v:v + 1], ALU.add,
                       [("tmpf", i), ("mods", pm)], kh([c], range(b0, b1)))

        norm_sched = {}
        nU = len(units_in)

        def nA(u):
            norm_A(units_in[u][0], units_in[u][1], 6 + u % 2)

        def nB(u):
            norm_B(units_in[u][0], units_in[u][1], 6 + u % 2)

        nA(0)
        if nU > 1:
            nA(1)
        nB(0)
        for u in range(nU):
            lst = []
            if u + 2 < nU:
                lst.append(lambda u=u: nA(u + 2))
            if u + 1 < nU:
                lst.append(lambda u=u: nB(u + 1))
            norm_sched[units_in[u][0]] = lst

        stage_gate(4 + 10 * l)
        slot_u = ring_load(win_d[l][:, 1280:1792], 512, 0)
        targets = set(blocks_q)
        pooled_in_unit = {}

        def unit_index_q(tb):
            for ui, (b0, b1) in enumerate(units_q):
                if b0 <= tb < b1:
                    return ui
            return None

        pend = []

        def pool_stage2a(ui):
            b0, b1 = units_q[ui]
            n = (b1 - b0) * 128
            pk = [("pooledT", i) for i in range(b1 - b0)]
            for g in range(4):
                bk = 4 + g % 2
                MM(bank(bk, n), wp[:, pm, g, :], pooledT[:, g, 0:n], True, True, [("wp", pm)] + pk, [("ps", bk)])
                TS(zf[:, g, 0:n], bank(bk, n), c_psc(l, g), ALU.mult, [("ps", bk), "vec"], [("zf", g)])
            pend.append([1, lambda: pool_stage2b(ui)])

        def pool_stage2b(ui):
            b0, b1 = units_q[ui]
            n = (b1 - b0) * 128
            TT(sq[:, :, 0:n], zf[:, :, 0:n], zf[:, :, 0:n], ALU.mult, [("zf", g) for g in range(4)], ["sq"])
            for g in range(4):
                MM(bank(4, n), onesb[:], sq[:, g, 0:n], g == 0, g == 3, ["sq", "onesb"], [("ps", 4)], accum=(g > 0))
            rsqrt_from_stats(bank(4, n), rb[:, 0:n], 512.0, [("ps", 4)], "rb")
            TT(zT[:, :, tok(b0, b1)], zf[:, :, 0:n], rb[:, 0:n].unsqueeze(1).broadcast_to([128, 4, n]), ALU.mult,
               [("zf", g) for g in range(4)] + ["rb"], kz(range(4), range(b0, b1)))

        pcount = {"i": 0}

        def pool_target(tb):
            ui = unit_index_q(tb)
            b0, b1 = units_q[ui]
            if tb < 4:
                dtype_i = 0 if tb % 2 == 0 else 1
                prev_b = tb - 1 if tb % 2 == 1 else None
                next_b = tb + 1 if tb % 2 == 0 else None
                et_prev, et_next = 0, 1
            else:
                dtype_i = 2 if tb == 4 else 3
                prev_b = tb - 1 if tb > 4 else None
                next_b = tb + 1
                et_prev, et_next = 2, 3
            PB = 2 + pcount["i"] % 2
            pcount["i"] += 1
            for g in range(4):
                gs = slice(g * 128, (g + 1) * 128)
                MM(bank(PB, 128, g * 128), utm[:, tb % 4, gs], pmd[:, dtype_i, g, :], True,
                   (prev_b is None and next_b is None), [("utm", tb % 4), "pmd"], [("ps", PB)], accum=(g > 0))
                if prev_b is not None:
                    MM(bank(PB, 16, g * 128), utm[:, prev_b % 4, gs], pme[:, et_prev, g, :], False, next_b is None,
                       [("utm", prev_b % 4), "pme"], [("ps", PB)], accum=True)
                if next_b is not None:
                    MM(bank(PB, 16, g * 128 + 112), utm[:, next_b % 4, gs], pme[:, et_next, g, :], False, True,
                       [("utm", next_b % 4), "pme"], [("ps", PB)], accum=True)
            off = (tb - b0) * 128
            VCOPY(pooledT[:, :, off:off + 128], bank(PB).rearrange("p (g t) -> p g t", g=4), [("ps", PB)],
                  [("pooledT", tb - b0)])
            pooled_in_unit[ui] = pooled_in_unit.get(ui, 0) + 1
            if pooled_in_unit[ui] == b1 - b0:
                pend.append([1, lambda: pool_stage2a(ui)])

        def run_pending(flush=False):
            while True:
                due = [p for p in pend if p[0] <= 0 or flush]
                if not due:
                    break
                p = due[0]
                pend.remove(p)
                p[1]()
            for p in pend:
                p[0] -= 1

        for bi, b in enumerate(blocks_in):
            pb = bi % 2
            for k in range(8):
                MM(bank(pb), hT[:, k, tok(b, b + 1)], ring[:, slot_u, k, :], k == 0, k == 7,
                   [("ring", slot_u)] + kh([k], [b]), [("ps", pb)], accum=(k > 0))
            ACOPY(utm[:, b % 4, :], bank(pb), [("ps", pb)], [("utm", b % 4)])
            for fn in norm_sched.get(b, []):
                fn()
            run_pending()
            if b < 4:
                if b % 2 == 1:
                    pend.append([0, lambda b=b: (pool_target(b - 1), pool_target(b))])
            elif b - 1 >= 4 and (b - 1) in targets:
                pend.append([0, lambda b=b: pool_target(b - 1)])
        run_pending(flush=True)

        stage_gate(5 + 10 * l)
        slot_gp = ring_load(win_d[l][:, 1792:2304], 512, 1)
        for j in range(4):
            banks = [0, 1, 2, 3] if j % 2 == 0 else [4, 5, 6, 7]

            def evac_gp(ui, b0, b1, pb, j=j):
                n = (b1 - b0) * 128
                i = cnt["evac"] % 2
                cnt["evac"] += 1
                ACT(q16[:, i, 0:n], bank(pb, n), AF.Silu, [("ps", pb)], [("q16", i)])
                STT(zT[:, j, tok(b0, b1)], zT[:, j, tok(b0, b1)], c_pn(l, j), q16[:, i, 0:n], ALU.mult, ALU.mult,
                    kz([j], range(b0, b1)) + [("q16", i), "vec"], kz([j], range(b0, b1)))
            f_slab(slot_gp, j * 128, units_q, h_chunk, banks, evac_gp)

        stage_gate(5.1 + 10 * l)
        slot_kv = ring_load(win_d[l][:, 512:768], 256, 0)
        DMA("pool", ckt[:], ck_d[l].rearrange("(b p) f -> p b f", p=128), [], ["ckt"], "ckt")
        for cb in range(2):
            e0 = 2 * (16 + cb)
            dstv = vaug[:, 64 + 128 * e0: 64 + 128 * e0 + 256].rearrange("p (g x) -> p g x", g=2)[:, :, 0:64]
            srcv = cv_d[l][cb * 128:(cb + 1) * 128, :].rearrange("p (g d) -> p g d", g=2)
            DMA("pool", dstv, srcv, [], kv([16 + cb]), "cv%d" % cb)
        stage_gate(5.2 + 10 * l)
        ctb = bank(7, 128).bitcast(BF16)
        for cb in range(2):
            TR(ctb[:, cb * 128:(cb + 1) * 128], ckt[:, cb, :], identb[:], ["ckt", "identb"], [("ps", 7)])
        ACOPY(kT[:, NTOK:NTOK + 256], ctb, [("ps", 7)], kk([16, 17]))

        stage_gate(5.3 + 10 * l)
        for bi, b in enumerate(blocks_in):
            pb = 4 + bi % 2
            if b < 4:
                for k in range(8):
                    MM(bank(pb, 256), hT[:, k, tok(b, b + 1)], ring[:, slot_kv, k, 0:256], k == 0, k == 7,
                       [("ring", slot_kv)] + kh([k], [b]), [("ps", pb)], accum=(k > 0))
                ACOPY(kvf[:], bank(pb, 256), [("ps", pb)], ["kvf"])
                out_ops.append(DMA("sp", nk_d[l][b * 128:(b + 1) * 128, :], kvf[:, 0:128], ["kvf"], [], "okv"))
                out_ops.append(DMA("sp", nv_d[l][b * 128:(b + 1) * 128, :], kvf[:, 128:256], ["kvf"], [], "okv"))
                vsrc = bank(pb, 128, 128).rearrange("p (g x) -> p g x", g=2)
            else:
                for k in range(8):
                    MM(bank(pb, 128), hT[:, k, tok(b, b + 1)], ring[:, slot_kv, k, 128:256], k == 0, k == 7,
                       [("ring", slot_kv)] + kh([k], [b]), [("ps", pb)], accum=(k > 0))
                vsrc = bank(pb, 128).rearrange("p (g x) -> p g x", g=2)
            e0 = 2 * b
            dstv = vaug[:, 64 + 128 * e0: 64 + 128 * e0 + 256].rearrange("p (g x) -> p g x", g=2)[:, :, 0:64]
            VCOPY(dstv, vsrc, [("ps", pb)], kv([b]))

        stage_gate(5.4 + 10 * l)

        def evac_k(ui, b0, b1, pb):
            n = (b1 - b0) * 128
            if b0 < 4:
                ACOPY(kT[:, tok(b0, b1)], bank(pb, n), [("ps", pb)], kk(range(b0, b1)))
            else:
                rope_evac(pb, n, (b0 - 4) * 128, kT[:, tok(b0, b1)], kk(range(b0, b1)))
        f_slab(slot_kv, 0, units_in, h_chunk, [0, 1, 2, 3], evac_k)

        stage_gate(7 + 10 * l)
        slot_q = ring_load(win_d[l][:, 0:512], 512, 1)
        for c in range(4):
            def evac_q(ui, b0, b1, pb, c=c):
                n = (b1 - b0) * 128
                if b0 < 4:
                    ACOPY(qT[:, c, tok(b0, b1)], bank(pb, n), [("ps", pb)], kq([c], range(b0, b1)))
                else:
                    rope_evac(pb, n, (b0 - 4) * 128, qT[:, c, tok(b0, b1)], kq([c], range(b0, b1)))
            f_slab(slot_q, c * 128, units_q, h_chunk, [0, 1, 2, 3], evac_q)
            if c == 1 and l + 1 < DEPTH:
                emit_ada_granule(l + 1, 0, 0)

        stage_gate(8 + 10 * l)
        VCOPY(esb[:].rearrange("p g (h q) -> p (g h) q", h=4), esink[:, l * 8:(l + 1) * 8].unsqueeze(2).broadcast_to([128, 8, 128]),
              ["esink"], ["esb"])
        ada_pending = [1] if l + 1 < DEPTH else []
        ptasks = []
        for qi, qb in enumerate(blocks_q):
            if qb < 4:
                s0 = (qb // 2) * 2
                keys = [(s0, None), (s0 + 1, None)]
            else:
                keys = []
                if qb > 4:
                    keys.append((qb - 1, 0))
                keys.append((qb, None))
                keys.append((qb + 1, 1))
                keys += [(16, None), (17, None)]
            ptasks.append(dict(qi=qi, qb=qb, keys=keys, ob=((4, 5) if qi % 2 == 0 else (6, 7)), ab=qi % 2))
        psteps = [(ti, ki) for ti, t in enumerate(ptasks) for ki in range(len(t["keys"]))]
        last_ps = {}
        for si_, (ti_, ki_) in enumerate(psteps):
            last_ps[ti_] = si_
        deferred = []

        def emit_qk(si):
            ti, ki = psteps[si]
            t = ptasks[ti]
            qb = t["qb"]
            kb, mtype = t["keys"][ki]
            sp = si % 2
            pt = si % 3
            for g in range(2):
                lo, hi = 64 * g, 64 * g + 64
                MM(bank(2 * sp + g), kT[lo:hi, kb * 128:(kb + 1) * 128], qT[lo:hi, :, tok(qb, qb + 1)], True, True,
                   kk([kb]) + kq(range(4), [qb]), [("ps", 2 * sp + g)])
            ACT(PT[:, pt, :], psum[:, 2 * sp * 512:(2 * sp + 2) * 512], AF.Exp, [("ps", 2 * sp), ("ps", 2 * sp + 1)],
                [("PT", pt)], scale=0.125)
            if mtype is not None:
                ptv = PT[:, pt, :].rearrange("p (h q) -> p h q", h=8)
                TT(ptv, ptv, mask2[:, mtype, :].unsqueeze(1).broadcast_to([128, 8, 128]), ALU.mult,
                   [("PT", pt), "mask2"], [("PT", pt)])

        def emit_unit_rmsnorm(ub0, ub1):
            n = (ub1 - ub0) * 128
            uk = kq(range(4), range(ub0, ub1))
            TT(sq[:, :, 0:n], qT[:, :, tok(ub0, ub1)], qT[:, :, tok(ub0, ub1)], ALU.mult, uk, ["sq"])
            for c in range(4):
                MM(bank(3, n), onesb[:], sq[:, c, 0:n], c == 0, c == 3, ["sq", "onesb"], [("ps", 3)], accum=(c > 0))
            rsqrt_from_stats(bank(3, n), rb[:, 0:n], 512.0, [("ps", 3)], "rb")
            TT(qT[:, :, tok(ub0, ub1)], qT[:, :, tok(ub0, ub1)], rb[:, 0:n].unsqueeze(1).broadcast_to([128, 4, n]), ALU.mult,
               uk + ["rb"], uk)

        def emit_pv(si):
            ti, ki = psteps[si]
            t = ptasks[ti]
            qb, ab = t["qb"], t["ab"]
            kb, mtype = t["keys"][ki]
            pt = si % 3
            first, last = (ki == 0), (ki == len(t["keys"]) - 1)
            for g in range(2):
                ob = t["ob"][g]
                en = 2 * kb + g
                MM(bank(ob), vaug[:, 64 + 128 * en: 64 + 128 * en + 128], PT[:, pt, g * 512:(g + 1) * 512], first, last,
                   kv([kb]) + [("PT", pt)], [("ps", ob)], accum=(not first))
            if not last:
                return
            for g in range(2):
                ob = t["ob"][g]
                TT(lnt[64:128, g, :], bank(ob)[64:128, :], esb[64:128, g, :], ALU.add, [("ps", ob), "esb"], [("lnt", g)])

            def finish2(t=t, qb=qb):
                ACT(lnt[64:128, :, :], lnt[64:128, :, :], AF.Ln, [("lnt", 0), ("lnt", 1)], [("lnt", 0), ("lnt", 1)])
                ACT(lnt[64:128, :, :], lnt[64:128, :, :], AF.Exp, [("lnt", 0), ("lnt", 1)], [("lnt", 0), ("lnt", 1)], scale=-1.0)
                for g in range(2):
                    ob = t["ob"][g]
                    TT(qT[0:64, 2 * g:2 * g + 2, tok(qb, qb + 1)], bank(ob, 256)[0:64, :].rearrange("p (j t) -> p j t", j=2),
                       lnt[64:128, g, 0:256].rearrange("p (j t) -> p j t", j=2), ALU.mult,
                       [("ps", ob), ("lnt", g)], kq([2 * g, 2 * g + 1], [qb]))
                    TT(qT[64:128, 2 * g:2 * g + 2, tok(qb, qb + 1)], bank(ob, 256, 256)[0:64, :].rearrange("p (j t) -> p j t", j=2),
                       lnt[64:128, g, 256:512].rearrange("p (j t) -> p j t", j=2), ALU.mult,
                       [("ps", ob), ("lnt", g)], kq([2 * g, 2 * g + 1], [qb]))
            deferred.append((si + 1, finish2))
            for (ub0, ub1) in units_q:
                if qb == ub1 - 1:
                    deferred.append((si + 3, lambda ub0=ub0, ub1=ub1: emit_unit_rmsnorm(ub0, ub1)))
            if ada_pending and t["qi"] >= 3:
                gi = ada_pending.pop(0)
                deferred.append((si + 2, lambda gi=gi: emit_ada_granule(l + 1, gi, 1)))

        nst = len(psteps)
        PIPE = 2
        for si in range(nst + PIPE):
            if si < nst:
                emit_qk(si)
            if si >= PIPE:
                cur = si - PIPE
                emit_pv(cur)
                for d in [d for d in deferred if d[0] <= cur]:
                    deferred.remove(d)
                    d[1]()
        for d in list(deferred):
            d[1]()
        deferred.clear()
        while ada_pending:
            emit_ada_granule(l + 1, ada_pending.pop(0), 1)

        stage_gate(9 + 10 * l)
        slot_ga = ring_load(win_d[l][:, 768:1280], 512, 0)
        if l + 1 < DEPTH:
            emit_ada_granule(l + 1, 2, 1)
        for j in range(4):
            banks = [0, 1, 2, 3] if j % 2 == 0 else [4, 5, 6, 7]

            def evac_ga(ui, b0, b1, pb, j=j):
                n = (b1 - b0) * 128
                i = cnt["evac"] % 2
                cnt["evac"] += 1
                ACT(q16[:, i, 0:n], bank(pb, n), AF.Silu, [("ps", pb)], [("q16", i)])
                STT(qT[:, j, tok(b0, b1)], qT[:, j, tok(b0, b1)], c_an(l, j), q16[:, i, 0:n], ALU.mult, ALU.mult,
                    kq([j], range(b0, b1)) + [("q16", i), "vec"], kq([j], range(b0, b1)))
            f_slab(slot_ga, j * 128, units_q, h_chunk, banks, evac_ga)

        stage_gate(10 + 10 * l)
        for half in range(2):
            slot_o = ring_load(wout_d[l][:, half * 512:(half + 1) * 512], 512, 1 - half)
            for cc in range(4):
                c = half * 4 + cc
                banks = [0, 1, 2, 3] if cc % 2 == 0 else [4, 5, 6, 7]
                if l + 1 < DEPTH and cc == 2:
                    emit_ada_granule(l + 1, 3 + half, half)

                def evac_o(ui, b0, b1, pb, c=c):
                    n = (b1 - b0) * 128
                    v = vsel(b0)
                    STT(xT[:, c, tok(b0, b1)], bank(pb, n), mods[:, pm, 16 + c, v:v + 1], xT[:, c, tok(b0, b1)],
                        ALU.mult, ALU.add, [("ps", pb), ("mods", pm)] + kx([c], range(b0, b1)), kx([c], range(b0, b1)))
                f_slab(slot_o, cc * 128, units_q, a_chunk, banks, evac_o)
        if l + 1 < DEPTH:
            emit_ada_granule(l + 1, 5, 1)
            emit_mods_finish(l + 1)

        if DEBUG_DUMP:
            out_ops.append(DMA("sp", dbg_d[l], xT[:], kx(range(8), range(NBLK)), [], "dbg"))

    def emit_epilogue():
        DMA("sp", fnb, fn_d.broadcast_to([128, D]), [], kh([0], range(NBLK)), "fnb")
        fnk = kh([0], range(NBLK))
        own = list(range(4)) + list(range(4, 12))
        for i, b in enumerate(own):
            pb = (i % 2) * 2
            st = i % 2
            for c in range(8):
                TR(bank(pb + c // 4, 128, (c % 4) * 128), xT[:, c, tok(b, b + 1)], identf[:], kx([c], [b]) + ["identf"],
                   [("ps", pb + c // 4)])
            for hb in range(2):
                ACT(stage[:, st, hb * 512:(hb + 1) * 512], bank(pb + hb), AF.Square, [("ps", pb + hb)],
                    [("stage", st), ("ssq", st, hb)], accum_out=ssq[:, 2 * st + hb: 2 * st + hb + 1])
            s0 = ssq[:, 2 * st:2 * st + 1]
            TT(s0, s0, ssq[:, 2 * st + 1:2 * st + 2], ALU.add, [("ssq", st, 0), ("ssq", st, 1)], [("ssq", st, 0)])
            ACT(s0, s0, AF.Ln, [("ssq", st, 0)], [("ssq", st, 0)], bias=EPS, scale=1.0 / D)
            ACT(s0, s0, AF.Exp, [("ssq", st, 0)], [("ssq", st, 0)], scale=-0.5)
            for hb in range(2):
                STT(stage[:, st, hb * 512:(hb + 1) * 512], bank(pb + hb), s0, fnb[:, hb * 512:(hb + 1) * 512], ALU.mult, ALU.mult,
                    [("ps", pb + hb), ("ssq", st, 0)] + fnk, [("stage", st)])
            dst = yp_d[b * 128:(b + 1) * 128, :] if b < 4 else ys_d[(b - 4) * 128:(b - 3) * 128, :]
            out_ops.append(DMA("sp", dst, stage[:, st, :], [("stage", st)], [], "yout%d" % st))


    try:
        for l in range(DEPTH):
            emit_layer(l)
        stage_gate(50)
        emit_epilogue()
    except StopBuild:
        pass

    P.emit(final_wait_ops=out_ops)
    es.close()
    return nc


_CACHE = {}


def _prep_shared(inp):
    qp, ap = _qperm(), _aperm()
    w_in = np.asarray(inp["w_in"], np.float32)
    colperm = np.concatenate([qp, np.arange(512, 768), 768 + ap, np.arange(1280, 2304)])
    w_in_p = np.ascontiguousarray(w_in[:, :, colperm])
    w_out = np.asarray(inp["w_out"], np.float32)
    rowperm = np.concatenate([ap, np.arange(512, 1024)])
    w_out_p = np.ascontiguousarray(w_out[:, rowperm, :])
    attn_norm_p = np.asarray(inp["attn_norm"], np.float32)[:, ap]
    return w_in_p, w_out_p, attn_norm_p


def kernel(x_prompt, x_sample, cache_k, cache_v, c, c_ctx, norm_w, w_ada, b_ada, w_in, sink,
           attn_norm, pool_norm, w_pool, pool_scale, w_out, final_norm):
    f32 = np.float32
    inp = dict(w_in=w_in, w_out=w_out, attn_norm=attn_norm)
    w_in_p, w_out_p, attn_norm_p = _prep_shared(inp)
    x_prompt = np.asarray(x_prompt, f32)
    x_sample = np.asarray(x_sample, f32)
    cache_k = np.asarray(cache_k, f32)
    cache_v = np.asarray(cache_v, f32)
    c = np.asarray(c, f32)
    c_ctx = np.asarray(c_ctx, f32)
    w_ada = np.ascontiguousarray(np.asarray(w_ada, f32))
    w_pool = np.ascontiguousarray(np.asarray(w_pool, f32))
    b_ada = np.asarray(b_ada, f32)
    norm_w = np.asarray(norm_w, f32)
    pool_norm = np.asarray(pool_norm, f32)
    pool_scale = np.asarray(pool_scale, f32)
    sink = np.asarray(sink, f32)
    final_norm = np.asarray(final_norm, f32)

    bf = ml_dtypes.bfloat16
    ident = np.eye(128, dtype=f32)
    consts = {}
    for rev in (False, True):
        cos, sin = _rope_tables(rev)
        pd, pe = _pool_tables(rev)
        consts[rev] = dict(cosT=cos.astype(bf), sinT=sin.astype(bf), pm_diag=pd.astype(bf), pm_edge=pe.astype(bf))
    mask2 = np.ascontiguousarray(_masks()[:, :, 0, :]).astype(bf)
    permm = _perm_matrix().astype(bf)

    in_maps = []
    for i in range(NCORES):
        b, half = i // 2, i % 2
        rev = half == 1
        if not rev:
            xs = x_sample[b, 0:1536]
        else:
            xs = x_sample[b, ::-1][0:1536]
        vecs = np.zeros((256, 128), f32)
        vecs[0:96] = b_ada.reshape(DEPTH * 24, 128)
        vecs[96:128] = norm_w.reshape(DEPTH * 8, 128)
        vecs[128:144] = attn_norm_p.reshape(DEPTH * 4, 128)
        vecs[144:160] = pool_norm.reshape(DEPTH * 4, 128)
        vecs[160:176] = pool_scale.reshape(DEPTH * 4, 128)
        vecs[176:184] = c_ctx.reshape(8, 128)
        vecs[184:192] = c[b].reshape(8, 128)
        m = dict(
            xp=np.ascontiguousarray(x_prompt[2 * i:2 * i + 2].reshape(512, D)),
            xs=np.ascontiguousarray(xs),
            ck=np.ascontiguousarray(cache_k[b].reshape(DEPTH, 256, 128)),
            cv=np.ascontiguousarray(cache_v[b].reshape(DEPTH, 256, 128)),
            w_ada=w_ada, w_in=w_in_p, w_out=w_out_p, w_pool=w_pool,
            vecs=vecs, sinkv=np.ascontiguousarray(sink.reshape(1, 32)),
            fnorm=np.ascontiguousarray(final_norm.reshape(1, D)),
            ident_f=ident, ident_b=ident.astype(bf), perm=permm, mask2=mask2,
            **consts[rev],
        )
        in_maps.append(m)

    if "nc" not in _CACHE:
        _CACHE["nc"] = build_program()
    nc = _CACHE["nc"]
    res = run_bass_kernel_spmd(nc, in_maps, core_ids=list(range(NCORES)))
    outs = res.results

    y_prompt = np.zeros((16, 256, D), f32)
    y_sample = np.zeros((4, 2048, D), f32)
    new_k = np.zeros((16, DEPTH, 256, 2, 64), f32)
    new_v = np.zeros((16, DEPTH, 256, 2, 64), f32)
    for i in range(NCORES):
        b, half = i // 2, i % 2
        r = outs[i]
        y_prompt[2 * i:2 * i + 2] = np.asarray(r["yp"]).reshape(2, 256, D)
        ys = np.asarray(r["ys"])
        if half == 0:
            y_sample[b, 0:1024] = ys
        else:
            y_sample[b, 1024:2048] = ys[::-1]
        nk = np.asarray(r["nk"]).reshape(DEPTH, 2, 256, 2, 64)
        nv = np.asarray(r["nv"]).reshape(DEPTH, 2, 256, 2, 64)
        new_k[2 * i:2 * i + 2] = nk.transpose(1, 0, 2, 3, 4)
        new_v[2 * i:2 * i + 2] = nv.transpose(1, 0, 2, 3, 4)
    if DEBUG_DUMP:
        kernel.dbg = [np.asarray(o["dbg"]) for o in outs]
    return (y_prompt, y_sample, new_k, new_v)
```

```python
from contextlib import ExitStack
import numpy as np
import ml_dtypes
import concourse.bass as bass
import concourse.mybir as mybir
from concourse.bass_utils import run_bass_kernel_spmd

F32 = mybir.dt.float32
BF16 = mybir.dt.bfloat16
AF = mybir.ActivationFunctionType
ALU = mybir.AluOpType

D = 1024
DEPTH = 4
NCORES = 8
EPS = 1e-6
NTOK = 2048
NBLK = 16
KT_COLS = NTOK + 256
NRING = 2
DEBUG_DUMP = False
STOP = 99


class StopBuild(Exception):
    pass


def stage_gate(n):
    if n > STOP:
        raise StopBuild()

ENGS = ("pe", "act", "dve", "pool", "sp")


class Op:
    __slots__ = ("eng", "fn", "deps", "signal", "tick", "dma_sem", "dma_val")

    def __init__(self, eng, fn):
        self.eng = eng
        self.fn = fn
        self.deps = []
        self.signal = False
        self.tick = None
        self.dma_sem = None
        self.dma_val = None


class Prog:
    def __init__(self, nc):
        self.nc = nc
        self.ops = {e: [] for e in ENGS}
        self.res = {}
        self.dma_tot = {}

    def op(self, eng, fn, reads=(), writes=(), accum=False, dma_sem=None):
        o = Op(eng, fn)
        deps = []
        for r in reads:
            st = self.res.get(r)
            if st is not None and st[0] is not None:
                deps.append(("raw", st[0]))
            if st is not None and isinstance(r, tuple) and r[0] == "ps":
                for rd in st[1]:
                    if rd.eng != eng:
                        deps.append(("raw", rd))
        for w in writes:
            st = self.res.get(w)
            if st is not None:
                if st[0] is not None and not accum:
                    deps.append(("waw", st[0]))
                for rd in st[1]:
                    deps.append(("war", rd))
        seen = set()
        for kind, d in deps:
            if d is o or id(d) in seen:
                continue
            if d.eng == eng and d.dma_sem is None and dma_sem is None:
                if eng == "pe":
                    continue
            seen.add(id(d))
            o.deps.append(d)
        for r in reads:
            self.res.setdefault(r, [None, []])[1].append(o)
        for w in writes:
            st = self.res.setdefault(w, [None, []])
            st[0] = o
            st[1] = []
        if dma_sem is not None:
            o.dma_sem = dma_sem
            self.dma_tot[dma_sem] = self.dma_tot.get(dma_sem, 0) + 16
            o.dma_val = self.dma_tot[dma_sem]
        self.ops[eng].append(o)
        return o

    def emit(self, final_wait_ops=()):
        nc = self.nc
        for e in ENGS:
            for o in self.ops[e]:
                for d in o.deps:
                    d.signal = True
        for o in final_wait_ops:
            o.signal = True
        for e in ENGS:
            t = 0
            for o in self.ops[e]:
                if o.dma_sem is None and o.signal:
                    t += 1
                    o.tick = t
        with ExitStack() as es:
            sems = {}
            for e in ENGS:
                sems[e] = es.enter_context(nc.semaphore("s_" + e))
            for k in self.dma_tot:
                sems[("dma", k)] = es.enter_context(nc.semaphore("d_" + str(k)))
            block = es.enter_context(nc.Block())

            def run(eng_name, e):
                waited = {}
                for o in self.ops[eng_name]:
                    need = {}
                    for d in o.deps:
                        if d.dma_sem is not None:
                            key, val = ("dma", d.dma_sem), d.dma_val
                        else:
                            key, val = d.eng, d.tick
                        if need.get(key, 0) < val:
                            need[key] = val
                    for key, val in need.items():
                        if waited.get(key, 0) < val:
                            e.wait_ge(sems[key], val)
                            waited[key] = val
                    inst = o.fn(e)
                    if o.dma_sem is not None:
                        inst.then_inc(sems[("dma", o.dma_sem)], 16)
                    elif o.signal:
                        inst.then_inc(sems[eng_name], 1)
                if eng_name == "sp":
                    need = {}
                    for o in final_wait_ops:
                        if o.dma_sem is not None:
                            key, val = ("dma", o.dma_sem), o.dma_val
                        else:
                            key, val = o.eng, o.tick
                        need[key] = max(need.get(key, 0), val)
                    for key, val in need.items():
                        e.wait_ge(sems[key], val)

            @block.tensor
            def _(e):
                run("pe", e)

            @block.scalar
            def _(e):
                run("act", e)

            @block.vector
            def _(e):
                run("dve", e)

            @block.gpsimd
            def _(e):
                run("pool", e)

            @block.sync
            def _(e):
                run("sp", e)


def _aperm():
    idx = []
    for ac in range(4):
        g, j = ac // 2, ac % 2
        for h in (4 * g + j, 4 * g + 2 + j):
            idx += [h * 64 + d for d in range(64)]
    return np.array(idx)


def _qperm():
    idx = []
    for c in range(4):
        for h in (c, 4 + c):
            idx += [h * 64 + d for d in range(64)]
    return np.array(idx)


def _pool_op(L, w, n):
    M = np.zeros((n, n + 16), np.float64)
    for t in range(n):
        lo = min(max(t - w // 2, 0), L)
        hi = min(max(t + w // 2, 0), L)
        for s in range(lo, hi):
            if s < n + 16:
                M[t, s] += 1.0 / (hi - lo)
        M[t, t] -= 1.0
    return M


def _pool_tables(reverse):
    pd = np.zeros((128, 4, 4, 128), np.float32)
    pe = np.zeros((128, 4, 4, 16), np.float32)
    for g, w in enumerate((2, 4, 8, 16)):
        M = np.zeros((256, 256))
        for t in range(256):
            lo = min(max(t - w // 2, 0), 256)
            hi = min(max(t + w // 2, 0), 256)
            M[t, lo:hi] += 1.0 / (hi - lo)
            M[t, t] -= 1.0
        pd[:, 0, g, :] = M[0:128, 0:128].T
        pd[:, 1, g, :] = M[128:256, 128:256].T
        pe[:, 0, g, :] = M[128:144, 0:128].T
        pe[:, 1, g, :] = M[112:128, 128:256].T
        L = 2048
        Mg = np.zeros((L, L))
        for t in range(L):
            lo = min(max(t - w // 2, 0), L)
            hi = min(max(t + w // 2, 0), L)
            Mg[t, lo:hi] += 1.0 / (hi - lo)
            Mg[t, t] -= 1.0
        Ml = Mg[::-1, ::-1] if reverse else Mg
        pd[:, 2, g, :] = Ml[0:128, 0:128].T
        pd[:, 3, g, :] = Ml[128:256, 128:256].T
        pe[:, 2, g, :] = Ml[128:144, 0:128].T
        pe[:, 3, g, :] = Ml[240:256, 256:384].T
    return pd, pe


def _rope_tables(reverse):
    j = np.arange(1536)
    t = (2047 - j) if reverse else j
    row = (t // 64).astype(np.float32)
    col = (t % 64).astype(np.float32)
    inv = (10000.0 ** (-(np.arange(0, 32, 2, dtype=np.float32) / 32))).astype(np.float32)
    cos = np.zeros((128, 1536), np.float32)
    sin = np.zeros((128, 1536), np.float32)
    for p in range(128):
        d = p % 64
        pos = row if d < 32 else col
        ang = (pos * inv[d % 16]).astype(np.float32)
        cos[p] = np.cos(ang)
        sin[p] = np.sin(ang)
    return cos, sin


def _perm_matrix():
    pm = np.zeros((128, 128), np.float32)
    for jx in range(128):
        if jx % 32 < 16:
            pm[jx + 16, jx] = -1.0
        else:
            pm[jx - 16, jx] = 1.0
    return pm


def _masks():
    k = np.arange(128)[:, None]
    q = np.arange(128)[None, :]
    m = np.zeros((128, 2, 4, 128), np.float32)
    m[:, 0] = (k >= q)[:, None, :]
    m[:, 1] = (k <= q)[:, None, :]
    return m


def build_program():
    nc = bass.Bass("TRN2", target_bir_lowering=False)

    def din(name, shape, dt=F32):
        return nc.dram_tensor(name, list(shape), dt, kind="ExternalInput").ap()

    def dout(name, shape, dt=F32):
        return nc.dram_tensor(name, list(shape), dt, kind="ExternalOutput").ap()

    xp_d = din("xp", [512, D])
    xs_d = din("xs", [1536, D])
    ck_d = din("ck", [DEPTH, 256, 128])
    cv_d = din("cv", [DEPTH, 256, 128])
    wada_d = din("w_ada", [DEPTH, D, 3 * D])
    win_d = din("w_in", [DEPTH, D, 2304])
    wout_d = din("w_out", [DEPTH, D, D])
    wpool_d = din("w_pool", [DEPTH, 4, 128, 128])
    vecs_d = din("vecs", [256, 128])
    sink_d = din("sinkv", [1, 32])
    fn_d = din("fnorm", [1, D])
    identf_d = din("ident_f", [128, 128])
    identb_d = din("ident_b", [128, 128], BF16)
    perm_d = din("perm", [128, 128], BF16)
    cos_d = din("cosT", [128, 1536], BF16)
    sin_d = din("sinT", [128, 1536], BF16)
    mask_d = din("mask2", [128, 2, 128], BF16)
    pmd_d = din("pm_diag", [128, 4, 4, 128], BF16)
    pme_d = din("pm_edge", [128, 4, 4, 16], BF16)

    yp_d = dout("yp", [512, D])
    ys_d = dout("ys", [1024, D])
    nk_d = dout("nk", [DEPTH, 512, 128])
    nv_d = dout("nv", [DEPTH, 512, 128])
    if DEBUG_DUMP:
        dbg_d = dout("dbg", [DEPTH, 128, 8, NTOK])

    es = ExitStack()

    def sb(name, shape, dt=F32):
        return es.enter_context(nc.sbuf_tensor(name, list(shape), dt))

    psum = es.enter_context(nc.psum_tensor("psum", [128, 4096], F32))

    def bank(b, n=512, off=0):
        return psum[:, b * 512 + off: b * 512 + off + n]

    xT = sb("xT", [128, 8, NTOK])
    hT = sb("hT", [128, 8, NTOK], BF16)
    qT = sb("qT", [128, 4, NTOK], BF16)
    zT = sb("zT", [128, 4, NTOK], BF16)
    kT = sb("kT", [128, KT_COLS], BF16)
    NVA = 64 + 128 * 36
    vaug = sb("vaug", [128, NVA], BF16)
    ring = sb("ring", [128, NRING, 8, 512], BF16)
    AR = sb("arena", [128, 8192], BF16)
    zf = AR[:, 0:4096].bitcast(F32).rearrange("p (g t) -> p g t", g=4)
    stage = AR[:, 0:4096].bitcast(F32).rearrange("p (s t) -> p s t", s=2)
    pooledT = AR[:, 4096:6144].rearrange("p (g t) -> p g t", g=4)
    utm = AR[:, 6144:8192].rearrange("p (s t) -> p s t", s=4)
    PT = AR[:, 0:3072].rearrange("p (s t) -> p s t", s=3)
    atf = AR[:, 3072:5120].bitcast(F32).rearrange("p (a c t) -> p a c t", a=2, c=4)
    lnt = AR[:, 5120:7168].bitcast(F32).rearrange("p (a t) -> p a t", a=2)
    sqa = AR[:, 7168:7680].rearrange("p (c t) -> p c t", c=4)
    rb2 = AR[:, 7680:7936].bitcast(F32)
    vec_in = AR[:, 7168:7680].bitcast(F32).rearrange("p (s t) -> p s t", s=2)
    sq = sb("sq", [128, 4, 512], BF16)
    rb = sb("rb", [128, 512])
    tmpf = sb("tmpf", [128, 3, 512])
    q16 = sb("q16", [128, 2, 512], BF16)
    kvf = sb("kvf", [128, 256])
    ckt = sb("ckt", [128, 2, 128], BF16)
    cosT = sb("cosT_sb", [128, 1536], BF16)
    sinT = sb("sinT_sb", [128, 1536], BF16)
    mask2 = sb("mask2_sb", [128, 2, 128], BF16)
    pmd = sb("pmd_sb", [128, 4, 4, 128], BF16)
    pme = sb("pme_sb", [128, 4, 4, 16], BF16)
    wp = sb("wp_sb", [128, 2, 4, 128], BF16)
    identf = sb("identf_sb", [128, 128])
    identb = sb("identb_sb", [128, 128], BF16)
    onesb = sb("onesb", [128, 128], BF16)
    perm = sb("perm_sb", [128, 128], BF16)
    vec = sb("vec", [128, 256])
    esink = sb("esink", [128, 32])
    sT = sb("sT", [128, 8, 2], BF16)
    mods = sb("mods", [128, 2, 24, 2])
    gmul = sb("gmul", [128, 2, 8, 2])
    ssq = sb("ssq", [128, 4])
    selr = sb("selr", [1, 128], BF16)
    esb = sb("esb", [128, 2, 512], BF16)
    fnb = hT[:, 0, :].bitcast(F32)

    P = Prog(nc)
    cnt = {"dma": 0, "ring": 0, "ps_norm": 0, "evac": 0, "rope": 0}

    def MM(out, lhsT, rhs, start, stop, reads, writes, accum=False):
        return P.op("pe", lambda e: e.matmul(out, lhsT=lhsT, rhs=rhs, start=start, stop=stop),
                    reads=reads, writes=writes, accum=accum)

    def TR(out, in_, ident, reads, writes):
        return P.op("pe", lambda e: e.transpose(out=out, in_=in_, identity=ident), reads=reads, writes=writes)

    def ACT(out, in_, func, reads, writes, bias=None, scale=None, accum_out=None):
        kw = {}
        if bias is not None:
            kw["bias"] = bias
        if scale is not None:
            kw["scale"] = scale
        if accum_out is not None:
            kw["accum_out"] = accum_out
        return P.op("act", lambda e: e.activation(out=out, in_=in_, func=func, **kw), reads=reads, writes=writes)

    def ACOPY(out, in_, reads, writes):
        return P.op("act", lambda e: e.copy(out=out, in_=in_), reads=reads, writes=writes)

    def VCOPY(out, in_, reads, writes):
        return P.op("dve", lambda e: e.tensor_copy(out=out, in_=in_), reads=reads, writes=writes)

    def TT(out, in0, in1, op, reads, writes):
        return P.op("dve", lambda e: e.tensor_tensor(out=out, in0=in0, in1=in1, op=op), reads=reads, writes=writes)

    def STT(out, in0, scalar, in1, op0, op1, reads, writes):
        return P.op("dve", lambda e: e.scalar_tensor_tensor(out=out, in0=in0, scalar=scalar, in1=in1, op0=op0, op1=op1),
                    reads=reads, writes=writes)

    def TS(out, in0, scalar1, op0, reads, writes):
        return P.op("dve", lambda e: e.tensor_scalar(out=out, in0=in0, scalar1=scalar1, scalar2=None, op0=op0),
                    reads=reads, writes=writes)

    def DMA(eng, out, in_, reads, writes, sem):
        return P.op(eng, lambda e: e.dma_start(out=out, in_=in_), reads=reads, writes=writes, dma_sem=sem)

    def newsem(prefix):
        cnt["dma"] += 1
        return "%s%d" % (prefix, cnt["dma"])

    def c_bada(l):
        return vec[:, l * 24:(l + 1) * 24]

    def c_normw(l):
        return vec[:, 96 + l * 8: 96 + l * 8 + 8]

    def c_an(l, j):
        return vec[:, 128 + l * 4 + j: 128 + l * 4 + j + 1]

    def c_pn(l, j):
        return vec[:, 144 + l * 4 + j: 144 + l * 4 + j + 1]

    def c_psc(l, j):
        return vec[:, 160 + l * 4 + j: 160 + l * 4 + j + 1]

    def kx(cs, bs):
        return [("x", c, b) for c in cs for b in bs]

    def kh(cs, bs):
        return [("h", c, b) for c in cs for b in bs]

    def kq(cs, bs):
        return [("q", c, b) for c in cs for b in bs]

    def kz(cs, bs):
        return [("z", c, b) for c in cs for b in bs]

    def kk(bs):
        return [("k", b) for b in bs]

    def kv(bs):
        return [("v", b) for b in bs]

    def tok(b0, b1):
        return slice(b0 * 128, b1 * 128)

    DMA("sp", identf[:], identf_d, [], ["identf"], newsem("c"))
    DMA("sp", identb[:], identb_d, [], ["identb"], newsem("c"))
    DMA("sp", perm[:], perm_d, [], ["perm"], newsem("c"))
    DMA("sp", vec_in[:, 0, :], vecs_d[0:128, :], [], ["vec_in0"], newsem("c"))
    DMA("sp", vec_in[:, 1, :], vecs_d[128:256, :], [], ["vec_in1"], newsem("c"))
    DMA("sp", esink[:], sink_d.broadcast_to([128, 32]), [], ["esink"], newsem("c"))
    DMA("sp", cosT[:], cos_d, [], ["cos"], newsem("c"))
    DMA("sp", sinT[:], sin_d, [], ["sin"], newsem("c"))
    DMA("sp", mask2[:], mask_d, [], ["mask2"], newsem("c"))
    DMA("sp", pmd[:], pmd_d, [], ["pmd"], newsem("c"))
    DMA("sp", pme[:], pme_d, [], ["pme"], newsem("c"))
    P.op("dve", lambda e: e.memset(onesb[:], 1.0), writes=["onesb"])
    P.op("dve", lambda e: e.memset(selr[:, 0:64], 0.0), writes=["selr"])
    P.op("dve", lambda e: e.memset(selr[:, 64:128], 1.0), writes=["selr"])
    P.op("dve", lambda e: e.memset(vaug[:], 1.0), writes=kv(range(18)))

    for i in range(2):
        TR(bank(7, 128, i * 128), vec_in[:, i, :], identf[:], ["vec_in%d" % i, "identf"], [("ps", 7)])
    ACOPY(vec[:], bank(7, 256), [("ps", 7)], ["vec"])
    ACT(sT[:, :, 0], vec[:, 176:184], AF.Silu, ["vec"], ["sT"])
    ACT(sT[:, :, 1], vec[:, 184:192], AF.Silu, ["vec"], ["sT"])
    ACT(esink[:], esink[:], AF.Exp, ["esink"], ["esink"])

    def ring_load(src_ap, ncols, slot):
        src = src_ap.rearrange("(k p) c -> p k c", p=128)
        DMA("pool", ring[:, slot, :, 0:ncols], src, [], [("ring", slot)], "ring%d" % slot)
        return slot

    MODB = 3

    def emit_ada_granule(l, gi, slot):
        ring_load(wada_d[l][:, gi * 512:(gi + 1) * 512], 512, slot)
        pm = l % 2
        for jc in range(4):
            for k in range(8):
                MM(bank(7, 2, jc * 2), ring[:, slot, k, jc * 128:(jc + 1) * 128], sT[:, k, :], k == 0, k == 7,
                   [("ring", slot), "sT"], [("ps", 7)], accum=(k > 0))
        pv = bank(7, 8).rearrange("p (j v) -> p j v", v=2)
        for v in range(2):
            TT(mods[:, pm, gi * 4:(gi + 1) * 4, v], pv[:, :, v], vec[:, l * 24 + gi * 4: l * 24 + gi * 4 + 4], ALU.add,
               [("ps", 7), "vec"], [("mods", pm)])

    def emit_mods_finish(l):
        pm = l % 2
        for v in range(2):
            STT(gmul[:, pm, :, v], mods[:, pm, 8:16, v], 1.0, c_normw(l), ALU.add, ALU.mult,
                [("mods", pm), "vec"], [("gmul", pm)])

    if STOP >= 1:
        for gi in range(6):
            emit_ada_granule(0, gi, gi % 2)
        emit_mods_finish(0)

    for b in (range(NBLK) if STOP >= 2 else []):
        st = b % 2
        src = xp_d[b * 128:(b + 1) * 128, :] if b < 4 else xs_d[(b - 4) * 128:(b - 3) * 128, :]
        DMA("sp", stage[:, st, :], src, [], [("stage", st)], "xin%d" % st)
        pb = (b % 2) * 2
        for c in range(8):
            TR(bank(pb + c // 4, 128, (c % 4) * 128), stage[:, st, c * 128:(c + 1) * 128], identf[:],
               [("stage", st), "identf"], [("ps", pb + c // 4)])
        src_ps = psum[:, pb * 512: pb * 512 + 1024].rearrange("p (c t) -> p c t", c=8)
        if b % 2 == 0:
            ACOPY(xT[:, :, tok(b, b + 1)], src_ps, [("ps", pb), ("ps", pb + 1)], kx(range(8), [b]))
        else:
            VCOPY(xT[:, :, tok(b, b + 1)], src_ps, [("ps", pb), ("ps", pb + 1)], kx(range(8), [b]))

    def units_of(nsamp):
        us = [(0, 4)]
        b = 4
        while b < 4 + nsamp:
            us.append((b, min(b + 4, 4 + nsamp)))
            b += 4
        return us

    def rsqrt_from_stats(ps_ap, dst, scale_div, rd_keys, key):
        ACT(dst, ps_ap, AF.Ln, rd_keys, [key], bias=EPS, scale=1.0 / scale_div)
        ACT(dst, dst, AF.Exp, [key], [key], scale=-0.5)

    def rope_evac(ps_b, n, lt0, dst_ap, dst_keys):
        i = cnt["rope"] % 2
        rbk = 4 + (cnt["rope"] % 4)
        cnt["rope"] += 1
        ACOPY(q16[:, i, 0:n], bank(ps_b, n), [("ps", ps_b)], [("q16", i)])
        MM(bank(rbk, n), perm[:], q16[:, i, 0:n], True, True, [("q16", i), "perm"], [("ps", rbk)])
        TT(tmpf[:, i, 0:n], bank(ps_b, n), cosT[:, lt0:lt0 + n], ALU.mult, [("ps", ps_b), "cos"], [("tmpf", i)])
        TT(tmpf[:, 2, 0:n], bank(rbk, n), sinT[:, lt0:lt0 + n], ALU.mult, [("ps", rbk), "sin"], [("tmpf", 2)])
        TT(dst_ap, tmpf[:, i, 0:n], tmpf[:, 2, 0:n], ALU.add, [("tmpf", i), ("tmpf", 2)], dst_keys)

    def f_slab(slot, col0, units, kchunks_fn, banks, evac_fn):
        for ui, (b0, b1) in enumerate(units):
            n = (b1 - b0) * 128
            pb = banks[ui]
            for k in range(8):
                rhs_ap, rkeys = kchunks_fn(k, b0, b1)
                MM(bank(pb, n), ring[:, slot, k, col0:col0 + 128], rhs_ap, k == 0, k == 7,
                   [("ring", slot)] + rkeys, [("ps", pb)], accum=(k > 0))
        for ui, (b0, b1) in enumerate(units):
            evac_fn(ui, b0, b1, banks[ui])

    def h_chunk(k, b0, b1):
        return hT[:, k, tok(b0, b1)], kh([k], range(b0, b1))

    def a_chunk(k, b0, b1):
        if k < 4:
            return qT[:, k, tok(b0, b1)], kq([k], range(b0, b1))
        return zT[:, k - 4, tok(b0, b1)], kz([k - 4], range(b0, b1))

    def vsel(b0):
        return 0 if b0 < 4 else 1

    out_ops = []

    def emit_layer(l):
        pm = l % 2
        ni = 12 - l
        nq = 11 - l
        units_in = units_of(ni)
        units_q = units_of(nq)
        blocks_in = list(range(4)) + list(range(4, 4 + ni))
        blocks_q = list(range(4)) + list(range(4, 4 + nq))

        DMA("pool", wp[:, pm], wpool_d[l].rearrange("g c d -> c g d"), [], [("wp", pm)], "wp%d" % pm)

        stage_gate(3 + 10 * l)
        def norm_A(b0, b1, pb):
            n = (b1 - b0) * 128
            for hf in range(2):
                ACT(sq[:, :, 0:n], xT[:, hf * 4:hf * 4 + 4, tok(b0, b1)], AF.Square,
                    kx(range(hf * 4, hf * 4 + 4), range(b0, b1)), ["sq"])
                for c4 in range(4):
                    c = hf * 4 + c4
                    MM(bank(pb, n), onesb[:], sq[:, c4, 0:n], c == 0, c == 7, ["sq", "onesb"], [("ps", pb)], accum=(c > 0))

        def norm_B(b0, b1, pb):
            n = (b1 - b0) * 128
            v = vsel(b0)
            rsqrt_from_stats(bank(pb, n), rb[:, 0:n], float(D), [("ps", pb)], "rb")
            for c in range(8):
                i = c % 2
                STT(tmpf[:, i, 0:n], xT[:, c, tok(b0, b1)], gmul[:, pm, c, v:v + 1], rb[:, 0:n], ALU.mult, ALU.mult,
                    kx([c], range(b0, b1)) + [("gmul", pm), "rb"], [("tmpf", i)])
                if c % 2 == 0:
                    ACT(hT[:, c, tok(b0, b1)], tmpf[:, i, 0:n], AF.Identity, [("tmpf", i), ("mods", pm)],
                        kh([c], range(b0, b1)), bias=mods[:, pm, c, v:v + 1], scale=1.0)
                else:
                    TS(hT[:, c, tok(b0, b1)], tmpf[:, i, 0:n], mods[:, pm, c, v:v + 1], ALU.add,
                       [("tmpf", i), ("mods", pm)], kh([c], range(b0, b1)))

        norm_sched = {}
        nU = len(units_in)

        def nA(u):
            norm_A(units_in[u][0], units_in[u][1], 6 + u % 2)

        def nB(u):
            norm_B(units_in[u][0], units_in[u][1], 6 + u % 2)

        nA(0)
        if nU > 1:
            nA(1)
        nB(0)
        for u in range(nU):
            lst = []
            if u + 2 < nU:
                lst.append(lambda u=u: nA(u + 2))
            if u + 1 < nU:
                lst.append(lambda u=u: nB(u + 1))
            norm_sched[units_in[u][0]] = lst

        stage_gate(4 + 10 * l)
        slot_u = ring_load(win_d[l][:, 1280:1792], 512, 0)
        targets = set(blocks_q)
        pooled_in_unit = {}

        def unit_index_q(tb):
            for ui, (b0, b1) in enumerate(units_q):
                if b0 <= tb < b1:
                    return ui
            return None

        pend = []

        def pool_stage2a(ui):
            b0, b1 = units_q[ui]
            n = (b1 - b0) * 128
            pk = [("pooledT", i) for i in range(b1 - b0)]
            for g in range(4):
                bk = 4 + g % 2
                MM(bank(bk, n), wp[:, pm, g, :], pooledT[:, g, 0:n], True, True, [("wp", pm)] + pk, [("ps", bk)])
                TS(zf[:, g, 0:n], bank(bk, n), c_psc(l, g), ALU.mult, [("ps", bk), "vec"], [("zf", g)])
            pend.append([1, lambda: pool_stage2b(ui)])

        def pool_stage2b(ui):
            b0, b1 = units_q[ui]
            n = (b1 - b0) * 128
            TT(sq[:, :, 0:n], zf[:, :, 0:n], zf[:, :, 0:n], ALU.mult, [("zf", g) for g in range(4)], ["sq"])
            for g in range(4):
                MM(bank(4, n), onesb[:], sq[:, g, 0:n], g == 0, g == 3, ["sq", "onesb"], [("ps", 4)], accum=(g > 0))
            rsqrt_from_stats(bank(4, n), rb[:, 0:n], 512.0, [("ps", 4)], "rb")
            TT(zT[:, :, tok(b0, b1)], zf[:, :, 0:n], rb[:, 0:n].unsqueeze(1).broadcast_to([128, 4, n]), ALU.mult,
               [("zf", g) for g in range(4)] + ["rb"], kz(range(4), range(b0, b1)))

        pcount = {"i": 0}

        def pool_target(tb):
            ui = unit_index_q(tb)
            b0, b1 = units_q[ui]
            if tb < 4:
                dtype_i = 0 if tb % 2 == 0 else 1
                prev_b = tb - 1 if tb % 2 == 1 else None
                next_b = tb + 1 if tb % 2 == 0 else None
                et_prev, et_next = 0, 1
            else:
                dtype_i = 2 if tb == 4 else 3
                prev_b = tb - 1 if tb > 4 else None
                next_b = tb + 1
                et_prev, et_next = 2, 3
            PB = 2 + pcount["i"] % 2
            pcount["i"] += 1
            for g in range(4):
                gs = slice(g * 128, (g + 1) * 128)
                MM(bank(PB, 128, g * 128), utm[:, tb % 4, gs], pmd[:, dtype_i, g, :], True,
                   (prev_b is None and next_b is None), [("utm", tb % 4), "pmd"], [("ps", PB)], accum=(g > 0))
                if prev_b is not None:
                    MM(bank(PB, 16, g * 128), utm[:, prev_b % 4, gs], pme[:, et_prev, g, :], False, next_b is None,
                       [("utm", prev_b % 4), "pme"], [("ps", PB)], accum=True)
                if next_b is not None:
                    MM(bank(PB, 16, g * 128 + 112), utm[:, next_b % 4, gs], pme[:, et_next, g, :], False, True,
                       [("utm", next_b % 4), "pme"], [("ps", PB)], accum=True)
            off = (tb - b0) * 128
            VCOPY(pooledT[:, :, off:off + 128], bank(PB).rearrange("p (g t) -> p g t", g=4), [("ps", PB)],
                  [("pooledT", tb - b0)])
            pooled_in_unit[ui] = pooled_in_unit.get(ui, 0) + 1
            if pooled_in_unit[ui] == b1 - b0:
                pend.append([1, lambda: pool_stage2a(ui)])

        def run_pending(flush=False):
            while True:
                due = [p for p in pend if p[0] <= 0 or flush]
                if not due:
                    break
                p = due[0]
                pend.remove(p)
                p[1]()
            for p in pend:
                p[0] -= 1

        for bi, b in enumerate(blocks_in):
            pb = bi % 2
            for k in range(8):
                MM(bank(pb), hT[:, k, tok(b, b + 1)], ring[:, slot_u, k, :], k == 0, k == 7,
                   [("ring", slot_u)] + kh([k], [b]), [("ps", pb)], accum=(k > 0))
            ACOPY(utm[:, b % 4, :], bank(pb), [("ps", pb)], [("utm", b % 4)])
            for fn in norm_sched.get(b, []):
                fn()
            run_pending()
            if b < 4:
                if b % 2 == 1:
                    pend.append([0, lambda b=b: (pool_target(b - 1), pool_target(b))])
            elif b - 1 >= 4 and (b - 1) in targets:
                pend.append([0, lambda b=b: pool_target(b - 1)])
        run_pending(flush=True)

        stage_gate(5.1 + 10 * l)
        slot_kv = ring_load(win_d[l][:, 512:768], 256, 1)
        DMA("pool", ckt[:], ck_d[l].rearrange("(b p) f -> p b f", p=128), [], ["ckt"], "ckt")
        for cb in range(2):
            e0 = 2 * (16 + cb)
            dstv = vaug[:, 64 + 128 * e0: 64 + 128 * e0 + 256].rearrange("p (g x) -> p g x", g=2)[:, :, 0:64]
            srcv = cv_d[l][cb * 128:(cb + 1) * 128, :].rearrange("p (g d) -> p g d", g=2)
            DMA("pool", dstv, srcv, [], kv([16 + cb]), "cv%d" % cb)
        stage_gate(5.2 + 10 * l)
        ctb = bank(7, 128).bitcast(BF16)
        for cb in range(2):
            TR(ctb[:, cb * 128:(cb + 1) * 128], ckt[:, cb, :], identb[:], ["ckt", "identb"], [("ps", 7)])
        ACOPY(kT[:, NTOK:NTOK + 256], ctb, [("ps", 7)], kk([16, 17]))

        stage_gate(5.3 + 10 * l)
        for bi, b in enumerate(blocks_in):
            pb = 4 + bi % 2
            if b < 4:
                for k in range(8):
                    MM(bank(pb, 256), hT[:, k, tok(b, b + 1)], ring[:, slot_kv, k, 0:256], k == 0, k == 7,
                       [("ring", slot_kv)] + kh([k], [b]), [("ps", pb)], accum=(k > 0))
                ACOPY(kvf[:], bank(pb, 256), [("ps", pb)], ["kvf"])
                out_ops.append(DMA("sp", nk_d[l][b * 128:(b + 1) * 128, :], kvf[:, 0:128], ["kvf"], [], "okv"))
                out_ops.append(DMA("sp", nv_d[l][b * 128:(b + 1) * 128, :], kvf[:, 128:256], ["kvf"], [], "okv"))
                vsrc = bank(pb, 128, 128).rearrange("p (g x) -> p g x", g=2)
            else:
                for k in range(8):
                    MM(bank(pb, 128), hT[:, k, tok(b, b + 1)], ring[:, slot_kv, k, 128:256], k == 0, k == 7,
                       [("ring", slot_kv)] + kh([k], [b]), [("ps", pb)], accum=(k > 0))
                vsrc = bank(pb, 128).rearrange("p (g x) -> p g x", g=2)
            e0 = 2 * b
            dstv = vaug[:, 64 + 128 * e0: 64 + 128 * e0 + 256].rearrange("p (g x) -> p g x", g=2)[:, :, 0:64]
            VCOPY(dstv, vsrc, [("ps", pb)], kv([b]))

        stage_gate(5.4 + 10 * l)

        def evac_k(ui, b0, b1, pb):
            n = (b1 - b0) * 128
            if b0 < 4:
                ACOPY(kT[:, tok(b0, b1)], bank(pb, n), [("ps", pb)], kk(range(b0, b1)))
            else:
                rope_evac(pb, n, (b0 - 4) * 128, kT[:, tok(b0, b1)], kk(range(b0, b1)))
        f_slab(slot_kv, 0, units_in, h_chunk, [0, 1, 2, 3], evac_k)

        stage_gate(5 + 10 * l)
        slot_gp = ring_load(win_d[l][:, 1792:2304], 512, 0)
        for j in range(4):
            banks = [0, 1, 2, 3] if j % 2 == 0 else [4, 5, 6, 7]

            def evac_gp(ui, b0, b1, pb, j=j):
                n = (b1 - b0) * 128
                i = cnt["evac"] % 2
                cnt["evac"] += 1
                ACT(q16[:, i, 0:n], bank(pb, n), AF.Silu, [("ps", pb)], [("q16", i)])
                STT(zT[:, j, tok(b0, b1)], zT[:, j, tok(b0, b1)], c_pn(l, j), q16[:, i, 0:n], ALU.mult, ALU.mult,
                    kz([j], range(b0, b1)) + [("q16", i), "vec"], kz([j], range(b0, b1)))
            f_slab(slot_gp, j * 128, units_q, h_chunk, banks, evac_gp)

        stage_gate(7 + 10 * l)
        slot_q = ring_load(win_d[l][:, 0:512], 512, 1)
        for c in range(4):
            def evac_q(ui, b0, b1, pb, c=c):
                n = (b1 - b0) * 128
                if b0 < 4:
                    ACOPY(qT[:, c, tok(b0, b1)], bank(pb, n), [("ps", pb)], kq([c], range(b0, b1)))
                else:
                    rope_evac(pb, n, (b0 - 4) * 128, qT[:, c, tok(b0, b1)], kq([c], range(b0, b1)))
            f_slab(slot_q, c * 128, units_q, h_chunk, [0, 1, 2, 3], evac_q)
            if c == 1 and l + 1 < DEPTH:
                emit_ada_granule(l + 1, 0, 0)

        stage_gate(8 + 10 * l)
        VCOPY(esb[:].rearrange("p g (h q) -> p (g h) q", h=4), esink[:, l * 8:(l + 1) * 8].unsqueeze(2).broadcast_to([128, 8, 128]),
              ["esink"], ["esb"])
        ada_pending = [1] if l + 1 < DEPTH else []
        ptasks = []
        for qi, qb in enumerate(blocks_q):
            if qb < 4:
                s0 = (qb // 2) * 2
                keys = [(s0, None), (s0 + 1, None)]
            else:
                keys = []
                if qb > 4:
                    keys.append((qb - 1, 0))
                keys.append((qb, None))
                keys.append((qb + 1, 1))
                keys += [(16, None), (17, None)]
            ptasks.append(dict(qi=qi, qb=qb, keys=keys, ob=((4, 5) if qi % 2 == 0 else (6, 7)), ab=qi % 2))
        psteps = [(ti, ki) for ti, t in enumerate(ptasks) for ki in range(len(t["keys"]))]
        last_ps = {}
        for si_, (ti_, ki_) in enumerate(psteps):
            last_ps[ti_] = si_
        deferred = []

        def emit_qk(si):
            ti, ki = psteps[si]
            t = ptasks[ti]
            qb = t["qb"]
            kb, mtype = t["keys"][ki]
            sp = si % 2
            pt = si % 3
            for g in range(2):
                lo, hi = 64 * g, 64 * g + 64
                MM(bank(2 * sp + g), kT[lo:hi, kb * 128:(kb + 1) * 128], qT[lo:hi, :, tok(qb, qb + 1)], True, True,
                   kk([kb]) + kq(range(4), [qb]), [("ps", 2 * sp + g)])
            ACT(PT[:, pt, :], psum[:, 2 * sp * 512:(2 * sp + 2) * 512], AF.Exp, [("ps", 2 * sp), ("ps", 2 * sp + 1)],
                [("PT", pt)], scale=0.125)
            if mtype is not None:
                ptv = PT[:, pt, :].rearrange("p (h q) -> p h q", h=8)
                TT(ptv, ptv, mask2[:, mtype, :].unsqueeze(1).broadcast_to([128, 8, 128]), ALU.mult,
                   [("PT", pt), "mask2"], [("PT", pt)])

        def emit_unit_rmsnorm(ub0, ub1):
            n = (ub1 - ub0) * 128
            uk = kq(range(4), range(ub0, ub1))
            TT(sq[:, :, 0:n], qT[:, :, tok(ub0, ub1)], qT[:, :, tok(ub0, ub1)], ALU.mult, uk, ["sq"])
            for c in range(4):
                MM(bank(3, n), onesb[:], sq[:, c, 0:n], c == 0, c == 3, ["sq", "onesb"], [("ps", 3)], accum=(c > 0))
            rsqrt_from_stats(bank(3, n), rb[:, 0:n], 512.0, [("ps", 3)], "rb")
            TT(qT[:, :, tok(ub0, ub1)], qT[:, :, tok(ub0, ub1)], rb[:, 0:n].unsqueeze(1).broadcast_to([128, 4, n]), ALU.mult,
               uk + ["rb"], uk)

        def emit_pv(si):
            ti, ki = psteps[si]
            t = ptasks[ti]
            qb, ab = t["qb"], t["ab"]
            kb, mtype = t["keys"][ki]
            pt = si % 3
            first, last = (ki == 0), (ki == len(t["keys"]) - 1)
            for g in range(2):
                ob = t["ob"][g]
                en = 2 * kb + g
                MM(bank(ob), vaug[:, 64 + 128 * en: 64 + 128 * en + 128], PT[:, pt, g * 512:(g + 1) * 512], first, last,
                   kv([kb]) + [("PT", pt)], [("ps", ob)], accum=(not first))
            if not last:
                return
            for g in range(2):
                ob = t["ob"][g]
                TT(lnt[64:128, g, :], bank(ob)[64:128, :], esb[64:128, g, :], ALU.add, [("ps", ob), "esb"], [("lnt", g)])

            def finish2(t=t, qb=qb):
                ACT(lnt[64:128, :, :], lnt[64:128, :, :], AF.Ln, [("lnt", 0), ("lnt", 1)], [("lnt", 0), ("lnt", 1)])
                ACT(lnt[64:128, :, :], lnt[64:128, :, :], AF.Exp, [("lnt", 0), ("lnt", 1)], [("lnt", 0), ("lnt", 1)], scale=-1.0)
                for g in range(2):
                    ob = t["ob"][g]
                    TT(qT[0:64, 2 * g:2 * g + 2, tok(qb, qb + 1)], bank(ob, 256)[0:64, :].rearrange("p (j t) -> p j t", j=2),
                       lnt[64:128, g, 0:256].rearrange("p (j t) -> p j t", j=2), ALU.mult,
                       [("ps", ob), ("lnt", g)], kq([2 * g, 2 * g + 1], [qb]))
                    TT(qT[64:128, 2 * g:2 * g + 2, tok(qb, qb + 1)], bank(ob, 256, 256)[0:64, :].rearrange("p (j t) -> p j t", j=2),
                       lnt[64:128, g, 256:512].rearrange("p (j t) -> p j t", j=2), ALU.mult,
                       [("ps", ob), ("lnt", g)], kq([2 * g, 2 * g + 1], [qb]))
            deferred.append((si + 1, finish2))
            for (ub0, ub1) in units_q:
                if qb == ub1 - 1:
                    deferred.append((si + 3, lambda ub0=ub0, ub1=ub1: emit_unit_rmsnorm(ub0, ub1)))
            if ada_pending and t["qi"] >= 3:
                gi = ada_pending.pop(0)
                deferred.append((si + 2, lambda gi=gi: emit_ada_granule(l + 1, gi, 1)))

        nst = len(psteps)
        PIPE = 2
        for si in range(nst + PIPE):
            if si < nst:
                emit_qk(si)
            if si >= PIPE:
                cur = si - PIPE
                emit_pv(cur)
                for d in [d for d in deferred if d[0] <= cur]:
                    deferred.remove(d)
                    d[1]()
        for d in list(deferred):
            d[1]()
        deferred.clear()
        while ada_pending:
            emit_ada_granule(l + 1, ada_pending.pop(0), 1)

        stage_gate(9 + 10 * l)
        slot_ga = ring_load(win_d[l][:, 768:1280], 512, 0)
        if l + 1 < DEPTH:
            emit_ada_granule(l + 1, 2, 1)
        for j in range(4):
            banks = [0, 1, 2, 3] if j % 2 == 0 else [4, 5, 6, 7]

            def evac_ga(ui, b0, b1, pb, j=j):
                n = (b1 - b0) * 128
                i = cnt["evac"] % 2
                cnt["evac"] += 1
                ACT(q16[:, i, 0:n], bank(pb, n), AF.Silu, [("ps", pb)], [("q16", i)])
                STT(qT[:, j, tok(b0, b1)], qT[:, j, tok(b0, b1)], c_an(l, j), q16[:, i, 0:n], ALU.mult, ALU.mult,
                    kq([j], range(b0, b1)) + [("q16", i), "vec"], kq([j], range(b0, b1)))
            f_slab(slot_ga, j * 128, units_q, h_chunk, banks, evac_ga)

        stage_gate(10 + 10 * l)
        for half in range(2):
            slot_o = ring_load(wout_d[l][:, half * 512:(half + 1) * 512], 512, 1 - half)
            for cc in range(4):
                c = half * 4 + cc
                banks = [0, 1, 2, 3] if cc % 2 == 0 else [4, 5, 6, 7]
                if l + 1 < DEPTH and cc == 2:
                    emit_ada_granule(l + 1, 3 + half, half)

                def evac_o(ui, b0, b1, pb, c=c):
                    n = (b1 - b0) * 128
                    v = vsel(b0)
                    STT(xT[:, c, tok(b0, b1)], bank(pb, n), mods[:, pm, 16 + c, v:v + 1], xT[:, c, tok(b0, b1)],
                        ALU.mult, ALU.add, [("ps", pb), ("mods", pm)] + kx([c], range(b0, b1)), kx([c], range(b0, b1)))
                f_slab(slot_o, cc * 128, units_q, a_chunk, banks, evac_o)
        if l + 1 < DEPTH:
            emit_ada_granule(l + 1, 5, 1)
            emit_mods_finish(l + 1)

        if DEBUG_DUMP:
            out_ops.append(DMA("sp", dbg_d[l], xT[:], kx(range(8), range(NBLK)), [], "dbg"))

    def emit_epilogue():
        DMA("sp", fnb, fn_d.broadcast_to([128, D]), [], kh([0], range(NBLK)), "fnb")
        fnk = kh([0], range(NBLK))
        own = list(range(4)) + list(range(4, 12))
        for i, b in enumerate(own):
            pb = (i % 2) * 2
            st = i % 2
            for c in range(8):
                TR(bank(pb + c // 4, 128, (c % 4) * 128), xT[:, c, tok(b, b + 1)], identf[:], kx([c], [b]) + ["identf"],
                   [("ps", pb + c // 4)])
            for hb in range(2):
                ACT(stage[:, st, hb * 512:(hb + 1) * 512], bank(pb + hb), AF.Square, [("ps", pb + hb)],
                    [("stage", st), ("ssq", st, hb)], accum_out=ssq[:, 2 * st + hb: 2 * st + hb + 1])
            s0 = ssq[:, 2 * st:2 * st + 1]
            TT(s0, s0, ssq[:, 2 * st + 1:2 * st + 2], ALU.add, [("ssq", st, 0), ("ssq", st, 1)], [("ssq", st, 0)])
            ACT(s0, s0, AF.Ln, [("ssq", st, 0)], [("ssq", st, 0)], bias=EPS, scale=1.0 / D)
            ACT(s0, s0, AF.Exp, [("ssq", st, 0)], [("ssq", st, 0)], scale=-0.5)
            for hb in range(2):
                STT(stage[:, st, hb * 512:(hb + 1) * 512], bank(pb + hb), s0, fnb[:, hb * 512:(hb + 1) * 512], ALU.mult, ALU.mult,
                    [("ps", pb + hb), ("ssq", st, 0)] + fnk, [("stage", st)])
            dst = yp_d[b * 128:(b + 1) * 128, :] if b < 4 else ys_d[(b - 4) * 128:(b - 3) * 128, :]
            out_ops.append(DMA("sp", dst, stage[:, st, :], [("stage", st)], [], "yout%d" % st))


    try:
        for l in range(DEPTH):
            emit_layer(l)
        stage_gate(50)
        emit_epilogue()
    except StopBuild:
        pass

    P.emit(final_wait_ops=out_ops)
    es.close()
    return nc


_CACHE = {}


def _prep_shared(inp):
    qp, ap = _qperm(), _aperm()
    w_in = np.asarray(inp["w_in"], np.float32)
    colperm = np.concatenate([qp, np.arange(512, 768), 768 + ap, np.arange(1280, 2304)])
    w_in_p = np.ascontiguousarray(w_in[:, :, colperm])
    w_out = np.asarray(inp["w_out"], np.float32)
    rowperm = np.concatenate([ap, np.arange(512, 1024)])
    w_out_p = np.ascontiguousarray(w_out[:, rowperm, :])
    attn_norm_p = np.asarray(inp["attn_norm"], np.float32)[:, ap]
    return w_in_p, w_out_p, attn_norm_p


def kernel(x_prompt, x_sample, cache_k, cache_v, c, c_ctx, norm_w, w_ada, b_ada, w_in, sink,
           attn_norm, pool_norm, w_pool, pool_scale, w_out, final_norm):
    f32 = np.float32
    inp = dict(w_in=w_in, w_out=w_out, attn_norm=attn_norm)
    w_in_p, w_out_p, attn_norm_p = _prep_shared(inp)
    x_prompt = np.asarray(x_prompt, f32)
    x_sample = np.asarray(x_sample, f32)
    cache_k = np.asarray(cache_k, f32)
    cache_v = np.asarray(cache_v, f32)
    c = np.asarray(c, f32)
    c_ctx = np.asarray(c_ctx, f32)
    w_ada = np.ascontiguousarray(np.asarray(w_ada, f32))
    w_pool = np.ascontiguousarray(np.asarray(w_pool, f32))
    b_ada = np.asarray(b_ada, f32)
    norm_w = np.asarray(norm_w, f32)
    pool_norm = np.asarray(pool_norm, f32)
    pool_scale = np.asarray(pool_scale, f32)
    sink = np.asarray(sink, f32)
    final_norm = np.asarray(final_norm, f32)

    bf = ml_dtypes.bfloat16
    ident = np.eye(128, dtype=f32)
    consts = {}
    for rev in (False, True):
        cos, sin = _rope_tables(rev)
        pd, pe = _pool_tables(rev)
        consts[rev] = dict(cosT=cos.astype(bf), sinT=sin.astype(bf), pm_diag=pd.astype(bf), pm_edge=pe.astype(bf))
    mask2 = np.ascontiguousarray(_masks()[:, :, 0, :]).astype(bf)
    permm = _perm_matrix().astype(bf)

    in_maps = []
    for i in range(NCORES):
        b, half = i // 2, i % 2
        rev = half == 1
        if not rev:
            xs = x_sample[b, 0:1536]
        else:
            xs = x_sample[b, ::-1][0:1536]
        vecs = np.zeros((256, 128), f32)
        vecs[0:96] = b_ada.reshape(DEPTH * 24, 128)
        vecs[96:128] = norm_w.reshape(DEPTH * 8, 128)
        vecs[128:144] = attn_norm_p.reshape(DEPTH * 4, 128)
        vecs[144:160] = pool_norm.reshape(DEPTH * 4, 128)
        vecs[160:176] = pool_scale.reshape(DEPTH * 4, 128)
        vecs[176:184] = c_ctx.reshape(8, 128)
        vecs[184:192] = c[b].reshape(8, 128)
        m = dict(
            xp=np.ascontiguousarray(x_prompt[2 * i:2 * i + 2].reshape(512, D)),
            xs=np.ascontiguousarray(xs),
            ck=np.ascontiguousarray(cache_k[b].reshape(DEPTH, 256, 128)),
            cv=np.ascontiguousarray(cache_v[b].reshape(DEPTH, 256, 128)),
            w_ada=w_ada, w_in=w_in_p, w_out=w_out_p, w_pool=w_pool,
            vecs=vecs, sinkv=np.ascontiguousarray(sink.reshape(1, 32)),
            fnorm=np.ascontiguousarray(final_norm.reshape(1, D)),
            ident_f=ident, ident_b=ident.astype(bf), perm=permm, mask2=mask2,
            **consts[rev],
        )
        in_maps.append(m)

    if "nc" not in _CACHE:
        _CACHE["nc"] = build_program()
    nc = _CACHE["nc"]
    res = run_bass_kernel_spmd(nc, in_maps, core_ids=list(range(NCORES)))
    outs = res.results

    y_prompt = np.zeros((16, 256, D), f32)
    y_sample = np.zeros((4, 2048, D), f32)
    new_k = np.zeros((16, DEPTH, 256, 2, 64), f32)
    new_v = np.zeros((16, DEPTH, 256, 2, 64), f32)
    for i in range(NCORES):
        b, half = i // 2, i % 2
        r = outs[i]
        y_prompt[2 * i:2 * i + 2] = np.asarray(r["yp"]).reshape(2, 256, D)
        ys = np.asarray(r["ys"])
        if half == 0:
            y_sample[b, 0:1024] = ys
        else:
            y_sample[b, 1024:2048] = ys[::-1]
        nk = np.asarray(r["nk"]).reshape(DEPTH, 2, 256, 2, 64)
        nv = np.asarray(r["nv"]).reshape(DEPTH, 2, 256, 2, 64)
        new_k[2 * i:2 * i + 2] = nk.transpose(1, 0, 2, 3, 4)
        new_v[2 * i:2 * i + 2] = nv.transpose(1, 0, 2, 3, 4)
    if DEBUG_DUMP:
        kernel.dbg = [np.asarray(o["dbg"]) for o in outs]
    return (y_prompt, y_sample, new_k, new_v)
```

```python
from contextlib import ExitStack
import numpy as np
import ml_dtypes
import concourse.bass as bass
import concourse.mybir as mybir
from concourse.bass_utils import run_bass_kernel_spmd

F32 = mybir.dt.float32
BF16 = mybir.dt.bfloat16
AF = mybir.ActivationFunctionType
ALU = mybir.AluOpType

D = 1024
DEPTH = 4
NCORES = 8
EPS = 1e-6
NTOK = 2048
NBLK = 16
KT_COLS = NTOK + 256
NRING = 2
DEBUG_DUMP = False
STOP = 99


class StopBuild(Exception):
    pass


def stage_gate(n):
    if n > STOP:
        raise StopBuild()

ENGS = ("pe", "act", "dve", "pool", "sp")


class Op:
    __slots__ = ("eng", "fn", "deps", "signal", "tick", "dma_sem", "dma_val")

    def __init__(self, eng, fn):
        self.eng = eng
        self.fn = fn
        self.deps = []
        self.signal = False
        self.tick = None
        self.dma_sem = None
        self.dma_val = None


class Prog:
    def __init__(self, nc):
        self.nc = nc
        self.ops = {e: [] for e in ENGS}
        self.res = {}
        self.dma_tot = {}

    def op(self, eng, fn, reads=(), writes=(), accum=False, dma_sem=None):
        o = Op(eng, fn)
        deps = []
        for r in reads:
            st = self.res.get(r)
            if st is not None and st[0] is not None:
                deps.append(("raw", st[0]))
            if st is not None and isinstance(r, tuple) and r[0] == "ps":
                for rd in st[1]:
                    if rd.eng != eng:
                        deps.append(("raw", rd))
        for w in writes:
            st = self.res.get(w)
            if st is not None:
                if st[0] is not None and not accum:
                    deps.append(("waw", st[0]))
                for rd in st[1]:
                    deps.append(("war", rd))
        seen = set()
        for kind, d in deps:
            if d is o or id(d) in seen:
                continue
            if d.eng == eng and d.dma_sem is None and dma_sem is None:
                if eng == "pe":
                    continue
            seen.add(id(d))
            o.deps.append(d)
        for r in reads:
            self.res.setdefault(r, [None, []])[1].append(o)
        for w in writes:
            st = self.res.setdefault(w, [None, []])
            st[0] = o
            st[1] = []
        if dma_sem is not None:
            o.dma_sem = dma_sem
            self.dma_tot[dma_sem] = self.dma_tot.get(dma_sem, 0) + 16
            o.dma_val = self.dma_tot[dma_sem]
        self.ops[eng].append(o)
        return o

    def emit(self, final_wait_ops=()):
        nc = self.nc
        for e in ENGS:
            for o in self.ops[e]:
                for d in o.deps:
                    d.signal = True
        for o in final_wait_ops:
            o.signal = True
        for e in ENGS:
            t = 0
            for o in self.ops[e]:
                if o.dma_sem is None and o.signal:
                    t += 1
                    o.tick = t
        with ExitStack() as es:
            sems = {}
            for e in ENGS:
                sems[e] = es.enter_context(nc.semaphore("s_" + e))
            for k in self.dma_tot:
                sems[("dma", k)] = es.enter_context(nc.semaphore("d_" + str(k)))
            block = es.enter_context(nc.Block())

            def run(eng_name, e):
                waited = {}
                for o in self.ops[eng_name]:
                    need = {}
                    for d in o.deps:
                        if d.dma_sem is not None:
                            key, val = ("dma", d.dma_sem), d.dma_val
                        else:
                            key, val = d.eng, d.tick
                        if need.get(key, 0) < val:
                            need[key] = val
                    for key, val in need.items():
                        if waited.get(key, 0) < val:
                            e.wait_ge(sems[key], val)
                            waited[key] = val
                    inst = o.fn(e)
                    if o.dma_sem is not None:
                        inst.then_inc(sems[("dma", o.dma_sem)], 16)
                    elif o.signal:
                        inst.then_inc(sems[eng_name], 1)
                if eng_name == "sp":
                    need = {}
                    for o in final_wait_ops:
                        if o.dma_sem is not None:
                            key, val = ("dma", o.dma_sem), o.dma_val
                        else:
                            key, val = o.eng, o.tick
                        need[key] = max(need.get(key, 0), val)
                    for key, val in need.items():
                        e.wait_ge(sems[key], val)

            @block.tensor
            def _(e):
                run("pe", e)

            @block.scalar
            def _(e):
                run("act", e)

            @block.vector
            def _(e):
                run("dve", e)

            @block.gpsimd
            def _(e):
                run("pool", e)

            @block.sync
            def _(e):
                run("sp", e)


def _aperm():
    idx = []
    for ac in range(4):
        g, j = ac // 2, ac % 2
        for h in (4 * g + j, 4 * g + 2 + j):
            idx += [h * 64 + d for d in range(64)]
    return np.array(idx)


def _qperm():
    idx = []
    for c in range(4):
        for h in (c, 4 + c):
            idx += [h * 64 + d for d in range(64)]
    return np.array(idx)


def _pool_op(L, w, n):
    M = np.zeros((n, n + 16), np.float64)
    for t in range(n):
        lo = min(max(t - w // 2, 0), L)
        hi = min(max(t + w // 2, 0), L)
        for s in range(lo, hi):
            if s < n + 16:
                M[t, s] += 1.0 / (hi - lo)
        M[t, t] -= 1.0
    return M


def _pool_tables(reverse):
    pd = np.zeros((128, 4, 4, 128), np.float32)
    pe = np.zeros((128, 4, 4, 16), np.float32)
    for g, w in enumerate((2, 4, 8, 16)):
        M = np.zeros((256, 256))
        for t in range(256):
            lo = min(max(t - w // 2, 0), 256)
            hi = min(max(t + w // 2, 0), 256)
            M[t, lo:hi] += 1.0 / (hi - lo)
            M[t, t] -= 1.0
        pd[:, 0, g, :] = M[0:128, 0:128].T
        pd[:, 1, g, :] = M[128:256, 128:256].T
        pe[:, 0, g, :] = M[128:144, 0:128].T
        pe[:, 1, g, :] = M[112:128, 128:256].T
        L = 2048
        Mg = np.zeros((L, L))
        for t in range(L):
            lo = min(max(t - w // 2, 0), L)
            hi = min(max(t + w // 2, 0), L)
            Mg[t, lo:hi] += 1.0 / (hi - lo)
            Mg[t, t] -= 1.0
        Ml = Mg[::-1, ::-1] if reverse else Mg
        pd[:, 2, g, :] = Ml[0:128, 0:128].T
        pd[:, 3, g, :] = Ml[128:256, 128:256].T
        pe[:, 2, g, :] = Ml[128:144, 0:128].T
        pe[:, 3, g, :] = Ml[240:256, 256:384].T
    return pd, pe


def _rope_tables(reverse):
    j = np.arange(1536)
    t = (2047 - j) if reverse else j
    row = (t // 64).astype(np.float32)
    col = (t % 64).astype(np.float32)
    inv = (10000.0 ** (-(np.arange(0, 32, 2, dtype=np.float32) / 32))).astype(np.float32)
    cos = np.zeros((128, 1536), np.float32)
    sin = np.zeros((128, 1536), np.float32)
    for p in range(128):
        d = p % 64
        pos = row if d < 32 else col
        ang = (pos * inv[d % 16]).astype(np.float32)
        cos[p] = np.cos(ang)
        sin[p] = np.sin(ang)
    return cos, sin


def _perm_matrix():
    pm = np.zeros((128, 128), np.float32)
    for jx in range(128):
        if jx % 32 < 16:
            pm[jx + 16, jx] = -1.0
        else:
            pm[jx - 16, jx] = 1.0
    return pm


def _masks():
    k = np.arange(128)[:, None]
    q = np.arange(128)[None, :]
    m = np.zeros((128, 2, 4, 128), np.float32)
    m[:, 0] = (k >= q)[:, None, :]
    m[:, 1] = (k <= q)[:, None, :]
    return m


def build_program():
    nc = bass.Bass("TRN2", target_bir_lowering=False)

    def din(name, shape, dt=F32):
        return nc.dram_tensor(name, list(shape), dt, kind="ExternalInput").ap()

    def dout(name, shape, dt=F32):
        return nc.dram_tensor(name, list(shape), dt, kind="ExternalOutput").ap()

    xp_d = din("xp", [512, D])
    xs_d = din("xs", [1536, D])
    ck_d = din("ck", [DEPTH, 256, 128])
    cv_d = din("cv", [DEPTH, 256, 128])
    wada_d = din("w_ada", [DEPTH, D, 3 * D])
    win_d = din("w_in", [DEPTH, D, 2304])
    wout_d = din("w_out", [DEPTH, D, D])
    wpool_d = din("w_pool", [DEPTH, 4, 128, 128])
    vecs_d = din("vecs", [256, 128])
    sink_d = din("sinkv", [1, 32])
    fn_d = din("fnorm", [1, D])
    identf_d = din("ident_f", [128, 128])
    identb_d = din("ident_b", [128, 128], BF16)
    perm_d = din("perm", [128, 128], BF16)
    cos_d = din("cosT", [128, 1536], BF16)
    sin_d = din("sinT", [128, 1536], BF16)
    mask_d = din("mask2", [128, 2, 128], BF16)
    pmd_d = din("pm_diag", [128, 4, 4, 128], BF16)
    pme_d = din("pm_edge", [128, 4, 4, 16], BF16)

    yp_d = dout("yp", [512, D])
    ys_d = dout("ys", [1024, D])
    nk_d = dout("nk", [DEPTH, 512, 128])
    nv_d = dout("nv", [DEPTH, 512, 128])
    if DEBUG_DUMP:
        dbg_d = dout("dbg", [DEPTH, 128, 8, NTOK])

    es = ExitStack()

    def sb(name, shape, dt=F32):
        return es.enter_context(nc.sbuf_tensor(name, list(shape), dt))

    psum = es.enter_context(nc.psum_tensor("psum", [128, 4096], F32))

    def bank(b, n=512, off=0):
        return psum[:, b * 512 + off: b * 512 + off + n]

    xT = sb("xT", [128, 8, NTOK])
    hT = sb("hT", [128, 8, NTOK], BF16)
    qT = sb("qT", [128, 4, NTOK], BF16)
    zT = sb("zT", [128, 4, NTOK], BF16)
    kT = sb("kT", [128, KT_COLS], BF16)
    NVA = 64 + 128 * 36
    vaug = sb("vaug", [128, NVA], BF16)
    ring = sb("ring", [128, NRING, 8, 512], BF16)
    AR = sb("arena", [128, 8192], BF16)
    zf = AR[:, 0:4096].bitcast(F32).rearrange("p (g t) -> p g t", g=4)
    stage = AR[:, 0:4096].bitcast(F32).rearrange("p (s t) -> p s t", s=2)
    pooledT = AR[:, 4096:6144].rearrange("p (g t) -> p g t", g=4)
    utm = AR[:, 6144:8192].rearrange("p (s t) -> p s t", s=4)
    PT = AR[:, 0:3072].rearrange("p (s t) -> p s t", s=3)
    atf = AR[:, 3072:5120].bitcast(F32).rearrange("p (a c t) -> p a c t", a=2, c=4)
    lnt = AR[:, 5120:7168].bitcast(F32).rearrange("p (a t) -> p a t", a=2)
    sqa = AR[:, 7168:7680].rearrange("p (c t) -> p c t", c=4)
    rb2 = AR[:, 7680:7936].bitcast(F32)
    vec_in = AR[:, 7168:7680].bitcast(F32).rearrange("p (s t) -> p s t", s=2)
    sq = sb("sq", [128, 4, 512], BF16)
    rb = sb("rb", [128, 512])
    tmpf = sb("tmpf", [128, 3, 512])
    q16 = sb("q16", [128, 2, 512], BF16)
    kvf = sb("kvf", [128, 256])
    ckt = sb("ckt", [128, 2, 128], BF16)
    cosT = sb("cosT_sb", [128, 1536], BF16)
    sinT = sb("sinT_sb", [128, 1536], BF16)
    mask2 = sb("mask2_sb", [128, 2, 128], BF16)
    pmd = sb("pmd_sb", [128, 4, 4, 128], BF16)
    pme = sb("pme_sb", [128, 4, 4, 16], BF16)
    wp = sb("wp_sb", [128, 2, 4, 128], BF16)
    identf = sb("identf_sb", [128, 128])
    identb = sb("identb_sb", [128, 128], BF16)
    onesb = sb("onesb", [128, 128], BF16)
    perm = sb("perm_sb", [128, 128], BF16)
    vec = sb("vec", [128, 256])
    esink = sb("esink", [128, 32])
    sT = sb("sT", [128, 8, 2], BF16)
    mods = sb("mods", [128, 2, 24, 2])
    gmul = sb("gmul", [128, 2, 8, 2])
    ssq = sb("ssq", [128, 4])
    selr = sb("selr", [1, 128], BF16)
    esb = sb("esb", [128, 2, 512], BF16)
    fnb = hT[:, 0, :].bitcast(F32)

    P = Prog(nc)
    cnt = {"dma": 0, "ring": 0, "ps_norm": 0, "evac": 0, "rope": 0}

    def MM(out, lhsT, rhs, start, stop, reads, writes, accum=False):
        return P.op("pe", lambda e: e.matmul(out, lhsT=lhsT, rhs=rhs, start=start, stop=stop),
                    reads=reads, writes=writes, accum=accum)

    def TR(out, in_, ident, reads, writes):
        return P.op("pe", lambda e: e.transpose(out=out, in_=in_, identity=ident), reads=reads, writes=writes)

    def ACT(out, in_, func, reads, writes, bias=None, scale=None, accum_out=None):
        kw = {}
        if bias is not None:
            kw["bias"] = bias
        if scale is not None:
            kw["scale"] = scale
        if accum_out is not None:
            kw["accum_out"] = accum_out
        return P.op("act", lambda e: e.activation(out=out, in_=in_, func=func, **kw), reads=reads, writes=writes)

    def ACOPY(out, in_, reads, writes):
        return P.op("act", lambda e: e.copy(out=out, in_=in_), reads=reads, writes=writes)

    def VCOPY(out, in_, reads, writes):
        return P.op("dve", lambda e: e.tensor_copy(out=out, in_=in_), reads=reads, writes=writes)

    def TT(out, in0, in1, op, reads, writes):
        return P.op("dve", lambda e: e.tensor_tensor(out=out, in0=in0, in1=in1, op=op), reads=reads, writes=writes)

    def STT(out, in0, scalar, in1, op0, op1, reads, writes):
        return P.op("dve", lambda e: e.scalar_tensor_tensor(out=out, in0=in0, scalar=scalar, in1=in1, op0=op0, op1=op1),
                    reads=reads, writes=writes)

    def TS(out, in0, scalar1, op0, reads, writes):
        return P.op("dve", lambda e: e.tensor_scalar(out=out, in0=in0, scalar1=scalar1, scalar2=None, op0=op0),
                    reads=reads, writes=writes)

    def DMA(eng, out, in_, reads, writes, sem):
        return P.op(eng, lambda e: e.dma_start(out=out, in_=in_), reads=reads, writes=writes, dma_sem=sem)

    def newsem(prefix):
        cnt["dma"] += 1
        return "%s%d" % (prefix, cnt["dma"])

    def c_bada(l):
        return vec[:, l * 24:(l + 1) * 24]

    def c_normw(l):
        return vec[:, 96 + l * 8: 96 + l * 8 + 8]

    def c_an(l, j):
        return vec[:, 128 + l * 4 + j: 128 + l * 4 + j + 1]

    def c_pn(l, j):
        return vec[:, 144 + l * 4 + j: 144 + l * 4 + j + 1]

    def c_psc(l, j):
        return vec[:, 160 + l * 4 + j: 160 + l * 4 + j + 1]

    def kx(cs, bs):
        return [("x", c, b) for c in cs for b in bs]

    def kh(cs, bs):
        return [("h", c, b) for c in cs for b in bs]

    def kq(cs, bs):
        return [("q", c, b) for c in cs for b in bs]

    def kz(cs, bs):
        return [("z", c, b) for c in cs for b in bs]

    def kk(bs):
        return [("k", b) for b in bs]

    def kv(bs):
        return [("v", b) for b in bs]

    def tok(b0, b1):
        return slice(b0 * 128, b1 * 128)

    DMA("sp", identf[:], identf_d, [], ["identf"], newsem("c"))
    DMA("sp", identb[:], identb_d, [], ["identb"], newsem("c"))
    DMA("sp", perm[:], perm_d, [], ["perm"], newsem("c"))
    DMA("sp", vec_in[:, 0, :], vecs_d[0:128, :], [], ["vec_in0"], newsem("c"))
    DMA("sp", vec_in[:, 1, :], vecs_d[128:256, :], [], ["vec_in1"], newsem("c"))
    DMA("sp", esink[:], sink_d.broadcast_to([128, 32]), [], ["esink"], newsem("c"))
    DMA("sp", cosT[:], cos_d, [], ["cos"], newsem("c"))
    DMA("sp", sinT[:], sin_d, [], ["sin"], newsem("c"))
    DMA("sp", mask2[:], mask_d, [], ["mask2"], newsem("c"))
    DMA("sp", pmd[:], pmd_d, [], ["pmd"], newsem("c"))
    DMA("sp", pme[:], pme_d, [], ["pme"], newsem("c"))
    P.op("dve", lambda e: e.memset(onesb[:], 1.0), writes=["onesb"])
    P.op("dve", lambda e: e.memset(selr[:, 0:64], 0.0), writes=["selr"])
    P.op("dve", lambda e: e.memset(selr[:, 64:128], 1.0), writes=["selr"])
    P.op("dve", lambda e: e.memset(vaug[:], 1.0), writes=kv(range(18)))

    for i in range(2):
        TR(bank(7, 128, i * 128), vec_in[:, i, :], identf[:], ["vec_in%d" % i, "identf"], [("ps", 7)])
    ACOPY(vec[:], bank(7, 256), [("ps", 7)], ["vec"])
    ACT(sT[:, :, 0], vec[:, 176:184], AF.Silu, ["vec"], ["sT"])
    ACT(sT[:, :, 1], vec[:, 184:192], AF.Silu, ["vec"], ["sT"])
    ACT(esink[:], esink[:], AF.Exp, ["esink"], ["esink"])

    def ring_load(src_ap, ncols, slot):
        src = src_ap.rearrange("(k p) c -> p k c", p=128)
        DMA("pool", ring[:, slot, :, 0:ncols], src, [], [("ring", slot)], "ring%d" % slot)
        return slot

    MODB = 3

    def emit_ada_granule(l, gi, slot):
        ring_load(wada_d[l][:, gi * 512:(gi + 1) * 512], 512, slot)
        pm = l % 2
        for jc in range(4):
            for k in range(8):
                MM(bank(7, 2, jc * 2), ring[:, slot, k, jc * 128:(jc + 1) * 128], sT[:, k, :], k == 0, k == 7,
                   [("ring", slot), "sT"], [("ps", 7)], accum=(k > 0))
        pv = bank(7, 8).rearrange("p (j v) -> p j v", v=2)
        for v in range(2):
            TT(mods[:, pm, gi * 4:(gi + 1) * 4, v], pv[:, :, v], vec[:, l * 24 + gi * 4: l * 24 + gi * 4 + 4], ALU.add,
               [("ps", 7), "vec"], [("mods", pm)])

    def emit_mods_finish(l):
        pm = l % 2
        for v in range(2):
            STT(gmul[:, pm, :, v], mods[:, pm, 8:16, v], 1.0, c_normw(l), ALU.add, ALU.mult,
                [("mods", pm), "vec"], [("gmul", pm)])

    if STOP >= 1:
        for gi in range(6):
            emit_ada_granule(0, gi, gi % 2)
        emit_mods_finish(0)

    for b in (range(NBLK) if STOP >= 2 else []):
        st = b % 2
        src = xp_d[b * 128:(b + 1) * 128, :] if b < 4 else xs_d[(b - 4) * 128:(b - 3) * 128, :]
        DMA("sp", stage[:, st, :], src, [], [("stage", st)], "xin%d" % st)
        pb = (b % 2) * 2
        for c in range(8):
            TR(bank(pb + c // 4, 128, (c % 4) * 128), stage[:, st, c * 128:(c + 1) * 128], identf[:],
               [("stage", st), "identf"], [("ps", pb + c // 4)])
        src_ps = psum[:, pb * 512: pb * 512 + 1024].rearrange("p (c t) -> p c t", c=8)
        if b % 2 == 0:
            ACOPY(xT[:, :, tok(b, b + 1)], src_ps, [("ps", pb), ("ps", pb + 1)], kx(range(8), [b]))
        else:
            VCOPY(xT[:, :, tok(b, b + 1)], src_ps, [("ps", pb), ("ps", pb + 1)], kx(range(8), [b]))

    def units_of(nsamp):
        us = [(0, 4)]
        b = 4
        while b < 4 + nsamp:
            us.append((b, min(b + 4, 4 + nsamp)))
            b += 4
        return us

    def rsqrt_from_stats(ps_ap, dst, scale_div, rd_keys, key):
        ACT(dst, ps_ap, AF.Ln, rd_keys, [key], bias=EPS, scale=1.0 / scale_div)
        ACT(dst, dst, AF.Exp, [key], [key], scale=-0.5)

    def rope_evac(ps_b, n, lt0, dst_ap, dst_keys):
        i = cnt["rope"] % 2
        rbk = 4 + (cnt["rope"] % 4)
        cnt["rope"] += 1
        ACOPY(q16[:, i, 0:n], bank(ps_b, n), [("ps", ps_b)], [("q16", i)])
        MM(bank(rbk, n), perm[:], q16[:, i, 0:n], True, True, [("q16", i), "perm"], [("ps", rbk)])
        TT(tmpf[:, i, 0:n], bank(ps_b, n), cosT[:, lt0:lt0 + n], ALU.mult, [("ps", ps_b), "cos"], [("tmpf", i)])
        TT(tmpf[:, 2, 0:n], bank(rbk, n), sinT[:, lt0:lt0 + n], ALU.mult, [("ps", rbk), "sin"], [("tmpf", 2)])
        TT(dst_ap, tmpf[:, i, 0:n], tmpf[:, 2, 0:n], ALU.add, [("tmpf", i), ("tmpf", 2)], dst_keys)

    def f_slab(slot, col0, units, kchunks_fn, banks, evac_fn):
        for ui, (b0, b1) in enumerate(units):
            n = (b1 - b0) * 128
            pb = banks[ui]
            for k in range(8):
                rhs_ap, rkeys = kchunks_fn(k, b0, b1)
                MM(bank(pb, n), ring[:, slot, k, col0:col0 + 128], rhs_ap, k == 0, k == 7,
                   [("ring", slot)] + rkeys, [("ps", pb)], accum=(k > 0))
        for ui, (b0, b1) in enumerate(units):
            evac_fn(ui, b0, b1, banks[ui])

    def h_chunk(k, b0, b1):
        return hT[:, k, tok(b0, b1)], kh([k], range(b0, b1))

    def a_chunk(k, b0, b1):
        if k < 4:
            return qT[:, k, tok(b0, b1)], kq([k], range(b0, b1))
        return zT[:, k - 4, tok(b0, b1)], kz([k - 4], range(b0, b1))

    def vsel(b0):
        return 0 if b0 < 4 else 1

    out_ops = []

    def emit_layer(l):
        pm = l % 2
        ni = 12 - l
        nq = 11 - l
        units_in = units_of(ni)
        units_q = units_of(nq)
        blocks_in = list(range(4)) + list(range(4, 4 + ni))
        blocks_q = list(range(4)) + list(range(4, 4 + nq))

        DMA("pool", wp[:, pm], wpool_d[l].rearrange("g c d -> c g d"), [], [("wp", pm)], "wp%d" % pm)

        stage_gate(3 + 10 * l)
        def norm_A(b0, b1, pb):
            n = (b1 - b0) * 128
            for hf in range(2):
                ACT(sq[:, :, 0:n], xT[:, hf * 4:hf * 4 + 4, tok(b0, b1)], AF.Square,
                    kx(range(hf * 4, hf * 4 + 4), range(b0, b1)), ["sq"])
                for c4 in range(4):
                    c = hf * 4 + c4
                    MM(bank(pb, n), onesb[:], sq[:, c4, 0:n], c == 0, c == 7, ["sq", "onesb"], [("ps", pb)], accum=(c > 0))

        def norm_B(b0, b1, pb):
            n = (b1 - b0) * 128
            v = vsel(b0)
            rsqrt_from_stats(bank(pb, n), rb[:, 0:n], float(D), [("ps", pb)], "rb")
            for c in range(8):
                i = c % 2
                STT(tmpf[:, i, 0:n], xT[:, c, tok(b0, b1)], gmul[:, pm, c, v:v + 1], rb[:, 0:n], ALU.mult, ALU.mult,
                    kx([c], range(b0, b1)) + [("gmul", pm), "rb"], [("tmpf", i)])
                if c % 2 == 0:
                    ACT(hT[:, c, tok(b0, b1)], tmpf[:, i, 0:n], AF.Identity, [("tmpf", i), ("mods", pm)],
                        kh([c], range(b0, b1)), bias=mods[:, pm, c, v:v + 1], scale=1.0)
                else:
                    TS(hT[:, c, tok(b0, b1)], tmpf[:, i, 0:n], mods[:, pm, c, v:v + 1], ALU.add,
                       [("tmpf", i), ("mods", pm)], kh([c], range(b0, b1)))

        norm_sched = {}
        nU = len(units_in)

        def nA(u):
            norm_A(units_in[u][0], units_in[u][1], 6 + u % 2)

        def nB(u):
            norm_B(units_in[u][0], units_in[u][1], 6 + u % 2)

        nA(0)
        if nU > 1:
            nA(1)
        nB(0)
        for u in range(nU):
            lst = []
            if u + 2 < nU:
                lst.append(lambda u=u: nA(u + 2))
            if u + 1 < nU:
                lst.append(lambda u=u: nB(u + 1))
            norm_sched[units_in[u][0]] = lst

        stage_gate(4 + 10 * l)
        slot_u = ring_load(win_d[l][:, 1280:1792], 512, 0)
        targets = set(blocks_q)
        pooled_in_unit = {}

        def unit_index_q(tb):
            for ui, (b0, b1) in enumerate(units_q):
                if b0 <= tb < b1:
                    return ui
            return None

        pend = []

        def pool_stage2a(ui):
            b0, b1 = units_q[ui]
            n = (b1 - b0) * 128
            pk = [("pooledT", i) for i in range(b1 - b0)]
            for g in range(4):
                bk = 4 + g % 2
                MM(bank(bk, n), wp[:, pm, g, :], pooledT[:, g, 0:n], True, True, [("wp", pm)] + pk, [("ps", bk)])
                TS(zf[:, g, 0:n], bank(bk, n), c_psc(l, g), ALU.mult, [("ps", bk), "vec"], [("zf", g)])
            pend.append([1, lambda: pool_stage2b(ui)])

        def pool_stage2b(ui):
            b0, b1 = units_q[ui]
            n = (b1 - b0) * 128
            TT(sq[:, :, 0:n], zf[:, :, 0:n], zf[:, :, 0:n], ALU.mult, [("zf", g) for g in range(4)], ["sq"])
            for g in range(4):
                MM(bank(4, n), onesb[:], sq[:, g, 0:n], g == 0, g == 3, ["sq", "onesb"], [("ps", 4)], accum=(g > 0))
            rsqrt_from_stats(bank(4, n), rb[:, 0:n], 512.0, [("ps", 4)], "rb")
            TT(zT[:, :, tok(b0, b1)], zf[:, :, 0:n], rb[:, 0:n].unsqueeze(1).broadcast_to([128, 4, n]), ALU.mult,
               [("zf", g) for g in range(4)] + ["rb"], kz(range(4), range(b0, b1)))

        pcount = {"i": 0}

        def pool_target(tb):
            ui = unit_index_q(tb)
            b0, b1 = units_q[ui]
            if tb < 4:
                dtype_i = 0 if tb % 2 == 0 else 1
                prev_b = tb - 1 if tb % 2 == 1 else None
                next_b = tb + 1 if tb % 2 == 0 else None
                et_prev, et_next = 0, 1
            else:
                dtype_i = 2 if tb == 4 else 3
                prev_b = tb - 1 if tb > 4 else None
                next_b = tb + 1
                et_prev, et_next = 2, 3
            PB = 2 + pcount["i"] % 2
            pcount["i"] += 1
            for g in range(4):
                gs = slice(g * 128, (g + 1) * 128)
                MM(bank(PB, 128, g * 128), utm[:, tb % 4, gs], pmd[:, dtype_i, g, :], True,
                   (prev_b is None and next_b is None), [("utm", tb % 4), "pmd"], [("ps", PB)], accum=(g > 0))
                if prev_b is not None:
                    MM(bank(PB, 16, g * 128), utm[:, prev_b % 4, gs], pme[:, et_prev, g, :], False, next_b is None,
                       [("utm", prev_b % 4), "pme"], [("ps", PB)], accum=True)
                if next_b is not None:
                    MM(bank(PB, 16, g * 128 + 112), utm[:, next_b % 4, gs], pme[:, et_next, g, :], False, True,
                       [("utm", next_b % 4), "pme"], [("ps", PB)], accum=True)
            off = (tb - b0) * 128
            VCOPY(pooledT[:, :, off:off + 128], bank(PB).rearrange("p (g t) -> p g t", g=4), [("ps", PB)],
                  [("pooledT", tb - b0)])
            pooled_in_unit[ui] = pooled_in_unit.get(ui, 0) + 1
            if pooled_in_unit[ui] == b1 - b0:
                pend.append([1, lambda: pool_stage2a(ui)])

        def run_pending(flush=False):
            while True:
                due = [p for p in pend if p[0] <= 0 or flush]
                if not due:
                    break
                p = due[0]
                pend.remove(p)
                p[1]()
            for p in pend:
                p[0] -= 1

        for bi, b in enumerate(blocks_in):
            pb = bi % 2
            for k in range(8):
                MM(bank(pb), hT[:, k, tok(b, b + 1)], ring[:, slot_u, k, :], k == 0, k == 7,
                   [("ring", slot_u)] + kh([k], [b]), [("ps", pb)], accum=(k > 0))
            ACOPY(utm[:, b % 4, :], bank(pb), [("ps", pb)], [("utm", b % 4)])
            for fn in norm_sched.get(b, []):
                fn()
            run_pending()
            if b < 4:
                if b % 2 == 1:
                    pend.append([0, lambda b=b: (pool_target(b - 1), pool_target(b))])
            elif b - 1 >= 4 and (b - 1) in targets:
                pend.append([0, lambda b=b: pool_target(b - 1)])
        run_pending(flush=True)

        stage_gate(5.1 + 10 * l)
        slot_kv = ring_load(win_d[l][:, 512:768], 256, 1)
        DMA("pool", ckt[:], ck_d[l].rearrange("(b p) f -> p b f", p=128), [], ["ckt"], "ckt")
        for cb in range(2):
            e0 = 2 * (16 + cb)
            dstv = vaug[:, 64 + 128 * e0: 64 + 128 * e0 + 256].rearrange("p (g x) -> p g x", g=2)[:, :, 0:64]
            srcv = cv_d[l][cb * 128:(cb + 1) * 128, :].rearrange("p (g d) -> p g d", g=2)
            DMA("pool", dstv, srcv, [], kv([16 + cb]), "cv%d" % cb)
        stage_gate(5.2 + 10 * l)
        ctb = bank(7, 128).bitcast(BF16)
        for cb in range(2):
            TR(ctb[:, cb * 128:(cb + 1) * 128], ckt[:, cb, :], identb[:], ["ckt", "identb"], [("ps", 7)])
        ACOPY(kT[:, NTOK:NTOK + 256], ctb, [("ps", 7)], kk([16, 17]))

        stage_gate(5.3 + 10 * l)
        for bi, b in enumerate(blocks_in):
            pb = 4 + bi % 2
            if b < 4:
                for k in range(8):
                    MM(bank(pb, 256), hT[:, k, tok(b, b + 1)], ring[:, slot_kv, k, 0:256], k == 0, k == 7,
                       [("ring", slot_kv)] + kh([k], [b]), [("ps", pb)], accum=(k > 0))
                ACOPY(kvf[:], bank(pb, 256), [("ps", pb)], ["kvf"])
                out_ops.append(DMA("sp", nk_d[l][b * 128:(b + 1) * 128, :], kvf[:, 0:128], ["kvf"], [], "okv"))
                out_ops.append(DMA("sp", nv_d[l][b * 128:(b + 1) * 128, :], kvf[:, 128:256], ["kvf"], [], "okv"))
                vsrc = bank(pb, 128, 128).rearrange("p (g x) -> p g x", g=2)
            else:
                for k in range(8):
                    MM(bank(pb, 128), hT[:, k, tok(b, b + 1)], ring[:, slot_kv, k, 128:256], k == 0, k == 7,
                       [("ring", slot_kv)] + kh([k], [b]), [("ps", pb)], accum=(k > 0))
                vsrc = bank(pb, 128).rearrange("p (g x) -> p g x", g=2)
            e0 = 2 * b
            dstv = vaug[:, 64 + 128 * e0: 64 + 128 * e0 + 256].rearrange("p (g x) -> p g x", g=2)[:, :, 0:64]
            VCOPY(dstv, vsrc, [("ps", pb)], kv([b]))

        stage_gate(5.4 + 10 * l)

        def evac_k(ui, b0, b1, pb):
            n = (b1 - b0) * 128
            if b0 < 4:
                ACOPY(kT[:, tok(b0, b1)], bank(pb, n), [("ps", pb)], kk(range(b0, b1)))
            else:
                rope_evac(pb, n, (b0 - 4) * 128, kT[:, tok(b0, b1)], kk(range(b0, b1)))
        f_slab(slot_kv, 0, units_in, h_chunk, [0, 1, 2, 3], evac_k)

        stage_gate(5 + 10 * l)
        slot_gp = ring_load(win_d[l][:, 1792:2304], 512, 0)
        for j in range(4):
            banks = [0, 1, 2, 3] if j % 2 == 0 else [4, 5, 6, 7]

            def evac_gp(ui, b0, b1, pb, j=j):
                n = (b1 - b0) * 128
                i = cnt["evac"] % 2
                cnt["evac"] += 1
                ACT(q16[:, i, 0:n], bank(pb, n), AF.Silu, [("ps", pb)], [("q16", i)])
                STT(zT[:, j, tok(b0, b1)], zT[:, j, tok(b0, b1)], c_pn(l, j), q16[:, i, 0:n], ALU.mult, ALU.mult,
                    kz([j], range(b0, b1)) + [("q16", i), "vec"], kz([j], range(b0, b1)))
            f_slab(slot_gp, j * 128, units_q, h_chunk, banks, evac_gp)

        stage_gate(7 + 10 * l)
        slot_q = ring_load(win_d[l][:, 0:512], 512, 1)
        for c in range(4):
            def evac_q(ui, b0, b1, pb, c=c):
                n = (b1 - b0) * 128
                if b0 < 4:
                    ACOPY(qT[:, c, tok(b0, b1)], bank(pb, n), [("ps", pb)], kq([c], range(b0, b1)))
                else:
                    rope_evac(pb, n, (b0 - 4) * 128, qT[:, c, tok(b0, b1)], kq([c], range(b0, b1)))
            f_slab(slot_q, c * 128, units_q, h_chunk, [0, 1, 2, 3], evac_q)
            if c == 1 and l + 1 < DEPTH:
                emit_ada_granule(l + 1, 0, 0)

        stage_gate(8 + 10 * l)
        VCOPY(esb[:].rearrange("p g (h q) -> p (g h) q", h=4), esink[:, l * 8:(l + 1) * 8].unsqueeze(2).broadcast_to([128, 8, 128]),
              ["esink"], ["esb"])
        ada_pending = [1] if l + 1 < DEPTH else []
        ptasks = []
        for qi, qb in enumerate(blocks_q):
            if qb < 4:
                s0 = (qb // 2) * 2
                keys = [(s0, None), (s0 + 1, None)]
            else:
                keys = []
                if qb > 4:
                    keys.append((qb - 1, 0))
                keys.append((qb, None))
                keys.append((qb + 1, 1))
                keys += [(16, None), (17, None)]
            ptasks.append(dict(qi=qi, qb=qb, keys=keys, ob=((4, 5) if qi % 2 == 0 else (6, 7)), ab=qi % 2))
        psteps = [(ti, ki) for ti, t in enumerate(ptasks) for ki in range(len(t["keys"]))]
        last_ps = {}
        for si_, (ti_, ki_) in enumerate(psteps):
            last_ps[ti_] = si_
        deferred = []

        def emit_qk(si):
            ti, ki = psteps[si]
            t = ptasks[ti]
            qb = t["qb"]
            kb, mtype = t["keys"][ki]
            sp = si % 2
            pt = si % 3
            for g in range(2):
                lo, hi = 64 * g, 64 * g + 64
                MM(bank(2 * sp + g), kT[lo:hi, kb * 128:(kb + 1) * 128], qT[lo:hi, :, tok(qb, qb + 1)], True, True,
                   kk([kb]) + kq(range(4), [qb]), [("ps", 2 * sp + g)])
            ACT(PT[:, pt, :], psum[:, 2 * sp * 512:(2 * sp + 2) * 512], AF.Exp, [("ps", 2 * sp), ("ps", 2 * sp + 1)],
                [("PT", pt)], scale=0.125)
            if mtype is not None:
                ptv = PT[:, pt, :].rearrange("p (h q) -> p h q", h=8)
                TT(ptv, ptv, mask2[:, mtype, :].unsqueeze(1).broadcast_to([128, 8, 128]), ALU.mult,
                   [("PT", pt), "mask2"], [("PT", pt)])

        def emit_unit_rmsnorm(ub0, ub1):
            n = (ub1 - ub0) * 128
            uk = kq(range(4), range(ub0, ub1))
            TT(sq[:, :, 0:n], qT[:, :, tok(ub0, ub1)], qT[:, :, tok(ub0, ub1)], ALU.mult, uk, ["sq"])
            for c in range(4):
                MM(bank(3, n), onesb[:], sq[:, c, 0:n], c == 0, c == 3, ["sq", "onesb"], [("ps", 3)], accum=(c > 0))
            rsqrt_from_stats(bank(3, n), rb[:, 0:n], 512.0, [("ps", 3)], "rb")
            TT(qT[:, :, tok(ub0, ub1)], qT[:, :, tok(ub0, ub1)], rb[:, 0:n].unsqueeze(1).broadcast_to([128, 4, n]), ALU.mult,
               uk + ["rb"], uk)

        def emit_pv(si):
            ti, ki = psteps[si]
            t = ptasks[ti]
            qb, ab = t["qb"], t["ab"]
            kb, mtype = t["keys"][ki]
            pt = si % 3
            first, last = (ki == 0), (ki == len(t["keys"]) - 1)
            for g in range(2):
                ob = t["ob"][g]
                en = 2 * kb + g
                MM(bank(ob), vaug[:, 64 + 128 * en: 64 + 128 * en + 128], PT[:, pt, g * 512:(g + 1) * 512], first, last,
                   kv([kb]) + [("PT", pt)], [("ps", ob)], accum=(not first))
            if not last:
                return
            for g in range(2):
                ob = t["ob"][g]
                TT(lnt[64:128, g, :], bank(ob)[64:128, :], esb[64:128, g, :], ALU.add, [("ps", ob), "esb"], [("lnt", g)])

            def finish2(t=t, qb=qb):
                ACT(lnt[64:128, :, :], lnt[64:128, :, :], AF.Ln, [("lnt", 0), ("lnt", 1)], [("lnt", 0), ("lnt", 1)])
                ACT(lnt[64:128, :, :], lnt[64:128, :, :], AF.Exp, [("lnt", 0), ("lnt", 1)], [("lnt", 0), ("lnt", 1)], scale=-1.0)
                for g in range(2):
                    ob = t["ob"][g]
                    TT(qT[0:64, 2 * g:2 * g + 2, tok(qb, qb + 1)], bank(ob, 256)[0:64, :].rearrange("p (j t) -> p j t", j=2),
                       lnt[64:128, g, 0:256].rearrange("p (j t) -> p j t", j=2), ALU.mult,
                       [("ps", ob), ("lnt", g)], kq([2 * g, 2 * g + 1], [qb]))
                    TT(qT[64:128, 2 * g:2 * g + 2, tok(qb, qb + 1)], bank(ob, 256, 256)[0:64, :].rearrange("p (j t) -> p j t", j=2),
                       lnt[64:128, g, 256:512].rearrange("p (j t) -> p j t", j=2), ALU.mult,
                       [("ps", ob), ("lnt", g)], kq([2 * g, 2 * g + 1], [qb]))
            deferred.append((si + 1, finish2))
            for (ub0, ub1) in units_q:
                if qb == ub1 - 1:
                    deferred.append((si + 3, lambda ub0=ub0, ub1=ub1: emit_unit_rmsnorm(ub0, ub1)))
            if ada_pending and t["qi"] >= 3:
                gi = ada_pending.pop(0)
                deferred.append((si + 2, lambda gi=gi: emit_ada_granule(l + 1, gi, 1)))

        nst = len(psteps)
        PIPE = 2
        for si in range(nst + PIPE):
            if si < nst:
                emit_qk(si)
            if si >= PIPE:
                cur = si - PIPE
                emit_pv(cur)
                for d in [d for d in deferred if d[0] <= cur]:
                    deferred.remove(d)
                    d[1]()
        while ada_pending:
            emit_ada_granule(l + 1, ada_pending.pop(0), 1)

        stage_gate(9 + 10 * l)
        slot_ga = ring_load(win_d[l][:, 768:1280], 512, 0)

        def ga_mm(j, ulist, banks):
            for ui, (b0, b1) in ulist:
                n = (b1 - b0) * 128
                pb = banks[ui]
                for k in range(8):
                    MM(bank(pb, n), ring[:, slot_ga, k, j * 128:(j + 1) * 128], hT[:, k, tok(b0, b1)], k == 0, k == 7,
                       [("ring", slot_ga)] + kh([k], range(b0, b1)), [("ps", pb)], accum=(k > 0))

        def evac_ga(ui, b0, b1, pb, j):
            n = (b1 - b0) * 128
            i = cnt["evac"] % 2
            cnt["evac"] += 1
            ACT(q16[:, i, 0:n], bank(pb, n), AF.Silu, [("ps", pb)], [("q16", i)])
            STT(qT[:, j, tok(b0, b1)], qT[:, j, tok(b0, b1)], c_an(l, j), q16[:, i, 0:n], ALU.mult, ALU.mult,
                kq([j], range(b0, b1)) + [("q16", i), "vec"], kq([j], range(b0, b1)))

        ulist = list(enumerate(units_q))
        ga_mm(0, ulist[:3], [0, 1, 2, 3])
        for d in list(deferred):
            d[1]()
        deferred.clear()
        if l + 1 < DEPTH:
            emit_ada_granule(l + 1, 2, 1)
        ga_mm(0, ulist[3:], [0, 1, 2, 3])
        for ui, (b0, b1) in ulist:
            evac_ga(ui, b0, b1, [0, 1, 2, 3][ui], 0)
        for j in range(1, 4):
            banks = [0, 1, 2, 3] if j % 2 == 0 else [4, 5, 6, 7]
            f_slab(slot_ga, j * 128, units_q, h_chunk, banks, lambda ui, b0, b1, pb, j=j: evac_ga(ui, b0, b1, pb, j))

        stage_gate(10 + 10 * l)
        for half in range(2):
            slot_o = ring_load(wout_d[l][:, half * 512:(half + 1) * 512], 512, 1 - half)
            for cc in range(4):
                c = half * 4 + cc
                banks = [0, 1, 2, 3] if cc % 2 == 0 else [4, 5, 6, 7]
                if l + 1 < DEPTH and cc == 2:
                    emit_ada_granule(l + 1, 3 + half, half)

                def evac_o(ui, b0, b1, pb, c=c):
                    n = (b1 - b0) * 128
                    v = vsel(b0)
                    STT(xT[:, c, tok(b0, b1)], bank(pb, n), mods[:, pm, 16 + c, v:v + 1], xT[:, c, tok(b0, b1)],
                        ALU.mult, ALU.add, [("ps", pb), ("mods", pm)] + kx([c], range(b0, b1)), kx([c], range(b0, b1)))
                f_slab(slot_o, cc * 128, units_q, a_chunk, banks, evac_o)
        if l + 1 < DEPTH:
            emit_ada_granule(l + 1, 5, 1)
            emit_mods_finish(l + 1)

        if DEBUG_DUMP:
            out_ops.append(DMA("sp", dbg_d[l], xT[:], kx(range(8), range(NBLK)), [], "dbg"))

    def emit_epilogue():
        DMA("sp", fnb, fn_d.broadcast_to([128, D]), [], kh([0], range(NBLK)), "fnb")
        fnk = kh([0], range(NBLK))
        own = list(range(4)) + list(range(4, 12))
        for i, b in enumerate(own):
            pb = (i % 2) * 2
            st = i % 2
            for c in range(8):
                TR(bank(pb + c // 4, 128, (c % 4) * 128), xT[:, c, tok(b, b + 1)], identf[:], kx([c], [b]) + ["identf"],
                   [("ps", pb + c // 4)])
            for hb in range(2):
                ACT(stage[:, st, hb * 512:(hb + 1) * 512], bank(pb + hb), AF.Square, [("ps", pb + hb)],
                    [("stage", st), ("ssq", st, hb)], accum_out=ssq[:, 2 * st + hb: 2 * st + hb + 1])
            s0 = ssq[:, 2 * st:2 * st + 1]
            TT(s0, s0, ssq[:, 2 * st + 1:2 * st + 2], ALU.add, [("ssq", st, 0), ("ssq", st, 1)], [("ssq", st, 0)])
            ACT(s0, s0, AF.Ln, [("ssq", st, 0)], [("ssq", st, 0)], bias=EPS, scale=1.0 / D)
            ACT(s0, s0, AF.Exp, [("ssq", st, 0)], [("ssq", st, 0)], scale=-0.5)
            for hb in range(2):
                STT(stage[:, st, hb * 512:(hb + 1) * 512], bank(pb + hb), s0, fnb[:, hb * 512:(hb + 1) * 512], ALU.mult, ALU.mult,
                    [("ps", pb + hb), ("ssq", st, 0)] + fnk, [("stage", st)])
            dst = yp_d[b * 128:(b + 1) * 128, :] if b < 4 else ys_d[(b - 4) * 128:(b - 3) * 128, :]
            out_ops.append(DMA("sp", dst, stage[:, st, :], [("stage", st)], [], "yout%d" % st))


    try:
        for l in range(DEPTH):
            emit_layer(l)
        stage_gate(50)
        emit_epilogue()
    except StopBuild:
        pass

    P.emit(final_wait_ops=out_ops)
    es.close()
    return nc


_CACHE = {}


def _prep_shared(inp):
    qp, ap = _qperm(), _aperm()
    w_in = np.asarray(inp["w_in"], np.float32)
    colperm = np.concatenate([qp, np.arange(512, 768), 768 + ap, np.arange(1280, 2304)])
    w_in_p = np.ascontiguousarray(w_in[:, :, colperm])
    w_out = np.asarray(inp["w_out"], np.float32)
    rowperm = np.concatenate([ap, np.arange(512, 1024)])
    w_out_p = np.ascontiguousarray(w_out[:, rowperm, :])
    attn_norm_p = np.asarray(inp["attn_norm"], np.float32)[:, ap]
    return w_in_p, w_out_p, attn_norm_p


def kernel(x_prompt, x_sample, cache_k, cache_v, c, c_ctx, norm_w, w_ada, b_ada, w_in, sink,
           attn_norm, pool_norm, w_pool, pool_scale, w_out, final_norm):
    f32 = np.float32
    inp = dict(w_in=w_in, w_out=w_out, attn_norm=attn_norm)
    w_in_p, w_out_p, attn_norm_p = _prep_shared(inp)
    x_prompt = np.asarray(x_prompt, f32)
    x_sample = np.asarray(x_sample, f32)
    cache_k = np.asarray(cache_k, f32)
    cache_v = np.asarray(cache_v, f32)
    c = np.asarray(c, f32)
    c_ctx = np.asarray(c_ctx, f32)
    w_ada = np.ascontiguousarray(np.asarray(w_ada, f32))
    w_pool = np.ascontiguousarray(np.asarray(w_pool, f32))
    b_ada = np.asarray(b_ada, f32)
    norm_w = np.asarray(norm_w, f32)
    pool_norm = np.asarray(pool_norm, f32)
    pool_scale = np.asarray(pool_scale, f32)
    sink = np.asarray(sink, f32)
    final_norm = np.asarray(final_norm, f32)

    bf = ml_dtypes.bfloat16
    ident = np.eye(128, dtype=f32)
    consts = {}
    for rev in (False, True):
        cos, sin = _rope_tables(rev)
        pd, pe = _pool_tables(rev)
        consts[rev] = dict(cosT=cos.astype(bf), sinT=sin.astype(bf), pm_diag=pd.astype(bf), pm_edge=pe.astype(bf))
    mask2 = np.ascontiguousarray(_masks()[:, :, 0, :]).astype(bf)
    permm = _perm_matrix().astype(bf)

    in_maps = []
    for i in range(NCORES):
        b, half = i // 2, i % 2
        rev = half == 1
        if not rev:
            xs = x_sample[b, 0:1536]
        else:
            xs = x_sample[b, ::-1][0:1536]
        vecs = np.zeros((256, 128), f32)
        vecs[0:96] = b_ada.reshape(DEPTH * 24, 128)
        vecs[96:128] = norm_w.reshape(DEPTH * 8, 128)
        vecs[128:144] = attn_norm_p.reshape(DEPTH * 4, 128)
        vecs[144:160] = pool_norm.reshape(DEPTH * 4, 128)
        vecs[160:176] = pool_scale.reshape(DEPTH * 4, 128)
        vecs[176:184] = c_ctx.reshape(8, 128)
        vecs[184:192] = c[b].reshape(8, 128)
        m = dict(
            xp=np.ascontiguousarray(x_prompt[2 * i:2 * i + 2].reshape(512, D)),
            xs=np.ascontiguousarray(xs),
            ck=np.ascontiguousarray(cache_k[b].reshape(DEPTH, 256, 128)),
            cv=np.ascontiguousarray(cache_v[b].reshape(DEPTH, 256, 128)),
            w_ada=w_ada, w_in=w_in_p, w_out=w_out_p, w_pool=w_pool,
            vecs=vecs, sinkv=np.ascontiguousarray(sink.reshape(1, 32)),
            fnorm=np.ascontiguousarray(final_norm.reshape(1, D)),
            ident_f=ident, ident_b=ident.astype(bf), perm=permm, mask2=mask2,
            **consts[rev],
        )
        in_maps.append(m)

    if "nc" not in _CACHE:
        _CACHE["nc"] = build_program()
    nc = _CACHE["nc"]
    res = run_bass_kernel_spmd(nc, in_maps, core_ids=list(range(NCORES)))
    outs = res.results

    y_prompt = np.zeros((16, 256, D), f32)
    y_sample = np.zeros((4, 2048, D), f32)
    new_k = np.zeros((16, DEPTH, 256, 2, 64), f32)
    new_v = np.zeros((16, DEPTH, 256, 2, 64), f32)
    for i in range(NCORES):
        b, half = i // 2, i % 2
        r = outs[i]
        y_prompt[2 * i:2 * i + 2] = np.asarray(r["yp"]).reshape(2, 256, D)
        ys = np.asarray(r["ys"])
        if half == 0:
            y_sample[b, 0:1024] = ys
        else:
            y_sample[b, 1024:2048] = ys[::-1]
        nk = np.asarray(r["nk"]).reshape(DEPTH, 2, 256, 2, 64)
        nv = np.asarray(r["nv"]).reshape(DEPTH, 2, 256, 2, 64)
        new_k[2 * i:2 * i + 2] = nk.transpose(1, 0, 2, 3, 4)
        new_v[2 * i:2 * i + 2] = nv.transpose(1, 0, 2, 3, 4)
    if DEBUG_DUMP:
        kernel.dbg = [np.asarray(o["dbg"]) for o in outs]
    return (y_prompt, y_sample, new_k, new_v)
```

```python
from contextlib import ExitStack
import numpy as np
import ml_dtypes
import concourse.bass as bass
import concourse.mybir as mybir
from concourse.bass_utils import run_bass_kernel_spmd

F32 = mybir.dt.float32
BF16 = mybir.dt.bfloat16
AF = mybir.ActivationFunctionType
ALU = mybir.AluOpType

D = 1024
DEPTH = 4
NCORES = 8
EPS = 1e-6
NTOK = 2048
NBLK = 16
KT_COLS = NTOK + 256
NRING = 2
DEBUG_DUMP = False
STOP = 99


class StopBuild(Exception):
    pass


def stage_gate(n):
    if n > STOP:
        raise StopBuild()

ENGS = ("pe", "act", "dve", "pool", "sp")


class Op:
    __slots__ = ("eng", "fn", "deps", "signal", "tick", "dma_sem", "dma_val")

    def __init__(self, eng, fn):
        self.eng = eng
        self.fn = fn
        self.deps = []
        self.signal = False
        self.tick = None
        self.dma_sem = None
        self.dma_val = None


class Prog:
    def __init__(self, nc):
        self.nc = nc
        self.ops = {e: [] for e in ENGS}
        self.res = {}
        self.dma_tot = {}

    def op(self, eng, fn, reads=(), writes=(), accum=False, dma_sem=None):
        o = Op(eng, fn)
        deps = []
        for r in reads:
            st = self.res.get(r)
            if st is not None and st[0] is not None:
                deps.append(("raw", st[0]))
            if st is not None and isinstance(r, tuple) and r[0] == "ps":
                for rd in st[1]:
                    if rd.eng != eng:
                        deps.append(("raw", rd))
        for w in writes:
            st = self.res.get(w)
            if st is not None:
                if st[0] is not None and not accum:
                    deps.append(("waw", st[0]))
                for rd in st[1]:
                    deps.append(("war", rd))
        seen = set()
        for kind, d in deps:
            if d is o or id(d) in seen:
                continue
            if d.eng == eng and d.dma_sem is None and dma_sem is None:
                if eng == "pe":
                    continue
            seen.add(id(d))
            o.deps.append(d)
        for r in reads:
            self.res.setdefault(r, [None, []])[1].append(o)
        for w in writes:
            st = self.res.setdefault(w, [None, []])
            st[0] = o
            st[1] = []
        if dma_sem is not None:
            o.dma_sem = dma_sem
            self.dma_tot[dma_sem] = self.dma_tot.get(dma_sem, 0) + 16
            o.dma_val = self.dma_tot[dma_sem]
        self.ops[eng].append(o)
        return o

    def emit(self, final_wait_ops=()):
        nc = self.nc
        for e in ENGS:
            for o in self.ops[e]:
                for d in o.deps:
                    d.signal = True
        for o in final_wait_ops:
            o.signal = True
        for e in ENGS:
            t = 0
            for o in self.ops[e]:
                if o.dma_sem is None and o.signal:
                    t += 1
                    o.tick = t
        with ExitStack() as es:
            sems = {}
            for e in ENGS:
                sems[e] = es.enter_context(nc.semaphore("s_" + e))
            for k in self.dma_tot:
                sems[("dma", k)] = es.enter_context(nc.semaphore("d_" + str(k)))
            block = es.enter_context(nc.Block())

            def run(eng_name, e):
                waited = {}
                for o in self.ops[eng_name]:
                    need = {}
                    for d in o.deps:
                        if d.dma_sem is not None:
                            key, val = ("dma", d.dma_sem), d.dma_val
                        else:
                            key, val = d.eng, d.tick
                        if need.get(key, 0) < val:
                            need[key] = val
                    for key, val in need.items():
                        if waited.get(key, 0) < val:
                            e.wait_ge(sems[key], val)
                            waited[key] = val
                    inst = o.fn(e)
                    if o.dma_sem is not None:
                        inst.then_inc(sems[("dma", o.dma_sem)], 16)
                    elif o.signal:
                        inst.then_inc(sems[eng_name], 1)
                if eng_name == "sp":
                    need = {}
                    for o in final_wait_ops:
                        if o.dma_sem is not None:
                            key, val = ("dma", o.dma_sem), o.dma_val
                        else:
                            key, val = o.eng, o.tick
                        need[key] = max(need.get(key, 0), val)
                    for key, val in need.items():
                        e.wait_ge(sems[key], val)

            @block.tensor
            def _(e):
                run("pe", e)

            @block.scalar
            def _(e):
                run("act", e)

            @block.vector
            def _(e):
                run("dve", e)

            @block.gpsimd
            def _(e):
                run("pool", e)

            @block.sync
            def _(e):
                run("sp", e)


def _aperm():
    idx = []
    for ac in range(4):
        g, j = ac // 2, ac % 2
        for h in (4 * g + j, 4 * g + 2 + j):
            idx += [h * 64 + d for d in range(64)]
    return np.array(idx)


def _qperm():
    idx = []
    for c in range(4):
        for h in (c, 4 + c):
            idx += [h * 64 + d for d in range(64)]
    return np.array(idx)


def _pool_op(L, w, n):
    M = np.zeros((n, n + 16), np.float64)
    for t in range(n):
        lo = min(max(t - w // 2, 0), L)
        hi = min(max(t + w // 2, 0), L)
        for s in range(lo, hi):
            if s < n + 16:
                M[t, s] += 1.0 / (hi - lo)
        M[t, t] -= 1.0
    return M


def _pool_tables(reverse):
    pd = np.zeros((128, 4, 4, 128), np.float32)
    pe = np.zeros((128, 4, 4, 16), np.float32)
    for g, w in enumerate((2, 4, 8, 16)):
        M = np.zeros((256, 256))
        for t in range(256):
            lo = min(max(t - w // 2, 0), 256)
            hi = min(max(t + w // 2, 0), 256)
            M[t, lo:hi] += 1.0 / (hi - lo)
            M[t, t] -= 1.0
        pd[:, 0, g, :] = M[0:128, 0:128].T
        pd[:, 1, g, :] = M[128:256, 128:256].T
        pe[:, 0, g, :] = M[128:144, 0:128].T
        pe[:, 1, g, :] = M[112:128, 128:256].T
        L = 2048
        Mg = np.zeros((L, L))
        for t in range(L):
            lo = min(max(t - w // 2, 0), L)
            hi = min(max(t + w // 2, 0), L)
            Mg[t, lo:hi] += 1.0 / (hi - lo)
            Mg[t, t] -= 1.0
        Ml = Mg[::-1, ::-1] if reverse else Mg
        pd[:, 2, g, :] = Ml[0:128, 0:128].T
        pd[:, 3, g, :] = Ml[128:256, 128:256].T
        pe[:, 2, g, :] = Ml[128:144, 0:128].T
        pe[:, 3, g, :] = Ml[240:256, 256:384].T
    return pd, pe


def _rope_tables(reverse):
    j = np.arange(1536)
    t = (2047 - j) if reverse else j
    row = (t // 64).astype(np.float32)
    col = (t % 64).astype(np.float32)
    inv = (10000.0 ** (-(np.arange(0, 32, 2, dtype=np.float32) / 32))).astype(np.float32)
    cos = np.zeros((128, 1536), np.float32)
    sin = np.zeros((128, 1536), np.float32)
    for p in range(128):
        d = p % 64
        pos = row if d < 32 else col
        ang = (pos * inv[d % 16]).astype(np.float32)
        cos[p] = np.cos(ang)
        sin[p] = np.sin(ang)
    return cos, sin


def _perm_matrix():
    pm = np.zeros((128, 128), np.float32)
    for jx in range(128):
        if jx % 32 < 16:
            pm[jx + 16, jx] = -1.0
        else:
            pm[jx - 16, jx] = 1.0
    return pm


def _masks():
    k = np.arange(128)[:, None]
    q = np.arange(128)[None, :]
    m = np.zeros((128, 2, 4, 128), np.float32)
    m[:, 0] = (k >= q)[:, None, :]
    m[:, 1] = (k <= q)[:, None, :]
    return m


def build_program():
    nc = bass.Bass("TRN2", target_bir_lowering=False)

    def din(name, shape, dt=F32):
        return nc.dram_tensor(name, list(shape), dt, kind="ExternalInput").ap()

    def dout(name, shape, dt=F32):
        return nc.dram_tensor(name, list(shape), dt, kind="ExternalOutput").ap()

    xp_d = din("xp", [512, D])
    xs_d = din("xs", [1536, D])
    ck_d = din("ck", [DEPTH, 256, 128])
    cv_d = din("cv", [DEPTH, 256, 128])
    wada_d = din("w_ada", [DEPTH, D, 3 * D])
    win_d = din("w_in", [DEPTH, D, 2304])
    wout_d = din("w_out", [DEPTH, D, D])
    wpool_d = din("w_pool", [DEPTH, 4, 128, 128])
    vecs_d = din("vecs", [256, 128])
    sink_d = din("sinkv", [1, 32])
    fn_d = din("fnorm", [1, D])
    identf_d = din("ident_f", [128, 128])
    identb_d = din("ident_b", [128, 128], BF16)
    perm_d = din("perm", [128, 128], BF16)
    cos_d = din("cosT", [128, 1536], BF16)
    sin_d = din("sinT", [128, 1536], BF16)
    mask_d = din("mask2", [128, 2, 128], BF16)
    pmd_d = din("pm_diag", [128, 4, 4, 128], BF16)
    pme_d = din("pm_edge", [128, 4, 4, 16], BF16)

    yp_d = dout("yp", [512, D])
    ys_d = dout("ys", [1024, D])
    nk_d = dout("nk", [DEPTH, 512, 128])
    nv_d = dout("nv", [DEPTH, 512, 128])
    if DEBUG_DUMP:
        dbg_d = dout("dbg", [DEPTH, 128, 8, NTOK])

    es = ExitStack()

    def sb(name, shape, dt=F32):
        return es.enter_context(nc.sbuf_tensor(name, list(shape), dt))

    psum = es.enter_context(nc.psum_tensor("psum", [128, 4096], F32))

    def bank(b, n=512, off=0):
        return psum[:, b * 512 + off: b * 512 + off + n]

    xT = sb("xT", [128, 8, NTOK])
    hT = sb("hT", [128, 8, NTOK], BF16)
    qT = sb("qT", [128, 4, NTOK], BF16)
    zT = sb("zT", [128, 4, NTOK], BF16)
    kT = sb("kT", [128, KT_COLS], BF16)
    NVA = 64 + 128 * 36
    vaug = sb("vaug", [128, NVA], BF16)
    ring = sb("ring", [128, NRING, 8, 512], BF16)
    AR = sb("arena", [128, 8192], BF16)
    zf = AR[:, 0:4096].bitcast(F32).rearrange("p (g t) -> p g t", g=4)
    stage = AR[:, 0:4096].bitcast(F32).rearrange("p (s t) -> p s t", s=2)
    pooledT = AR[:, 4096:6144].rearrange("p (g t) -> p g t", g=4)
    utm = AR[:, 6144:8192].rearrange("p (s t) -> p s t", s=4)
    PT = AR[:, 0:3072].rearrange("p (s t) -> p s t", s=3)
    atf = AR[:, 3072:5120].bitcast(F32).rearrange("p (a c t) -> p a c t", a=2, c=4)
    lnt = AR[:, 5120:7168].bitcast(F32).rearrange("p (a t) -> p a t", a=2)
    sqa = AR[:, 7168:7680].rearrange("p (c t) -> p c t", c=4)
    rb2 = AR[:, 7680:7936].bitcast(F32)
    vec_in = AR[:, 7168:7680].bitcast(F32).rearrange("p (s t) -> p s t", s=2)
    sq = sb("sq", [128, 4, 512], BF16)
    rb = sb("rb", [128, 512])
    tmpf = sb("tmpf", [128, 3, 512])
    q16 = sb("q16", [128, 2, 512], BF16)
    kvf = sb("kvf", [128, 256])
    ckt = sb("ckt", [128, 2, 128], BF16)
    cosT = sb("cosT_sb", [128, 1536], BF16)
    sinT = sb("sinT_sb", [128, 1536], BF16)
    mask2 = sb("mask2_sb", [128, 2, 128], BF16)
    pmd = sb("pmd_sb", [128, 4, 4, 128], BF16)
    pme = sb("pme_sb", [128, 4, 4, 16], BF16)
    wp = sb("wp_sb", [128, 2, 4, 128], BF16)
    identf = sb("identf_sb", [128, 128])
    identb = sb("identb_sb", [128, 128], BF16)
    onesb = sb("onesb", [128, 128], BF16)
    perm = sb("perm_sb", [128, 128], BF16)
    vec = sb("vec", [128, 256])
    esink = sb("esink", [128, 32])
    sT = sb("sT", [128, 8, 2], BF16)
    mods = sb("mods", [128, 2, 24, 2])
    gmul = sb("gmul", [128, 2, 8, 2])
    ssq = sb("ssq", [128, 4])
    selr = sb("selr", [1, 128], BF16)
    esb = sb("esb", [128, 2, 512], BF16)
    fnb = hT[:, 0, :].bitcast(F32)

    P = Prog(nc)
    cnt = {"dma": 0, "ring": 0, "ps_norm": 0, "evac": 0, "rope": 0}

    def MM(out, lhsT, rhs, start, stop, reads, writes, accum=False):
        return P.op("pe", lambda e: e.matmul(out, lhsT=lhsT, rhs=rhs, start=start, stop=stop),
                    reads=reads, writes=writes, accum=accum)

    def TR(out, in_, ident, reads, writes):
        return P.op("pe", lambda e: e.transpose(out=out, in_=in_, identity=ident), reads=reads, writes=writes)

    def ACT(out, in_, func, reads, writes, bias=None, scale=None, accum_out=None):
        kw = {}
        if bias is not None:
            kw["bias"] = bias
        if scale is not None:
            kw["scale"] = scale
        if accum_out is not None:
            kw["accum_out"] = accum_out
        return P.op("act", lambda e: e.activation(out=out, in_=in_, func=func, **kw), reads=reads, writes=writes)

    def ACOPY(out, in_, reads, writes):
        return P.op("act", lambda e: e.copy(out=out, in_=in_), reads=reads, writes=writes)

    def VCOPY(out, in_, reads, writes):
        return P.op("dve", lambda e: e.tensor_copy(out=out, in_=in_), reads=reads, writes=writes)

    def TT(out, in0, in1, op, reads, writes):
        return P.op("dve", lambda e: e.tensor_tensor(out=out, in0=in0, in1=in1, op=op), reads=reads, writes=writes)

    def STT(out, in0, scalar, in1, op0, op1, reads, writes):
        return P.op("dve", lambda e: e.scalar_tensor_tensor(out=out, in0=in0, scalar=scalar, in1=in1, op0=op0, op1=op1),
                    reads=reads, writes=writes)

    def TS(out, in0, scalar1, op0, reads, writes):
        return P.op("dve", lambda e: e.tensor_scalar(out=out, in0=in0, scalar1=scalar1, scalar2=None, op0=op0),
                    reads=reads, writes=writes)

    def DMA(eng, out, in_, reads, writes, sem):
        return P.op(eng, lambda e: e.dma_start(out=out, in_=in_), reads=reads, writes=writes, dma_sem=sem)

    def newsem(prefix):
        cnt["dma"] += 1
        return "%s%d" % (prefix, cnt["dma"])

    def c_bada(l):
        return vec[:, l * 24:(l + 1) * 24]

    def c_normw(l):
        return vec[:, 96 + l * 8: 96 + l * 8 + 8]

    def c_an(l, j):
        return vec[:, 128 + l * 4 + j: 128 + l * 4 + j + 1]

    def c_pn(l, j):
        return vec[:, 144 + l * 4 + j: 144 + l * 4 + j + 1]

    def c_psc(l, j):
        return vec[:, 160 + l * 4 + j: 160 + l * 4 + j + 1]

    def kx(cs, bs):
        return [("x", c, b) for c in cs for b in bs]

    def kh(cs, bs):
        return [("h", c, b) for c in cs for b in bs]

    def kq(cs, bs):
        return [("q", c, b) for c in cs for b in bs]

    def kz(cs, bs):
        return [("z", c, b) for c in cs for b in bs]

    def kk(bs):
        return [("k", b) for b in bs]

    def kv(bs):
        return [("v", b) for b in bs]

    def tok(b0, b1):
        return slice(b0 * 128, b1 * 128)

    DMA("sp", identf[:], identf_d, [], ["identf"], newsem("c"))
    DMA("sp", identb[:], identb_d, [], ["identb"], newsem("c"))
    DMA("sp", perm[:], perm_d, [], ["perm"], newsem("c"))
    DMA("sp", vec_in[:, 0, :], vecs_d[0:128, :], [], ["vec_in0"], newsem("c"))
    DMA("sp", vec_in[:, 1, :], vecs_d[128:256, :], [], ["vec_in1"], newsem("c"))
    DMA("sp", esink[:], sink_d.broadcast_to([128, 32]), [], ["esink"], newsem("c"))
    DMA("sp", cosT[:], cos_d, [], ["cos"], newsem("c"))
    DMA("sp", sinT[:], sin_d, [], ["sin"], newsem("c"))
    DMA("sp", mask2[:], mask_d, [], ["mask2"], newsem("c"))
    DMA("sp", pmd[:], pmd_d, [], ["pmd"], newsem("c"))
    DMA("sp", pme[:], pme_d, [], ["pme"], newsem("c"))
    P.op("dve", lambda e: e.memset(onesb[:], 1.0), writes=["onesb"])
    P.op("dve", lambda e: e.memset(selr[:, 0:64], 0.0), writes=["selr"])
    P.op("dve", lambda e: e.memset(selr[:, 64:128], 1.0), writes=["selr"])
    P.op("dve", lambda e: e.memset(vaug[:], 1.0), writes=kv(range(18)))

    for i in range(2):
        TR(bank(7, 128, i * 128), vec_in[:, i, :], identf[:], ["vec_in%d" % i, "identf"], [("ps", 7)])
    ACOPY(vec[:], bank(7, 256), [("ps", 7)], ["vec"])
    ACT(sT[:, :, 0], vec[:, 176:184], AF.Silu, ["vec"], ["sT"])
    ACT(sT[:, :, 1], vec[:, 184:192], AF.Silu, ["vec"], ["sT"])
    ACT(esink[:], esink[:], AF.Exp, ["esink"], ["esink"])

    def ring_load(src_ap, ncols, slot):
        src = src_ap.rearrange("(k p) c -> p k c", p=128)
        DMA("pool", ring[:, slot, :, 0:ncols], src, [], [("ring", slot)], "ring%d" % slot)
        return slot

    MODB = 3

    def emit_ada_granule(l, gi, slot):
        ring_load(wada_d[l][:, gi * 512:(gi + 1) * 512], 512, slot)
        pm = l % 2
        for jc in range(4):
            for k in range(8):
                MM(bank(7, 2, jc * 2), ring[:, slot, k, jc * 128:(jc + 1) * 128], sT[:, k, :], k == 0, k == 7,
                   [("ring", slot), "sT"], [("ps", 7)], accum=(k > 0))
        pv = bank(7, 8).rearrange("p (j v) -> p j v", v=2)
        for v in range(2):
            TT(mods[:, pm, gi * 4:(gi + 1) * 4, v], pv[:, :, v], vec[:, l * 24 + gi * 4: l * 24 + gi * 4 + 4], ALU.add,
               [("ps", 7), "vec"], [("mods", pm)])

    def emit_mods_finish(l):
        pm = l % 2
        for v in range(2):
            STT(gmul[:, pm, :, v], mods[:, pm, 8:16, v], 1.0, c_normw(l), ALU.add, ALU.mult,
                [("mods", pm), "vec"], [("gmul", pm)])

    if STOP >= 1:
        for gi in range(6):
            emit_ada_granule(0, gi, gi % 2)
        emit_mods_finish(0)

    for b in (range(NBLK) if STOP >= 2 else []):
        st = b % 2
        src = xp_d[b * 128:(b + 1) * 128, :] if b < 4 else xs_d[(b - 4) * 128:(b - 3) * 128, :]
        DMA("sp", stage[:, st, :], src, [], [("stage", st)], "xin%d" % st)
        pb = (b % 2) * 2
        for c in range(8):
            TR(bank(pb + c // 4, 128, (c % 4) * 128), stage[:, st, c * 128:(c + 1) * 128], identf[:],
               [("stage", st), "identf"], [("ps", pb + c // 4)])
        src_ps = psum[:, pb * 512: pb * 512 + 1024].rearrange("p (c t) -> p c t", c=8)
        if b % 2 == 0:
            ACOPY(xT[:, :, tok(b, b + 1)], src_ps, [("ps", pb), ("ps", pb + 1)], kx(range(8), [b]))
        else:
            VCOPY(xT[:, :, tok(b, b + 1)], src_ps, [("ps", pb), ("ps", pb + 1)], kx(range(8), [b]))

    def units_of(nsamp):
        us = [(0, 4)]
        b = 4
        while b < 4 + nsamp:
            us.append((b, min(b + 4, 4 + nsamp)))
            b += 4
        return us

    def rsqrt_from_stats(ps_ap, dst, scale_div, rd_keys, key):
        ACT(dst, ps_ap, AF.Ln, rd_keys, [key], bias=EPS, scale=1.0 / scale_div)
        ACT(dst, dst, AF.Exp, [key], [key], scale=-0.5)

    def rope_evac(ps_b, n, lt0, dst_ap, dst_keys):
        i = cnt["rope"] % 2
        rbk = 4 + (cnt["rope"] % 4)
        cnt["rope"] += 1
        ACOPY(q16[:, i, 0:n], bank(ps_b, n), [("ps", ps_b)], [("q16", i)])
        MM(bank(rbk, n), perm[:], q16[:, i, 0:n], True, True, [("q16", i), "perm"], [("ps", rbk)])
        TT(tmpf[:, i, 0:n], bank(ps_b, n), cosT[:, lt0:lt0 + n], ALU.mult, [("ps", ps_b), "cos"], [("tmpf", i)])
        TT(tmpf[:, 2, 0:n], bank(rbk, n), sinT[:, lt0:lt0 + n], ALU.mult, [("ps", rbk), "sin"], [("tmpf", 2)])
        TT(dst_ap, tmpf[:, i, 0:n], tmpf[:, 2, 0:n], ALU.add, [("tmpf", i), ("tmpf", 2)], dst_keys)

    def f_slab(slot, col0, units, kchunks_fn, banks, evac_fn):
        for ui, (b0, b1) in enumerate(units):
            n = (b1 - b0) * 128
            pb = banks[ui]
            for k in range(8):
                rhs_ap, rkeys = kchunks_fn(k, b0, b1)
                MM(bank(pb, n), ring[:, slot, k, col0:col0 + 128], rhs_ap, k == 0, k == 7,
                   [("ring", slot)] + rkeys, [("ps", pb)], accum=(k > 0))
        for ui, (b0, b1) in enumerate(units):
            evac_fn(ui, b0, b1, banks[ui])

    def h_chunk(k, b0, b1):
        return hT[:, k, tok(b0, b1)], kh([k], range(b0, b1))

    def a_chunk(k, b0, b1):
        if k < 4:
            return qT[:, k, tok(b0, b1)], kq([k], range(b0, b1))
        return zT[:, k - 4, tok(b0, b1)], kz([k - 4], range(b0, b1))

    def vsel(b0):
        return 0 if b0 < 4 else 1

    out_ops = []

    def emit_layer(l):
        pm = l % 2
        ni = 12 - l
        nq = 11 - l
        units_in = units_of(ni)
        units_q = units_of(nq)
        blocks_in = list(range(4)) + list(range(4, 4 + ni))
        blocks_q = list(range(4)) + list(range(4, 4 + nq))

        DMA("pool", wp[:, pm], wpool_d[l].rearrange("g c d -> c g d"), [], [("wp", pm)], "wp%d" % pm)

        stage_gate(3 + 10 * l)
        def norm_A(b0, b1, pb):
            n = (b1 - b0) * 128
            for hf in range(2):
                ACT(sq[:, :, 0:n], xT[:, hf * 4:hf * 4 + 4, tok(b0, b1)], AF.Square,
                    kx(range(hf * 4, hf * 4 + 4), range(b0, b1)), ["sq"])
                for c4 in range(4):
                    c = hf * 4 + c4
                    MM(bank(pb, n), onesb[:], sq[:, c4, 0:n], c == 0, c == 7, ["sq", "onesb"], [("ps", pb)], accum=(c > 0))

        def norm_B(b0, b1, pb):
            n = (b1 - b0) * 128
            v = vsel(b0)
            rsqrt_from_stats(bank(pb, n), rb[:, 0:n], float(D), [("ps", pb)], "rb")
            for c in range(8):
                i = c % 2
                STT(tmpf[:, i, 0:n], xT[:, c, tok(b0, b1)], gmul[:, pm, c, v:v + 1], rb[:, 0:n], ALU.mult, ALU.mult,
                    kx([c], range(b0, b1)) + [("gmul", pm), "rb"], [("tmpf", i)])
                if c % 2 == 0:
                    ACT(hT[:, c, tok(b0, b1)], tmpf[:, i, 0:n], AF.Identity, [("tmpf", i), ("mods", pm)],
                        kh([c], range(b0, b1)), bias=mods[:, pm, c, v:v + 1], scale=1.0)
                else:
                    TS(hT[:, c, tok(b0, b1)], tmpf[:, i, 0:n], mods[:, pm, c, v:v + 1], ALU.add,
                       [("tmpf", i), ("mods", pm)], kh([c], range(b0, b1)))

        norm_sched = {}
        nU = len(units_in)

        def nA(u):
            norm_A(units_in[u][0], units_in[u][1], 6 + u % 2)

        def nB(u):
            norm_B(units_in[u][0], units_in[u][1], 6 + u % 2)

        nA(0)
        if nU > 1:
            nA(1)
        nB(0)
        for u in range(nU):
            lst = []
            if u + 2 < nU:
                lst.append(lambda u=u: nA(u + 2))
            if u + 1 < nU:
                lst.append(lambda u=u: nB(u + 1))
            norm_sched[units_in[u][0]] = lst

        stage_gate(4 + 10 * l)
        slot_u = ring_load(win_d[l][:, 1280:1792], 512, 0)
        targets = set(blocks_q)
        pooled_in_unit = {}

        def unit_index_q(tb):
            for ui, (b0, b1) in enumerate(units_q):
                if b0 <= tb < b1:
                    return ui
            return None

        pend = []

        def pool_stage2a(ui):
            b0, b1 = units_q[ui]
            n = (b1 - b0) * 128
            pk = [("pooledT", i) for i in range(b1 - b0)]
            for g in range(4):
                bk = 4 + g % 2
                MM(bank(bk, n), wp[:, pm, g, :], pooledT[:, g, 0:n], True, True, [("wp", pm)] + pk, [("ps", bk)])
                TS(zf[:, g, 0:n], bank(bk, n), c_psc(l, g), ALU.mult, [("ps", bk), "vec"], [("zf", g)])
            pend.append([1, lambda: pool_stage2b(ui)])

        def pool_stage2b(ui):
            b0, b1 = units_q[ui]
            n = (b1 - b0) * 128
            TT(sq[:, :, 0:n], zf[:, :, 0:n], zf[:, :, 0:n], ALU.mult, [("zf", g) for g in range(4)], ["sq"])
            for g in range(4):
                MM(bank(4, n), onesb[:], sq[:, g, 0:n], g == 0, g == 3, ["sq", "onesb"], [("ps", 4)], accum=(g > 0))
            rsqrt_from_stats(bank(4, n), rb[:, 0:n], 512.0, [("ps", 4)], "rb")
            TT(zT[:, :, tok(b0, b1)], zf[:, :, 0:n], rb[:, 0:n].unsqueeze(1).broadcast_to([128, 4, n]), ALU.mult,
               [("zf", g) for g in range(4)] + ["rb"], kz(range(4), range(b0, b1)))

        pcount = {"i": 0}

        def pool_target(tb):
            ui = unit_index_q(tb)
            b0, b1 = units_q[ui]
            if tb < 4:
                dtype_i = 0 if tb % 2 == 0 else 1
                prev_b = tb - 1 if tb % 2 == 1 else None
                next_b = tb + 1 if tb % 2 == 0 else None
                et_prev, et_next = 0, 1
            else:
                dtype_i = 2 if tb == 4 else 3
                prev_b = tb - 1 if tb > 4 else None
                next_b = tb + 1
                et_prev, et_next = 2, 3
            PB = 2 + pcount["i"] % 2
            pcount["i"] += 1
            for g in range(4):
                gs = slice(g * 128, (g + 1) * 128)
                MM(bank(PB, 128, g * 128), utm[:, tb % 4, gs], pmd[:, dtype_i, g, :], True,
                   (prev_b is None and next_b is None), [("utm", tb % 4), "pmd"], [("ps", PB)], accum=(g > 0))
                if prev_b is not None:
                    MM(bank(PB, 16, g * 128), utm[:, prev_b % 4, gs], pme[:, et_prev, g, :], False, next_b is None,
                       [("utm", prev_b % 4), "pme"], [("ps", PB)], accum=True)
                if next_b is not None:
                    MM(bank(PB, 16, g * 128 + 112), utm[:, next_b % 4, gs], pme[:, et_next, g, :], False, True,
                       [("utm", next_b % 4), "pme"], [("ps", PB)], accum=True)
            off = (tb - b0) * 128
            VCOPY(pooledT[:, :, off:off + 128], bank(PB).rearrange("p (g t) -> p g t", g=4), [("ps", PB)],
                  [("pooledT", tb - b0)])
            pooled_in_unit[ui] = pooled_in_unit.get(ui, 0) + 1
            if pooled_in_unit[ui] == b1 - b0:
                pend.append([1, lambda: pool_stage2a(ui)])

        def run_pending(flush=False):
            while True:
                due = [p for p in pend if p[0] <= 0 or flush]
                if not due:
                    break
                p = due[0]
                pend.remove(p)
                p[1]()
            for p in pend:
                p[0] -= 1

        for bi, b in enumerate(blocks_in):
            pb = bi % 2
            for k in range(8):
                MM(bank(pb), hT[:, k, tok(b, b + 1)], ring[:, slot_u, k, :], k == 0, k == 7,
                   [("ring", slot_u)] + kh([k], [b]), [("ps", pb)], accum=(k > 0))
            ACOPY(utm[:, b % 4, :], bank(pb), [("ps", pb)], [("utm", b % 4)])
            for fn in norm_sched.get(b, []):
                fn()
            run_pending()
            if b < 4:
                if b % 2 == 1:
                    pend.append([0, lambda b=b: (pool_target(b - 1), pool_target(b))])
            elif b - 1 >= 4 and (b - 1) in targets:
                pend.append([0, lambda b=b: pool_target(b - 1)])
        stage_gate(5.1 + 10 * l)
        slot_kv = ring_load(win_d[l][:, 512:768], 256, 1)
        DMA("pool", ckt[:], ck_d[l].rearrange("(b p) f -> p b f", p=128), [], ["ckt"], "ckt")
        for cb in range(2):
            e0 = 2 * (16 + cb)
            dstv = vaug[:, 64 + 128 * e0: 64 + 128 * e0 + 256].rearrange("p (g x) -> p g x", g=2)[:, :, 0:64]
            srcv = cv_d[l][cb * 128:(cb + 1) * 128, :].rearrange("p (g d) -> p g d", g=2)
            DMA("pool", dstv, srcv, [], kv([16 + cb]), "cv%d" % cb)

        def kv_block(bi, b):
            pb = bi % 2
            if b < 4:
                for k in range(8):
                    MM(bank(pb, 256), hT[:, k, tok(b, b + 1)], ring[:, slot_kv, k, 0:256], k == 0, k == 7,
                       [("ring", slot_kv)] + kh([k], [b]), [("ps", pb)], accum=(k > 0))
                if b % 2 == 0:
                    kbuf, kkey = kvf[:], "kvf"
                else:
                    kbuf, kkey = tmpf[:, 2, 0:256], ("tmpf", 2)
                ACOPY(kbuf, bank(pb, 256), [("ps", pb)], [kkey])
                out_ops.append(DMA("sp", nk_d[l][b * 128:(b + 1) * 128, :], kbuf[:, 0:128], [kkey], [], "okv%d" % (b % 2)))
                out_ops.append(DMA("sp", nv_d[l][b * 128:(b + 1) * 128, :], kbuf[:, 128:256], [kkey], [], "okv%d" % (b % 2)))
                vsrc = bank(pb, 128, 128).rearrange("p (g x) -> p g x", g=2)
            else:
                for k in range(8):
                    MM(bank(pb, 128), hT[:, k, tok(b, b + 1)], ring[:, slot_kv, k, 128:256], k == 0, k == 7,
                       [("ring", slot_kv)] + kh([k], [b]), [("ps", pb)], accum=(k > 0))
                vsrc = bank(pb, 128).rearrange("p (g x) -> p g x", g=2)
            e0 = 2 * b
            dstv = vaug[:, 64 + 128 * e0: 64 + 128 * e0 + 256].rearrange("p (g x) -> p g x", g=2)[:, :, 0:64]
            VCOPY(dstv, vsrc, [("ps", pb)], kv([b]))

        for bi, b in enumerate(blocks_in[:4]):
            kv_block(bi, b)
        run_pending(flush=True)
        ctb = bank(7, 128).bitcast(BF16)
        for cb in range(2):
            TR(ctb[:, cb * 128:(cb + 1) * 128], ckt[:, cb, :], identb[:], ["ckt", "identb"], [("ps", 7)])
        ACOPY(kT[:, NTOK:NTOK + 256], ctb, [("ps", 7)], kk([16, 17]))
        for bi, b in enumerate(blocks_in):
            if bi >= 4:
                kv_block(bi, b)

        stage_gate(5.4 + 10 * l)

        def evac_k(ui, b0, b1, pb):
            n = (b1 - b0) * 128
            if b0 < 4:
                ACOPY(kT[:, tok(b0, b1)], bank(pb, n), [("ps", pb)], kk(range(b0, b1)))
            else:
                rope_evac(pb, n, (b0 - 4) * 128, kT[:, tok(b0, b1)], kk(range(b0, b1)))
        f_slab(slot_kv, 0, units_in, h_chunk, [0, 1, 2, 3], evac_k)

        stage_gate(5 + 10 * l)
        slot_gp = ring_load(win_d[l][:, 1792:2304], 512, 0)
        for j in range(4):
            banks = [0, 1, 2, 3] if j % 2 == 0 else [4, 5, 6, 7]

            def evac_gp(ui, b0, b1, pb, j=j):
                n = (b1 - b0) * 128
                i = cnt["evac"] % 2
                cnt["evac"] += 1
                ACT(q16[:, i, 0:n], bank(pb, n), AF.Silu, [("ps", pb)], [("q16", i)])
                STT(zT[:, j, tok(b0, b1)], zT[:, j, tok(b0, b1)], c_pn(l, j), q16[:, i, 0:n], ALU.mult, ALU.mult,
                    kz([j], range(b0, b1)) + [("q16", i), "vec"], kz([j], range(b0, b1)))
            f_slab(slot_gp, j * 128, units_q, h_chunk, banks, evac_gp)

        stage_gate(7 + 10 * l)
        slot_q = ring_load(win_d[l][:, 0:512], 512, 1)
        for c in range(4):
            def evac_q(ui, b0, b1, pb, c=c):
                n = (b1 - b0) * 128
                if b0 < 4:
                    ACOPY(qT[:, c, tok(b0, b1)], bank(pb, n), [("ps", pb)], kq([c], range(b0, b1)))
                else:
                    rope_evac(pb, n, (b0 - 4) * 128, qT[:, c, tok(b0, b1)], kq([c], range(b0, b1)))
            f_slab(slot_q, c * 128, units_q, h_chunk, [0, 1, 2, 3], evac_q)
            if c == 1 and l + 1 < DEPTH:
                emit_ada_granule(l + 1, 0, 0)

        stage_gate(8 + 10 * l)
        VCOPY(esb[:].rearrange("p g (h q) -> p (g h) q", h=4), esink[:, l * 8:(l + 1) * 8].unsqueeze(2).broadcast_to([128, 8, 128]),
              ["esink"], ["esb"])
        ada_pending = [1] if l + 1 < DEPTH else []
        ptasks = []
        for qi, qb in enumerate(blocks_q):
            if qb < 4:
                s0 = (qb // 2) * 2
                keys = [(s0, None), (s0 + 1, None)]
            else:
                keys = []
                if qb > 4:
                    keys.append((qb - 1, 0))
                keys.append((qb, None))
                keys.append((qb + 1, 1))
                keys += [(16, None), (17, None)]
            ptasks.append(dict(qi=qi, qb=qb, keys=keys, ob=((4, 5) if qi % 2 == 0 else (6, 7)), ab=qi % 2))
        psteps = [(ti, ki) for ti, t in enumerate(ptasks) for ki in range(len(t["keys"]))]
        last_ps = {}
        for si_, (ti_, ki_) in enumerate(psteps):
            last_ps[ti_] = si_
        deferred = []

        def emit_qk(si):
            ti, ki = psteps[si]
            t = ptasks[ti]
            qb = t["qb"]
            kb, mtype = t["keys"][ki]
            sp = si % 2
            pt = si % 3
            for g in range(2):
                lo, hi = 64 * g, 64 * g + 64
                MM(bank(2 * sp + g), kT[lo:hi, kb * 128:(kb + 1) * 128], qT[lo:hi, :, tok(qb, qb + 1)], True, True,
                   kk([kb]) + kq(range(4), [qb]), [("ps", 2 * sp + g)])
            ACT(PT[:, pt, :], psum[:, 2 * sp * 512:(2 * sp + 2) * 512], AF.Exp, [("ps", 2 * sp), ("ps", 2 * sp + 1)],
                [("PT", pt)], scale=0.125)
            if mtype is not None:
                ptv = PT[:, pt, :].rearrange("p (h q) -> p h q", h=8)
                TT(ptv, ptv, mask2[:, mtype, :].unsqueeze(1).broadcast_to([128, 8, 128]), ALU.mult,
                   [("PT", pt), "mask2"], [("PT", pt)])

        def emit_unit_rmsnorm(ub0, ub1):
            n = (ub1 - ub0) * 128
            uk = kq(range(4), range(ub0, ub1))
            TT(sq[:, :, 0:n], qT[:, :, tok(ub0, ub1)], qT[:, :, tok(ub0, ub1)], ALU.mult, uk, ["sq"])
            for c in range(4):
                MM(bank(3, n), onesb[:], sq[:, c, 0:n], c == 0, c == 3, ["sq", "onesb"], [("ps", 3)], accum=(c > 0))
            rsqrt_from_stats(bank(3, n), rb[:, 0:n], 512.0, [("ps", 3)], "rb")
            TT(qT[:, :, tok(ub0, ub1)], qT[:, :, tok(ub0, ub1)], rb[:, 0:n].unsqueeze(1).broadcast_to([128, 4, n]), ALU.mult,
               uk + ["rb"], uk)

        def emit_pv(si):
            ti, ki = psteps[si]
            t = ptasks[ti]
            qb, ab = t["qb"], t["ab"]
            kb, mtype = t["keys"][ki]
            pt = si % 3
            first, last = (ki == 0), (ki == len(t["keys"]) - 1)
            for g in range(2):
                ob = t["ob"][g]
                en = 2 * kb + g
                MM(bank(ob), vaug[:, 64 + 128 * en: 64 + 128 * en + 128], PT[:, pt, g * 512:(g + 1) * 512], first, last,
                   kv([kb]) + [("PT", pt)], [("ps", ob)], accum=(not first))
            if not last:
                return
            for g in range(2):
                ob = t["ob"][g]
                TT(lnt[64:128, g, :], bank(ob)[64:128, :], esb[64:128, g, :], ALU.add, [("ps", ob), "esb"], [("lnt", g)])

            def finish2(t=t, qb=qb):
                ACT(lnt[64:128, :, :], lnt[64:128, :, :], AF.Ln, [("lnt", 0), ("lnt", 1)], [("lnt", 0), ("lnt", 1)])
                ACT(lnt[64:128, :, :], lnt[64:128, :, :], AF.Exp, [("lnt", 0), ("lnt", 1)], [("lnt", 0), ("lnt", 1)], scale=-1.0)
                for g in range(2):
                    ob = t["ob"][g]
                    TT(qT[0:64, 2 * g:2 * g + 2, tok(qb, qb + 1)], bank(ob, 256)[0:64, :].rearrange("p (j t) -> p j t", j=2),
                       lnt[64:128, g, 0:256].rearrange("p (j t) -> p j t", j=2), ALU.mult,
                       [("ps", ob), ("lnt", g)], kq([2 * g, 2 * g + 1], [qb]))
                    TT(qT[64:128, 2 * g:2 * g + 2, tok(qb, qb + 1)], bank(ob, 256, 256)[0:64, :].rearrange("p (j t) -> p j t", j=2),
                       lnt[64:128, g, 256:512].rearrange("p (j t) -> p j t", j=2), ALU.mult,
                       [("ps", ob), ("lnt", g)], kq([2 * g, 2 * g + 1], [qb]))
            deferred.append((si + 1, finish2))
            for (ub0, ub1) in units_q:
                if qb == ub1 - 1:
                    deferred.append((si + 3, lambda ub0=ub0, ub1=ub1: emit_unit_rmsnorm(ub0, ub1)))
            if ada_pending and t["qi"] >= 3:
                gi = ada_pending.pop(0)
                deferred.append((si + 2, lambda gi=gi: emit_ada_granule(l + 1, gi, 1)))

        nst = len(psteps)
        PIPE = 2
        for si in range(nst + PIPE):
            if si < nst:
                emit_qk(si)
            if si >= PIPE:
                cur = si - PIPE
                emit_pv(cur)
                for d in [d for d in deferred if d[0] <= cur]:
                    deferred.remove(d)
                    d[1]()
        while ada_pending:
            emit_ada_granule(l + 1, ada_pending.pop(0), 1)

        stage_gate(9 + 10 * l)
        slot_ga = ring_load(win_d[l][:, 768:1280], 512, 0)

        def ga_mm(j, ulist, banks):
            for ui, (b0, b1) in ulist:
                n = (b1 - b0) * 128
                pb = banks[ui]
                for k in range(8):
                    MM(bank(pb, n), ring[:, slot_ga, k, j * 128:(j + 1) * 128], hT[:, k, tok(b0, b1)], k == 0, k == 7,
                       [("ring", slot_ga)] + kh([k], range(b0, b1)), [("ps", pb)], accum=(k > 0))

        def evac_ga(ui, b0, b1, pb, j):
            n = (b1 - b0) * 128
            i = cnt["evac"] % 2
            cnt["evac"] += 1
            ACT(q16[:, i, 0:n], bank(pb, n), AF.Silu, [("ps", pb)], [("q16", i)])
            STT(qT[:, j, tok(b0, b1)], qT[:, j, tok(b0, b1)], c_an(l, j), q16[:, i, 0:n], ALU.mult, ALU.mult,
                kq([j], range(b0, b1)) + [("q16", i), "vec"], kq([j], range(b0, b1)))

        ulist = list(enumerate(units_q))
        ga_mm(0, ulist[:3], [0, 1, 2, 3])
        for d in list(deferred):
            d[1]()
        deferred.clear()
        if l + 1 < DEPTH:
            emit_ada_granule(l + 1, 2, 1)
        ga_mm(0, ulist[3:], [0, 1, 2, 3])
        for ui, (b0, b1) in ulist:
            evac_ga(ui, b0, b1, [0, 1, 2, 3][ui], 0)
        for j in range(1, 4):
            banks = [0, 1, 2, 3] if j % 2 == 0 else [4, 5, 6, 7]
            f_slab(slot_ga, j * 128, units_q, h_chunk, banks, lambda ui, b0, b1, pb, j=j: evac_ga(ui, b0, b1, pb, j))

        stage_gate(10 + 10 * l)
        for half in range(2):
            slot_o = ring_load(wout_d[l][:, half * 512:(half + 1) * 512], 512, 1 - half)
            for cc in range(4):
                c = half * 4 + cc
                banks = [0, 1, 2, 3] if cc % 2 == 0 else [4, 5, 6, 7]
                if l + 1 < DEPTH and cc == 2:
                    emit_ada_granule(l + 1, 3 + half, half)

                def evac_o(ui, b0, b1, pb, c=c):
                    n = (b1 - b0) * 128
                    v = vsel(b0)
                    STT(xT[:, c, tok(b0, b1)], bank(pb, n), mods[:, pm, 16 + c, v:v + 1], xT[:, c, tok(b0, b1)],
                        ALU.mult, ALU.add, [("ps", pb), ("mods", pm)] + kx([c], range(b0, b1)), kx([c], range(b0, b1)))
                f_slab(slot_o, cc * 128, units_q, a_chunk, banks, evac_o)
        if l + 1 < DEPTH:
            emit_ada_granule(l + 1, 5, 1)
            emit_mods_finish(l + 1)

        if DEBUG_DUMP:
            out_ops.append(DMA("sp", dbg_d[l], xT[:], kx(range(8), range(NBLK)), [], "dbg"))

    def emit_epilogue():
        DMA("sp", fnb, fn_d.broadcast_to([128, D]), [], kh([0], range(NBLK)), "fnb")
        fnk = kh([0], range(NBLK))
        own = list(range(4)) + list(range(4, 12))
        for i, b in enumerate(own):
            pb = (i % 2) * 2
            st = i % 2
            for c in range(8):
                TR(bank(pb + c // 4, 128, (c % 4) * 128), xT[:, c, tok(b, b + 1)], identf[:], kx([c], [b]) + ["identf"],
                   [("ps", pb + c // 4)])
            for hb in range(2):
                ACT(stage[:, st, hb * 512:(hb + 1) * 512], bank(pb + hb), AF.Square, [("ps", pb + hb)],
                    [("stage", st), ("ssq", st, hb)], accum_out=ssq[:, 2 * st + hb: 2 * st + hb + 1])
            s0 = ssq[:, 2 * st:2 * st + 1]
            TT(s0, s0, ssq[:, 2 * st + 1:2 * st + 2], ALU.add, [("ssq", st, 0), ("ssq", st, 1)], [("ssq", st, 0)])
            ACT(s0, s0, AF.Ln, [("ssq", st, 0)], [("ssq", st, 0)], bias=EPS, scale=1.0 / D)
            ACT(s0, s0, AF.Exp, [("ssq", st, 0)], [("ssq", st, 0)], scale=-0.5)
            for hb in range(2):
                STT(stage[:, st, hb * 512:(hb + 1) * 512], bank(pb + hb), s0, fnb[:, hb * 512:(hb + 1) * 512], ALU.mult, ALU.mult,
                    [("ps", pb + hb), ("ssq", st, 0)] + fnk, [("stage", st)])
            dst = yp_d[b * 128:(b + 1) * 128, :] if b < 4 else ys_d[(b - 4) * 128:(b - 3) * 128, :]
            out_ops.append(DMA("sp", dst, stage[:, st, :], [("stage", st)], [], "yout%d" % st))


    try:
        for l in range(DEPTH):
            emit_layer(l)
        stage_gate(50)
        emit_epilogue()
    except StopBuild:
        pass

    P.emit(final_wait_ops=out_ops)
    es.close()
    return nc


_CACHE = {}


def _prep_shared(inp):
    qp, ap = _qperm(), _aperm()
    w_in = np.asarray(inp["w_in"], np.float32)
    colperm = np.concatenate([qp, np.arange(512, 768), 768 + ap, np.arange(1280, 2304)])
    w_in_p = np.ascontiguousarray(w_in[:, :, colperm])
    w_out = np.asarray(inp["w_out"], np.float32)
    rowperm = np.concatenate([ap, np.arange(512, 1024)])
    w_out_p = np.ascontiguousarray(w_out[:, rowperm, :])
    attn_norm_p = np.asarray(inp["attn_norm"], np.float32)[:, ap]
    return w_in_p, w_out_p, attn_norm_p


def kernel(x_prompt, x_sample, cache_k, cache_v, c, c_ctx, norm_w, w_ada, b_ada, w_in, sink,
           attn_norm, pool_norm, w_pool, pool_scale, w_out, final_norm):
    f32 = np.float32
    inp = dict(w_in=w_in, w_out=w_out, attn_norm=attn_norm)
    w_in_p, w_out_p, attn_norm_p = _prep_shared(inp)
    x_prompt = np.asarray(x_prompt, f32)
    x_sample = np.asarray(x_sample, f32)
    cache_k = np.asarray(cache_k, f32)
    cache_v = np.asarray(cache_v, f32)
    c = np.asarray(c, f32)
    c_ctx = np.asarray(c_ctx, f32)
    w_ada = np.ascontiguousarray(np.asarray(w_ada, f32))
    w_pool = np.ascontiguousarray(np.asarray(w_pool, f32))
    b_ada = np.asarray(b_ada, f32)
    norm_w = np.asarray(norm_w, f32)
    pool_norm = np.asarray(pool_norm, f32)
    pool_scale = np.asarray(pool_scale, f32)
    sink = np.asarray(sink, f32)
    final_norm = np.asarray(final_norm, f32)

    bf = ml_dtypes.bfloat16
    ident = np.eye(128, dtype=f32)
    consts = {}
    for rev in (False, True):
        cos, sin = _rope_tables(rev)
        pd, pe = _pool_tables(rev)
        consts[rev] = dict(cosT=cos.astype(bf), sinT=sin.astype(bf), pm_diag=pd.astype(bf), pm_edge=pe.astype(bf))
    mask2 = np.ascontiguousarray(_masks()[:, :, 0, :]).astype(bf)
    permm = _perm_matrix().astype(bf)

    in_maps = []
    for i in range(NCORES):
        b, half = i // 2, i % 2
        rev = half == 1
        if not rev:
            xs = x_sample[b, 0:1536]
        else:
            xs = x_sample[b, ::-1][0:1536]
        vecs = np.zeros((256, 128), f32)
        vecs[0:96] = b_ada.reshape(DEPTH * 24, 128)
        vecs[96:128] = norm_w.reshape(DEPTH * 8, 128)
        vecs[128:144] = attn_norm_p.reshape(DEPTH * 4, 128)
        vecs[144:160] = pool_norm.reshape(DEPTH * 4, 128)
        vecs[160:176] = pool_scale.reshape(DEPTH * 4, 128)
        vecs[176:184] = c_ctx.reshape(8, 128)
        vecs[184:192] = c[b].reshape(8, 128)
        m = dict(
            xp=np.ascontiguousarray(x_prompt[2 * i:2 * i + 2].reshape(512, D)),
            xs=np.ascontiguousarray(xs),
            ck=np.ascontiguousarray(cache_k[b].reshape(DEPTH, 256, 128)),
            cv=np.ascontiguousarray(cache_v[b].reshape(DEPTH, 256, 128)),
            w_ada=w_ada, w_in=w_in_p, w_out=w_out_p, w_pool=w_pool,
            vecs=vecs, sinkv=np.ascontiguousarray(sink.reshape(1, 32)),
            fnorm=np.ascontiguousarray(final_norm.reshape(1, D)),
            ident_f=ident, ident_b=ident.astype(bf), perm=permm, mask2=mask2,
            **consts[rev],
        )
        in_maps.append(m)

    if "nc" not in _CACHE:
        _CACHE["nc"] = build_program()
    nc = _CACHE["nc"]
    res = run_bass_kernel_spmd(nc, in_maps, core_ids=list(range(NCORES)))
    outs = res.results

    y_prompt = np.zeros((16, 256, D), f32)
    y_sample = np.zeros((4, 2048, D), f32)
    new_k = np.zeros((16, DEPTH, 256, 2, 64), f32)
    new_v = np.zeros((16, DEPTH, 256, 2, 64), f32)
    for i in range(NCORES):
        b, half = i // 2, i % 2
        r = outs[i]
        y_prompt[2 * i:2 * i + 2] = np.asarray(r["yp"]).reshape(2, 256, D)
        ys = np.asarray(r["ys"])
        if half == 0:
            y_sample[b, 0:1024] = ys
        else:
            y_sample[b, 1024:2048] = ys[::-1]
        nk = np.asarray(r["nk"]).reshape(DEPTH, 2, 256, 2, 64)
        nv = np.asarray(r["nv"]).reshape(DEPTH, 2, 256, 2, 64)
        new_k[2 * i:2 * i + 2] = nk.transpose(1, 0, 2, 3, 4)
        new_v[2 * i:2 * i + 2] = nv.transpose(1, 0, 2, 3, 4)
    if DEBUG_DUMP:
        kernel.dbg = [np.asarray(o["dbg"]) for o in outs]
    return (y_prompt, y_sample, new_k, new_v)
```

```python
from contextlib import ExitStack
import numpy as np
import ml_dtypes
import concourse.bass as bass
import concourse.mybir as mybir
from concourse.bass_utils import run_bass_kernel_spmd

F32 = mybir.dt.float32
BF16 = mybir.dt.bfloat16
AF = mybir.ActivationFunctionType
ALU = mybir.AluOpType

D = 1024
DEPTH = 4
NCORES = 8
EPS = 1e-6
NTOK = 2048
NBLK = 16
KT_COLS = NTOK + 256
NRING = 2
DEBUG_DUMP = False
STOP = 99


class StopBuild(Exception):
    pass


def stage_gate(n):
    if n > STOP:
        raise StopBuild()

ENGS = ("pe", "act", "dve", "pool", "sp")


class Op:
    __slots__ = ("eng", "fn", "deps", "signal", "tick", "dma_sem", "dma_val")

    def __init__(self, eng, fn):
        self.eng = eng
        self.fn = fn
        self.deps = []
        self.signal = False
        self.tick = None
        self.dma_sem = None
        self.dma_val = None


class Prog:
    def __init__(self, nc):
        self.nc = nc
        self.ops = {e: [] for e in ENGS}
        self.res = {}
        self.dma_tot = {}

    def op(self, eng, fn, reads=(), writes=(), accum=False, dma_sem=None):
        o = Op(eng, fn)
        deps = []
        for r in reads:
            st = self.res.get(r)
            if st is not None and st[0] is not None:
                deps.append(("raw", st[0]))
            if st is not None and isinstance(r, tuple) and r[0] == "ps":
                for rd in st[1]:
                    if rd.eng != eng:
                        deps.append(("raw", rd))
        for w in writes:
            st = self.res.get(w)
            if st is not None:
                if st[0] is not None and not accum:
                    deps.append(("waw", st[0]))
                for rd in st[1]:
                    deps.append(("war", rd))
        seen = set()
        for kind, d in deps:
            if d is o or id(d) in seen:
                continue
            if d.eng == eng and d.dma_sem is None and dma_sem is None:
                if eng == "pe":
                    continue
            seen.add(id(d))
            o.deps.append(d)
        for r in reads:
            self.res.setdefault(r, [None, []])[1].append(o)
        for w in writes:
            st = self.res.setdefault(w, [None, []])
            st[0] = o
            st[1] = []
        if dma_sem is not None:
            o.dma_sem = dma_sem
            self.dma_tot[dma_sem] = self.dma_tot.get(dma_sem, 0) + 16
            o.dma_val = self.dma_tot[dma_sem]
        self.ops[eng].append(o)
        return o

    def emit(self, final_wait_ops=()):
        nc = self.nc
        for e in ENGS:
            for o in self.ops[e]:
                for d in o.deps:
                    d.signal = True
        for o in final_wait_ops:
            o.signal = True
        for e in ENGS:
            t = 0
            for o in self.ops[e]:
                if o.dma_sem is None and o.signal:
                    t += 1
                    o.tick = t
        with ExitStack() as es:
            sems = {}
            for e in ENGS:
                sems[e] = es.enter_context(nc.semaphore("s_" + e))
            for k in self.dma_tot:
                sems[("dma", k)] = es.enter_context(nc.semaphore("d_" + str(k)))
            block = es.enter_context(nc.Block())

            def run(eng_name, e):
                waited = {}
                for o in self.ops[eng_name]:
                    need = {}
                    for d in o.deps:
                        if d.dma_sem is not None:
                            key, val = ("dma", d.dma_sem), d.dma_val
                        else:
                            key, val = d.eng, d.tick
                        if need.get(key, 0) < val:
                            need[key] = val
                    for key, val in need.items():
                        if waited.get(key, 0) < val:
                            e.wait_ge(sems[key], val)
                            waited[key] = val
                    inst = o.fn(e)
                    if o.dma_sem is not None:
                        inst.then_inc(sems[("dma", o.dma_sem)], 16)
                    elif o.signal:
                        inst.then_inc(sems[eng_name], 1)
                if eng_name == "sp":
                    need = {}
                    for o in final_wait_ops:
                        if o.dma_sem is not None:
                            key, val = ("dma", o.dma_sem), o.dma_val
                        else:
                            key, val = o.eng, o.tick
                        need[key] = max(need.get(key, 0), val)
                    for key, val in need.items():
                        e.wait_ge(sems[key], val)

            @block.tensor
            def _(e):
                run("pe", e)

            @block.scalar
            def _(e):
                run("act", e)

            @block.vector
            def _(e):
                run("dve", e)

            @block.gpsimd
            def _(e):
                run("pool", e)

            @block.sync
            def _(e):
                run("sp", e)


def _aperm():
    idx = []
    for ac in range(4):
        g, j = ac // 2, ac % 2
        for h in (4 * g + j, 4 * g + 2 + j):
            idx += [h * 64 + d for d in range(64)]
    return np.array(idx)


def _qperm():
    idx = []
    for c in range(4):
        for h in (c, 4 + c):
            idx += [h * 64 + d for d in range(64)]
    return np.array(idx)


def _pool_op(L, w, n):
    M = np.zeros((n, n + 16), np.float64)
    for t in range(n):
        lo = min(max(t - w // 2, 0), L)
        hi = min(max(t + w // 2, 0), L)
        for s in range(lo, hi):
            if s < n + 16:
                M[t, s] += 1.0 / (hi - lo)
        M[t, t] -= 1.0
    return M


def _pool_tables(reverse):
    pd = np.zeros((128, 4, 4, 128), np.float32)
    pe = np.zeros((128, 4, 4, 16), np.float32)
    for g, w in enumerate((2, 4, 8, 16)):
        M = np.zeros((256, 256))
        for t in range(256):
            lo = min(max(t - w // 2, 0), 256)
            hi = min(max(t + w // 2, 0), 256)
            M[t, lo:hi] += 1.0 / (hi - lo)
            M[t, t] -= 1.0
        pd[:, 0, g, :] = M[0:128, 0:128].T
        pd[:, 1, g, :] = M[128:256, 128:256].T
        pe[:, 0, g, :] = M[128:144, 0:128].T
        pe[:, 1, g, :] = M[112:128, 128:256].T
        L = 2048
        Mg = np.zeros((L, L))
        for t in range(L):
            lo = min(max(t - w // 2, 0), L)
            hi = min(max(t + w // 2, 0), L)
            Mg[t, lo:hi] += 1.0 / (hi - lo)
            Mg[t, t] -= 1.0
        Ml = Mg[::-1, ::-1] if reverse else Mg
        pd[:, 2, g, :] = Ml[0:128, 0:128].T
        pd[:, 3, g, :] = Ml[128:256, 128:256].T
        pe[:, 2, g, :] = Ml[128:144, 0:128].T
        pe[:, 3, g, :] = Ml[240:256, 256:384].T
    return pd, pe


def _rope_tables(reverse):
    j = np.arange(1536)
    t = (2047 - j) if reverse else j
    row = (t // 64).astype(np.float32)
    col = (t % 64).astype(np.float32)
    inv = (10000.0 ** (-(np.arange(0, 32, 2, dtype=np.float32) / 32))).astype(np.float32)
    cos = np.zeros((128, 1536), np.float32)
    sin = np.zeros((128, 1536), np.float32)
    for p in range(128):
        d = p % 64
        pos = row if d < 32 else col
        ang = (pos * inv[d % 16]).astype(np.float32)
        cos[p] = np.cos(ang)
        sin[p] = np.sin(ang)
    return cos, sin


def _perm_matrix():
    pm = np.zeros((128, 128), np.float32)
    for jx in range(128):
        if jx % 32 < 16:
            pm[jx + 16, jx] = -1.0
        else:
            pm[jx - 16, jx] = 1.0
    return pm


def _masks():
    k = np.arange(128)[:, None]
    q = np.arange(128)[None, :]
    m = np.zeros((128, 2, 4, 128), np.float32)
    m[:, 0] = (k >= q)[:, None, :]
    m[:, 1] = (k <= q)[:, None, :]
    return m


def build_program():
    nc = bass.Bass("TRN2", target_bir_lowering=False)

    def din(name, shape, dt=F32):
        return nc.dram_tensor(name, list(shape), dt, kind="ExternalInput").ap()

    def dout(name, shape, dt=F32):
        return nc.dram_tensor(name, list(shape), dt, kind="ExternalOutput").ap()

    xp_d = din("xp", [512, D])
    xs_d = din("xs", [1536, D])
    ck_d = din("ck", [DEPTH, 256, 128])
    cv_d = din("cv", [DEPTH, 256, 128])
    wada_d = din("w_ada", [DEPTH, D, 3 * D])
    win_d = din("w_in", [DEPTH, D, 2304])
    wout_d = din("w_out", [DEPTH, D, D])
    wpool_d = din("w_pool", [DEPTH, 4, 128, 128])
    vecs_d = din("vecs", [256, 128])
    sink_d = din("sinkv", [1, 32])
    fn_d = din("fnorm", [1, D])
    identf_d = din("ident_f", [128, 128])
    identb_d = din("ident_b", [128, 128], BF16)
    perm_d = din("perm", [128, 128], BF16)
    cos_d = din("cosT", [128, 1536], BF16)
    sin_d = din("sinT", [128, 1536], BF16)
    mask_d = din("mask2", [128, 2, 128], BF16)
    pmd_d = din("pm_diag", [128, 4, 4, 128], BF16)
    pme_d = din("pm_edge", [128, 4, 4, 16], BF16)

    yp_d = dout("yp", [512, D])
    ys_d = dout("ys", [1024, D])
    nk_d = dout("nk", [DEPTH, 512, 128])
    nv_d = dout("nv", [DEPTH, 512, 128])
    if DEBUG_DUMP:
        dbg_d = dout("dbg", [DEPTH, 128, 8, NTOK])

    es = ExitStack()

    def sb(name, shape, dt=F32):
        return es.enter_context(nc.sbuf_tensor(name, list(shape), dt))

    psum = es.enter_context(nc.psum_tensor("psum", [128, 4096], F32))

    def bank(b, n=512, off=0):
        return psum[:, b * 512 + off: b * 512 + off + n]

    xT = sb("xT", [128, 8, NTOK])
    hT = sb("hT", [128, 8, NTOK], BF16)
    qT = sb("qT", [128, 4, NTOK], BF16)
    zT = sb("zT", [128, 4, NTOK], BF16)
    kT = sb("kT", [128, KT_COLS], BF16)
    NVA = 64 + 128 * 36
    vaug = sb("vaug", [128, NVA], BF16)
    ring = sb("ring", [128, NRING, 8, 512], BF16)
    AR = sb("arena", [128, 8192], BF16)
    zf = AR[:, 0:4096].bitcast(F32).rearrange("p (g t) -> p g t", g=4)
    stage = AR[:, 0:4096].bitcast(F32).rearrange("p (s t) -> p s t", s=2)
    pooledT = AR[:, 4096:6144].rearrange("p (g t) -> p g t", g=4)
    utm = AR[:, 6144:8192].rearrange("p (s t) -> p s t", s=4)
    PT = AR[:, 0:3072].rearrange("p (s t) -> p s t", s=3)
    atf = AR[:, 3072:5120].bitcast(F32).rearrange("p (a c t) -> p a c t", a=2, c=4)
    lnt = AR[:, 5120:7168].bitcast(F32).rearrange("p (a t) -> p a t", a=2)
    sqa = AR[:, 7168:7680].rearrange("p (c t) -> p c t", c=4)
    rb2 = AR[:, 7680:7936].bitcast(F32)
    vec_in = AR[:, 7168:7680].bitcast(F32).rearrange("p (s t) -> p s t", s=2)
    sq = sb("sq", [128, 4, 512], BF16)
    rb = sb("rb", [128, 512])
    tmpf = sb("tmpf", [128, 3, 512])
    q16 = sb("q16", [128, 2, 512], BF16)
    kvf = sb("kvf", [128, 256])
    ckt = sb("ckt", [128, 2, 128], BF16)
    cosT = sb("cosT_sb", [128, 1536], BF16)
    sinT = sb("sinT_sb", [128, 1536], BF16)
    mask2 = sb("mask2_sb", [128, 2, 128], BF16)
    pmd = sb("pmd_sb", [128, 4, 4, 128], BF16)
    pme = sb("pme_sb", [128, 4, 4, 16], BF16)
    wp = sb("wp_sb", [128, 2, 4, 128], BF16)
    identf = sb("identf_sb", [128, 128])
    identb = sb("identb_sb", [128, 128], BF16)
    onesb = sb("onesb", [128, 128], BF16)
    perm = sb("perm_sb", [128, 128], BF16)
    vec = sb("vec", [128, 256])
    esink = sb("esink", [128, 32])
    sT = sb("sT", [128, 8, 2], BF16)
    mods = sb("mods", [128, 2, 24, 2])
    gmul = sb("gmul", [128, 2, 8, 2])
    ssq = sb("ssq", [128, 4])
    selr = sb("selr", [1, 128], BF16)
    esb = sb("esb", [128, 2, 512], BF16)
    fnb = hT[:, 0, :].bitcast(F32)

    P = Prog(nc)
    cnt = {"dma": 0, "ring": 0, "ps_norm": 0, "evac": 0, "rope": 0}

    def MM(out, lhsT, rhs, start, stop, reads, writes, accum=False):
        return P.op("pe", lambda e: e.matmul(out, lhsT=lhsT, rhs=rhs, start=start, stop=stop),
                    reads=reads, writes=writes, accum=accum)

    def TR(out, in_, ident, reads, writes):
        return P.op("pe", lambda e: e.transpose(out=out, in_=in_, identity=ident), reads=reads, writes=writes)

    def ACT(out, in_, func, reads, writes, bias=None, scale=None, accum_out=None):
        kw = {}
        if bias is not None:
            kw["bias"] = bias
        if scale is not None:
            kw["scale"] = scale
        if accum_out is not None:
            kw["accum_out"] = accum_out
        return P.op("act", lambda e: e.activation(out=out, in_=in_, func=func, **kw), reads=reads, writes=writes)

    def ACOPY(out, in_, reads, writes):
        return P.op("act", lambda e: e.copy(out=out, in_=in_), reads=reads, writes=writes)

    def VCOPY(out, in_, reads, writes):
        return P.op("dve", lambda e: e.tensor_copy(out=out, in_=in_), reads=reads, writes=writes)

    def TT(out, in0, in1, op, reads, writes):
        return P.op("dve", lambda e: e.tensor_tensor(out=out, in0=in0, in1=in1, op=op), reads=reads, writes=writes)

    def STT(out, in0, scalar, in1, op0, op1, reads, writes):
        return P.op("dve", lambda e: e.scalar_tensor_tensor(out=out, in0=in0, scalar=scalar, in1=in1, op0=op0, op1=op1),
                    reads=reads, writes=writes)

    def TS(out, in0, scalar1, op0, reads, writes):
        return P.op("dve", lambda e: e.tensor_scalar(out=out, in0=in0, scalar1=scalar1, scalar2=None, op0=op0),
                    reads=reads, writes=writes)

    def DMA(eng, out, in_, reads, writes, sem):
        return P.op(eng, lambda e: e.dma_start(out=out, in_=in_), reads=reads, writes=writes, dma_sem=sem)

    def newsem(prefix):
        cnt["dma"] += 1
        return "%s%d" % (prefix, cnt["dma"])

    def c_bada(l):
        return vec[:, l * 24:(l + 1) * 24]

    def c_normw(l):
        return vec[:, 96 + l * 8: 96 + l * 8 + 8]

    def c_an(l, j):
        return vec[:, 128 + l * 4 + j: 128 + l * 4 + j + 1]

    def c_pn(l, j):
        return vec[:, 144 + l * 4 + j: 144 + l * 4 + j + 1]

    def c_psc(l, j):
        return vec[:, 160 + l * 4 + j: 160 + l * 4 + j + 1]

    def kx(cs, bs):
        return [("x", c, b) for c in cs for b in bs]

    def kh(cs, bs):
        return [("h", c, b) for c in cs for b in bs]

    def kq(cs, bs):
        return [("q", c, b) for c in cs for b in bs]

    def kz(cs, bs):
        return [("z", c, b) for c in cs for b in bs]

    def kk(bs):
        return [("k", b) for b in bs]

    def kv(bs):
        return [("v", b) for b in bs]

    def tok(b0, b1):
        return slice(b0 * 128, b1 * 128)

    DMA("sp", identf[:], identf_d, [], ["identf"], newsem("c"))
    DMA("sp", identb[:], identb_d, [], ["identb"], newsem("c"))
    DMA("sp", perm[:], perm_d, [], ["perm"], newsem("c"))
    DMA("sp", vec_in[:, 0, :], vecs_d[0:128, :], [], ["vec_in0"], newsem("c"))
    DMA("sp", vec_in[:, 1, :], vecs_d[128:256, :], [], ["vec_in1"], newsem("c"))
    DMA("sp", esink[:], sink_d.broadcast_to([128, 32]), [], ["esink"], newsem("c"))
    DMA("sp", cosT[:], cos_d, [], ["cos"], newsem("c"))
    DMA("sp", sinT[:], sin_d, [], ["sin"], newsem("c"))
    DMA("sp", mask2[:], mask_d, [], ["mask2"], newsem("c"))
    DMA("sp", pmd[:], pmd_d, [], ["pmd"], newsem("c"))
    DMA("sp", pme[:], pme_d, [], ["pme"], newsem("c"))
    P.op("dve", lambda e: e.memset(onesb[:], 1.0), writes=["onesb"])
    P.op("dve", lambda e: e.memset(selr[:, 0:64], 0.0), writes=["selr"])
    P.op("dve", lambda e: e.memset(selr[:, 64:128], 1.0), writes=["selr"])
    P.op("dve", lambda e: e.memset(vaug[:], 1.0), writes=kv(range(18)))

    for i in range(2):
        TR(bank(7, 128, i * 128), vec_in[:, i, :], identf[:], ["vec_in%d" % i, "identf"], [("ps", 7)])
    ACOPY(vec[:], bank(7, 256), [("ps", 7)], ["vec"])
    ACT(sT[:, :, 0], vec[:, 176:184], AF.Silu, ["vec"], ["sT"])
    ACT(sT[:, :, 1], vec[:, 184:192], AF.Silu, ["vec"], ["sT"])
    ACT(esink[:], esink[:], AF.Exp, ["esink"], ["esink"])

    def ring_load(src_ap, ncols, slot):
        src = src_ap.rearrange("(k p) c -> p k c", p=128)
        DMA("pool", ring[:, slot, :, 0:ncols], src, [], [("ring", slot)], "ring%d" % slot)
        return slot

    MODB = 3

    def emit_ada_granule(l, gi, slot):
        ring_load(wada_d[l][:, gi * 512:(gi + 1) * 512], 512, slot)
        pm = l % 2
        for jc in range(4):
            for k in range(8):
                MM(bank(7, 2, jc * 2), ring[:, slot, k, jc * 128:(jc + 1) * 128], sT[:, k, :], k == 0, k == 7,
                   [("ring", slot), "sT"], [("ps", 7)], accum=(k > 0))
        pv = bank(7, 8).rearrange("p (j v) -> p j v", v=2)
        for v in range(2):
            TT(mods[:, pm, gi * 4:(gi + 1) * 4, v], pv[:, :, v], vec[:, l * 24 + gi * 4: l * 24 + gi * 4 + 4], ALU.add,
               [("ps", 7), "vec"], [("mods", pm)])

    def emit_mods_finish(l):
        pm = l % 2
        for v in range(2):
            STT(gmul[:, pm, :, v], mods[:, pm, 8:16, v], 1.0, c_normw(l), ALU.add, ALU.mult,
                [("mods", pm), "vec"], [("gmul", pm)])

    def load_x_block(b):
        st = b % 2
        src = xp_d[b * 128:(b + 1) * 128, :] if b < 4 else xs_d[(b - 4) * 128:(b - 3) * 128, :]
        DMA("sp", stage[:, st, :], src, [], [("stage", st)], "xin%d" % st)
        pb = (b % 2) * 2
        for c in range(8):
            TR(bank(pb + c // 4, 128, (c % 4) * 128), stage[:, st, c * 128:(c + 1) * 128], identf[:],
               [("stage", st), "identf"], [("ps", pb + c // 4)])
        src_ps = psum[:, pb * 512: pb * 512 + 1024].rearrange("p (c t) -> p c t", c=8)
        if b % 2 == 0:
            ACOPY(xT[:, :, tok(b, b + 1)], src_ps, [("ps", pb), ("ps", pb + 1)], kx(range(8), [b]))
        else:
            VCOPY(xT[:, :, tok(b, b + 1)], src_ps, [("ps", pb), ("ps", pb + 1)], kx(range(8), [b]))

    xb = 0
    for gi in range(6):
        emit_ada_granule(0, gi, gi % 2)
        for _ in range(3 if gi < 4 else 2):
            if xb < NBLK:
                load_x_block(xb)
                xb += 1
    while xb < NBLK:
        load_x_block(xb)
        xb += 1
    emit_mods_finish(0)

    def units_of(nsamp):
        us = [(0, 4)]
        b = 4
        while b < 4 + nsamp:
            us.append((b, min(b + 4, 4 + nsamp)))
            b += 4
        return us

    def rsqrt_from_stats(ps_ap, dst, scale_div, rd_keys, key):
        ACT(dst, ps_ap, AF.Ln, rd_keys, [key], bias=EPS, scale=1.0 / scale_div)
        ACT(dst, dst, AF.Exp, [key], [key], scale=-0.5)

    def rope_evac(ps_b, n, lt0, dst_ap, dst_keys):
        i = cnt["rope"] % 2
        rbk = 4 + (cnt["rope"] % 4)
        cnt["rope"] += 1
        ACOPY(q16[:, i, 0:n], bank(ps_b, n), [("ps", ps_b)], [("q16", i)])
        MM(bank(rbk, n), perm[:], q16[:, i, 0:n], True, True, [("q16", i), "perm"], [("ps", rbk)])
        TT(tmpf[:, i, 0:n], bank(ps_b, n), cosT[:, lt0:lt0 + n], ALU.mult, [("ps", ps_b), "cos"], [("tmpf", i)])
        TT(tmpf[:, 2, 0:n], bank(rbk, n), sinT[:, lt0:lt0 + n], ALU.mult, [("ps", rbk), "sin"], [("tmpf", 2)])
        TT(dst_ap, tmpf[:, i, 0:n], tmpf[:, 2, 0:n], ALU.add, [("tmpf", i), ("tmpf", 2)], dst_keys)

    def f_slab(slot, col0, units, kchunks_fn, banks, evac_fn):
        for ui, (b0, b1) in enumerate(units):
            n = (b1 - b0) * 128
            pb = banks[ui]
            for k in range(8):
                rhs_ap, rkeys = kchunks_fn(k, b0, b1)
                MM(bank(pb, n), ring[:, slot, k, col0:col0 + 128], rhs_ap, k == 0, k == 7,
                   [("ring", slot)] + rkeys, [("ps", pb)], accum=(k > 0))
        for ui, (b0, b1) in enumerate(units):
            evac_fn(ui, b0, b1, banks[ui])

    def h_chunk(k, b0, b1):
        return hT[:, k, tok(b0, b1)], kh([k], range(b0, b1))

    def a_chunk(k, b0, b1):
        if k < 4:
            return qT[:, k, tok(b0, b1)], kq([k], range(b0, b1))
        return zT[:, k - 4, tok(b0, b1)], kz([k - 4], range(b0, b1))

    def vsel(b0):
        return 0 if b0 < 4 else 1

    out_ops = []

    def emit_layer(l):
        pm = l % 2
        ni = 12 - l
        nq = 11 - l
        units_in = units_of(ni)
        units_q = units_of(nq)
        blocks_in = list(range(4)) + list(range(4, 4 + ni))
        blocks_q = list(range(4)) + list(range(4, 4 + nq))

        DMA("pool", wp[:, pm], wpool_d[l].rearrange("g c d -> c g d"), [], [("wp", pm)], "wp%d" % pm)

        stage_gate(3 + 10 * l)
        def norm_A(b0, b1, pb):
            n = (b1 - b0) * 128
            for hf in range(2):
                ACT(sq[:, :, 0:n], xT[:, hf * 4:hf * 4 + 4, tok(b0, b1)], AF.Square,
                    kx(range(hf * 4, hf * 4 + 4), range(b0, b1)), ["sq"])
                for c4 in range(4):
                    c = hf * 4 + c4
                    MM(bank(pb, n), onesb[:], sq[:, c4, 0:n], c == 0, c == 7, ["sq", "onesb"], [("ps", pb)], accum=(c > 0))

        def norm_B(b0, b1, pb):
            n = (b1 - b0) * 128
            v = vsel(b0)
            rsqrt_from_stats(bank(pb, n), rb[:, 0:n], float(D), [("ps", pb)], "rb")
            for c in range(8):
                i = c % 2
                STT(tmpf[:, i, 0:n], xT[:, c, tok(b0, b1)], gmul[:, pm, c, v:v + 1], rb[:, 0:n], ALU.mult, ALU.mult,
                    kx([c], range(b0, b1)) + [("gmul", pm), "rb"], [("tmpf", i)])
                if c % 2 == 0:
                    ACT(hT[:, c, tok(b0, b1)], tmpf[:, i, 0:n], AF.Identity, [("tmpf", i), ("mods", pm)],
                        kh([c], range(b0, b1)), bias=mods[:, pm, c, v:v + 1], scale=1.0)
                else:
                    TS(hT[:, c, tok(b0, b1)], tmpf[:, i, 0:n], mods[:, pm, c, v:v + 1], ALU.add,
                       [("tmpf", i), ("mods", pm)], kh([c], range(b0, b1)))

        norm_sched = {}
        nU = len(units_in)

        def nA(u):
            norm_A(units_in[u][0], units_in[u][1], 6 + u % 2)

        def nB(u):
            norm_B(units_in[u][0], units_in[u][1], 6 + u % 2)

        nA(0)
        if nU > 1:
            nA(1)
        nB(0)
        for u in range(nU):
            lst = []
            if u + 2 < nU:
                lst.append(lambda u=u: nA(u + 2))
            if u + 1 < nU:
                lst.append(lambda u=u: nB(u + 1))
            norm_sched[units_in[u][0]] = lst

        stage_gate(4 + 10 * l)
        slot_u = ring_load(win_d[l][:, 1280:1792], 512, 0)
        targets = set(blocks_q)
        pooled_in_unit = {}

        def unit_index_q(tb):
            for ui, (b0, b1) in enumerate(units_q):
                if b0 <= tb < b1:
                    return ui
            return None

        pend = []

        def pool_stage2a(ui):
            b0, b1 = units_q[ui]
            n = (b1 - b0) * 128
            pk = [("pooledT", i) for i in range(b1 - b0)]
            for g in range(4):
                bk = 4 + g % 2
                MM(bank(bk, n), wp[:, pm, g, :], pooledT[:, g, 0:n], True, True, [("wp", pm)] + pk, [("ps", bk)])
                TS(zf[:, g, 0:n], bank(bk, n), c_psc(l, g), ALU.mult, [("ps", bk), "vec"], [("zf", g)])
            pend.append([1, lambda: pool_stage2b(ui)])

        def pool_stage2b(ui):
            b0, b1 = units_q[ui]
            n = (b1 - b0) * 128
            TT(sq[:, :, 0:n], zf[:, :, 0:n], zf[:, :, 0:n], ALU.mult, [("zf", g) for g in range(4)], ["sq"])
            for g in range(4):
                MM(bank(4, n), onesb[:], sq[:, g, 0:n], g == 0, g == 3, ["sq", "onesb"], [("ps", 4)], accum=(g > 0))
            rsqrt_from_stats(bank(4, n), rb[:, 0:n], 512.0, [("ps", 4)], "rb")
            TT(zT[:, :, tok(b0, b1)], zf[:, :, 0:n], rb[:, 0:n].unsqueeze(1).broadcast_to([128, 4, n]), ALU.mult,
               [("zf", g) for g in range(4)] + ["rb"], kz(range(4), range(b0, b1)))

        pcount = {"i": 0}

        def pool_target(tb):
            ui = unit_index_q(tb)
            b0, b1 = units_q[ui]
            if tb < 4:
                dtype_i = 0 if tb % 2 == 0 else 1
                prev_b = tb - 1 if tb % 2 == 1 else None
                next_b = tb + 1 if tb % 2 == 0 else None
                et_prev, et_next = 0, 1
            else:
                dtype_i = 2 if tb == 4 else 3
                prev_b = tb - 1 if tb > 4 else None
                next_b = tb + 1
                et_prev, et_next = 2, 3
            PB = 2 + pcount["i"] % 2
            pcount["i"] += 1
            for g in range(4):
                gs = slice(g * 128, (g + 1) * 128)
                MM(bank(PB, 128, g * 128), utm[:, tb % 4, gs], pmd[:, dtype_i, g, :], True,
                   (prev_b is None and next_b is None), [("utm", tb % 4), "pmd"], [("ps", PB)], accum=(g > 0))
                if prev_b is not None:
                    MM(bank(PB, 16, g * 128), utm[:, prev_b % 4, gs], pme[:, et_prev, g, :], False, next_b is None,
                       [("utm", prev_b % 4), "pme"], [("ps", PB)], accum=True)
                if next_b is not None:
                    MM(bank(PB, 16, g * 128 + 112), utm[:, next_b % 4, gs], pme[:, et_next, g, :], False, True,
                       [("utm", next_b % 4), "pme"], [("ps", PB)], accum=True)
            off = (tb - b0) * 128
            VCOPY(pooledT[:, :, off:off + 128], bank(PB).rearrange("p (g t) -> p g t", g=4), [("ps", PB)],
                  [("pooledT", tb - b0)])
            pooled_in_unit[ui] = pooled_in_unit.get(ui, 0) + 1
            if pooled_in_unit[ui] == b1 - b0:
                pend.append([1, lambda: pool_stage2a(ui)])

        def run_pending(flush=False):
            while True:
                due = [p for p in pend if p[0] <= 0 or flush]
                if not due:
                    break
                p = due[0]
                pend.remove(p)
                p[1]()
            for p in pend:
                p[0] -= 1

        for bi, b in enumerate(blocks_in):
            pb = bi % 2
            for k in range(8):
                MM(bank(pb), hT[:, k, tok(b, b + 1)], ring[:, slot_u, k, :], k == 0, k == 7,
                   [("ring", slot_u)] + kh([k], [b]), [("ps", pb)], accum=(k > 0))
            ACOPY(utm[:, b % 4, :], bank(pb), [("ps", pb)], [("utm", b % 4)])
            for fn in norm_sched.get(b, []):
                fn()
            run_pending()
            if b < 4:
                if b % 2 == 1:
                    pend.append([0, lambda b=b: (pool_target(b - 1), pool_target(b))])
            elif b - 1 >= 4 and (b - 1) in targets:
                pend.append([0, lambda b=b: pool_target(b - 1)])
        stage_gate(5.1 + 10 * l)
        slot_kv = ring_load(win_d[l][:, 512:768], 256, 1)
        DMA("pool", ckt[:], ck_d[l].rearrange("(b p) f -> p b f", p=128), [], ["ckt"], "ckt")
        for cb in range(2):
            e0 = 2 * (16 + cb)
            dstv = vaug[:, 64 + 128 * e0: 64 + 128 * e0 + 256].rearrange("p (g x) -> p g x", g=2)[:, :, 0:64]
            srcv = cv_d[l][cb * 128:(cb + 1) * 128, :].rearrange("p (g d) -> p g d", g=2)
            DMA("pool", dstv, srcv, [], kv([16 + cb]), "cv%d" % cb)

        def kv_block(bi, b):
            pb = bi % 2
            if b < 4:
                for k in range(8):
                    MM(bank(pb, 256), hT[:, k, tok(b, b + 1)], ring[:, slot_kv, k, 0:256], k == 0, k == 7,
                       [("ring", slot_kv)] + kh([k], [b]), [("ps", pb)], accum=(k > 0))
                if b % 2 == 0:
                    kbuf, kkey = kvf[:], "kvf"
                else:
                    kbuf, kkey = tmpf[:, 2, 0:256], ("tmpf", 2)
                ACOPY(kbuf, bank(pb, 256), [("ps", pb)], [kkey])
                out_ops.append(DMA("sp", nk_d[l][b * 128:(b + 1) * 128, :], kbuf[:, 0:128], [kkey], [], "okv%d" % (b % 2)))
                out_ops.append(DMA("sp", nv_d[l][b * 128:(b + 1) * 128, :], kbuf[:, 128:256], [kkey], [], "okv%d" % (b % 2)))
                vsrc = bank(pb, 128, 128).rearrange("p (g x) -> p g x", g=2)
            else:
                for k in range(8):
                    MM(bank(pb, 128), hT[:, k, tok(b, b + 1)], ring[:, slot_kv, k, 128:256], k == 0, k == 7,
                       [("ring", slot_kv)] + kh([k], [b]), [("ps", pb)], accum=(k > 0))
                vsrc = bank(pb, 128).rearrange("p (g x) -> p g x", g=2)
            e0 = 2 * b
            dstv = vaug[:, 64 + 128 * e0: 64 + 128 * e0 + 256].rearrange("p (g x) -> p g x", g=2)[:, :, 0:64]
            VCOPY(dstv, vsrc, [("ps", pb)], kv([b]))

        for bi, b in enumerate(blocks_in[:4]):
            kv_block(bi, b)
        run_pending(flush=True)
        ctb = bank(7, 128).bitcast(BF16)
        for cb in range(2):
            TR(ctb[:, cb * 128:(cb + 1) * 128], ckt[:, cb, :], identb[:], ["ckt", "identb"], [("ps", 7)])
        ACOPY(kT[:, NTOK:NTOK + 256], ctb, [("ps", 7)], kk([16, 17]))
        for bi, b in enumerate(blocks_in):
            if bi >= 4:
                kv_block(bi, b)

        stage_gate(5.4 + 10 * l)

        def evac_k(ui, b0, b1, pb):
            n = (b1 - b0) * 128
            if b0 < 4:
                ACOPY(kT[:, tok(b0, b1)], bank(pb, n), [("ps", pb)], kk(range(b0, b1)))
            else:
                rope_evac(pb, n, (b0 - 4) * 128, kT[:, tok(b0, b1)], kk(range(b0, b1)))
        f_slab(slot_kv, 0, units_in, h_chunk, [0, 1, 2, 3], evac_k)

        stage_gate(5 + 10 * l)
        slot_gp = ring_load(win_d[l][:, 1792:2304], 512, 0)
        for j in range(4):
            banks = [0, 1, 2, 3] if j % 2 == 0 else [4, 5, 6, 7]

            def evac_gp(ui, b0, b1, pb, j=j):
                n = (b1 - b0) * 128
                i = cnt["evac"] % 2
                cnt["evac"] += 1
                ACT(q16[:, i, 0:n], bank(pb, n), AF.Silu, [("ps", pb)], [("q16", i)])
                STT(zT[:, j, tok(b0, b1)], zT[:, j, tok(b0, b1)], c_pn(l, j), q16[:, i, 0:n], ALU.mult, ALU.mult,
                    kz([j], range(b0, b1)) + [("q16", i), "vec"], kz([j], range(b0, b1)))
            f_slab(slot_gp, j * 128, units_q, h_chunk, banks, evac_gp)

        stage_gate(7 + 10 * l)
        slot_q = ring_load(win_d[l][:, 0:512], 512, 1)
        for c in range(4):
            def evac_q(ui, b0, b1, pb, c=c):
                n = (b1 - b0) * 128
                if b0 < 4:
                    ACOPY(qT[:, c, tok(b0, b1)], bank(pb, n), [("ps", pb)], kq([c], range(b0, b1)))
                else:
                    rope_evac(pb, n, (b0 - 4) * 128, qT[:, c, tok(b0, b1)], kq([c], range(b0, b1)))
            f_slab(slot_q, c * 128, units_q, h_chunk, [0, 1, 2, 3], evac_q)
            if c == 1 and l + 1 < DEPTH:
                emit_ada_granule(l + 1, 0, 0)

        stage_gate(8 + 10 * l)
        VCOPY(esb[:].rearrange("p g (h q) -> p (g h) q", h=4), esink[:, l * 8:(l + 1) * 8].unsqueeze(2).broadcast_to([128, 8, 128]),
              ["esink"], ["esb"])
        ada_pending = [1] if l + 1 < DEPTH else []
        ptasks = []
        for qi, qb in enumerate(blocks_q):
            if qb < 4:
                s0 = (qb // 2) * 2
                keys = [(s0, None), (s0 + 1, None)]
            else:
                keys = []
                if qb > 4:
                    keys.append((qb - 1, 0))
                keys.append((qb, None))
                keys.append((qb + 1, 1))
                keys += [(16, None), (17, None)]
            ptasks.append(dict(qi=qi, qb=qb, keys=keys, ob=((4, 5) if qi % 2 == 0 else (6, 7)), ab=qi % 2))
        psteps = [(ti, ki) for ti, t in enumerate(ptasks) for ki in range(len(t["keys"]))]
        last_ps = {}
        for si_, (ti_, ki_) in enumerate(psteps):
            last_ps[ti_] = si_
        deferred = []

        def emit_qk(si):
            ti, ki = psteps[si]
            t = ptasks[ti]
            qb = t["qb"]
            kb, mtype = t["keys"][ki]
            sp = si % 2
            pt = si % 3
            for g in range(2):
                lo, hi = 64 * g, 64 * g + 64
                MM(bank(2 * sp + g), kT[lo:hi, kb * 128:(kb + 1) * 128], qT[lo:hi, :, tok(qb, qb + 1)], True, True,
                   kk([kb]) + kq(range(4), [qb]), [("ps", 2 * sp + g)])
            ACT(PT[:, pt, :], psum[:, 2 * sp * 512:(2 * sp + 2) * 512], AF.Exp, [("ps", 2 * sp), ("ps", 2 * sp + 1)],
                [("PT", pt)], scale=0.125)
            if mtype is not None:
                ptv = PT[:, pt, :].rearrange("p (h q) -> p h q", h=8)
                TT(ptv, ptv, mask2[:, mtype, :].unsqueeze(1).broadcast_to([128, 8, 128]), ALU.mult,
                   [("PT", pt), "mask2"], [("PT", pt)])

        def emit_unit_rmsnorm(ub0, ub1):
            n = (ub1 - ub0) * 128
            uk = kq(range(4), range(ub0, ub1))
            TT(sq[:, :, 0:n], qT[:, :, tok(ub0, ub1)], qT[:, :, tok(ub0, ub1)], ALU.mult, uk, ["sq"])
            for c in range(4):
                MM(bank(3, n), onesb[:], sq[:, c, 0:n], c == 0, c == 3, ["sq", "onesb"], [("ps", 3)], accum=(c > 0))
            rsqrt_from_stats(bank(3, n), rb[:, 0:n], 512.0, [("ps", 3)], "rb")
            TT(qT[:, :, tok(ub0, ub1)], qT[:, :, tok(ub0, ub1)], rb[:, 0:n].unsqueeze(1).broadcast_to([128, 4, n]), ALU.mult,
               uk + ["rb"], uk)

        def emit_pv(si):
            ti, ki = psteps[si]
            t = ptasks[ti]
            qb, ab = t["qb"], t["ab"]
            kb, mtype = t["keys"][ki]
            pt = si % 3
            first, last = (ki == 0), (ki == len(t["keys"]) - 1)
            for g in range(2):
                ob = t["ob"][g]
                en = 2 * kb + g
                MM(bank(ob), vaug[:, 64 + 128 * en: 64 + 128 * en + 128], PT[:, pt, g * 512:(g + 1) * 512], first, last,
                   kv([kb]) + [("PT", pt)], [("ps", ob)], accum=(not first))
            if not last:
                return
            for g in range(2):
                ob = t["ob"][g]
                TT(lnt[64:128, g, :], bank(ob)[64:128, :], esb[64:128, g, :], ALU.add, [("ps", ob), "esb"], [("lnt", g)])

            def finish2(t=t, qb=qb):
                ACT(lnt[64:128, :, :], lnt[64:128, :, :], AF.Ln, [("lnt", 0), ("lnt", 1)], [("lnt", 0), ("lnt", 1)])
                ACT(lnt[64:128, :, :], lnt[64:128, :, :], AF.Exp, [("lnt", 0), ("lnt", 1)], [("lnt", 0), ("lnt", 1)], scale=-1.0)
                for g in range(2):
                    ob = t["ob"][g]
                    TT(qT[0:64, 2 * g:2 * g + 2, tok(qb, qb + 1)], bank(ob, 256)[0:64, :].rearrange("p (j t) -> p j t", j=2),
                       lnt[64:128, g, 0:256].rearrange("p (j t) -> p j t", j=2), ALU.mult,
                       [("ps", ob), ("lnt", g)], kq([2 * g, 2 * g + 1], [qb]))
                    TT(qT[64:128, 2 * g:2 * g + 2, tok(qb, qb + 1)], bank(ob, 256, 256)[0:64, :].rearrange("p (j t) -> p j t", j=2),
                       lnt[64:128, g, 256:512].rearrange("p (j t) -> p j t", j=2), ALU.mult,
                       [("ps", ob), ("lnt", g)], kq([2 * g, 2 * g + 1], [qb]))
            deferred.append((si + 1, finish2))
            for (ub0, ub1) in units_q:
                if qb == ub1 - 1:
                    deferred.append((si + 3, lambda ub0=ub0, ub1=ub1: emit_unit_rmsnorm(ub0, ub1)))
            if ada_pending and t["qi"] >= 3:
                gi = ada_pending.pop(0)
                deferred.append((si + 2, lambda gi=gi: emit_ada_granule(l + 1, gi, 1)))

        nst = len(psteps)
        PIPE = 2
        for si in range(nst + PIPE):
            if si < nst:
                emit_qk(si)
            if si >= PIPE:
                cur = si - PIPE
                emit_pv(cur)
                for d in [d for d in deferred if d[0] <= cur]:
                    deferred.remove(d)
                    d[1]()
        while ada_pending:
            emit_ada_granule(l + 1, ada_pending.pop(0), 1)

        stage_gate(9 + 10 * l)
        slot_ga = ring_load(win_d[l][:, 768:1280], 512, 0)

        def ga_mm(j, ulist, banks):
            for ui, (b0, b1) in ulist:
                n = (b1 - b0) * 128
                pb = banks[ui]
                for k in range(8):
                    MM(bank(pb, n), ring[:, slot_ga, k, j * 128:(j + 1) * 128], hT[:, k, tok(b0, b1)], k == 0, k == 7,
                       [("ring", slot_ga)] + kh([k], range(b0, b1)), [("ps", pb)], accum=(k > 0))

        def evac_ga(ui, b0, b1, pb, j):
            n = (b1 - b0) * 128
            i = cnt["evac"] % 2
            cnt["evac"] += 1
            ACT(q16[:, i, 0:n], bank(pb, n), AF.Silu, [("ps", pb)], [("q16", i)])
            STT(qT[:, j, tok(b0, b1)], qT[:, j, tok(b0, b1)], c_an(l, j), q16[:, i, 0:n], ALU.mult, ALU.mult,
                kq([j], range(b0, b1)) + [("q16", i), "vec"], kq([j], range(b0, b1)))

        ulist = list(enumerate(units_q))
        ga_mm(0, ulist[:3], [0, 1, 2, 3])
        for d in list(deferred):
            d[1]()
        deferred.clear()
        if l + 1 < DEPTH:
            emit_ada_granule(l + 1, 2, 1)
        ga_mm(0, ulist[3:], [0, 1, 2, 3])
        for ui, (b0, b1) in ulist:
            evac_ga(ui, b0, b1, [0, 1, 2, 3][ui], 0)
        for j in range(1, 4):
            banks = [0, 1, 2, 3] if j % 2 == 0 else [4, 5, 6, 7]
            f_slab(slot_ga, j * 128, units_q, h_chunk, banks, lambda ui, b0, b1, pb, j=j: evac_ga(ui, b0, b1, pb, j))

        stage_gate(10 + 10 * l)
        for half in range(2):
            slot_o = ring_load(wout_d[l][:, half * 512:(half + 1) * 512], 512, 1 - half)
            for cc in range(4):
                c = half * 4 + cc
                banks = [0, 1, 2, 3] if cc % 2 == 0 else [4, 5, 6, 7]
                if l + 1 < DEPTH and cc == 2:
                    emit_ada_granule(l + 1, 3 + half, half)

                def evac_o(ui, b0, b1, pb, c=c):
                    n = (b1 - b0) * 128
                    v = vsel(b0)
                    STT(xT[:, c, tok(b0, b1)], bank(pb, n), mods[:, pm, 16 + c, v:v + 1], xT[:, c, tok(b0, b1)],
                        ALU.mult, ALU.add, [("ps", pb), ("mods", pm)] + kx([c], range(b0, b1)), kx([c], range(b0, b1)))
                f_slab(slot_o, cc * 128, units_q, a_chunk, banks, evac_o)
        if l + 1 < DEPTH:
            emit_ada_granule(l + 1, 5, 1)
            emit_mods_finish(l + 1)

        if DEBUG_DUMP:
            out_ops.append(DMA("sp", dbg_d[l], xT[:], kx(range(8), range(NBLK)), [], "dbg"))

    def emit_epilogue():
        DMA("sp", fnb, fn_d.broadcast_to([128, D]), [], kh([0], range(NBLK)), "fnb")
        fnk = kh([0], range(NBLK))
        own = list(range(4)) + list(range(4, 12))
        for i, b in enumerate(own):
            pb = (i % 2) * 2
            st = i % 2
            for c in range(8):
                TR(bank(pb + c // 4, 128, (c % 4) * 128), xT[:, c, tok(b, b + 1)], identf[:], kx([c], [b]) + ["identf"],
                   [("ps", pb + c // 4)])
            for hb in range(2):
                ACT(stage[:, st, hb * 512:(hb + 1) * 512], bank(pb + hb), AF.Square, [("ps", pb + hb)],
                    [("stage", st), ("ssq", st, hb)], accum_out=ssq[:, 2 * st + hb: 2 * st + hb + 1])
            s0 = ssq[:, 2 * st:2 * st + 1]
            TT(s0, s0, ssq[:, 2 * st + 1:2 * st + 2], ALU.add, [("ssq", st, 0), ("ssq", st, 1)], [("ssq", st, 0)])
            ACT(s0, s0, AF.Ln, [("ssq", st, 0)], [("ssq", st, 0)], bias=EPS, scale=1.0 / D)
            ACT(s0, s0, AF.Exp, [("ssq", st, 0)], [("ssq", st, 0)], scale=-0.5)
            for hb in range(2):
                STT(stage[:, st, hb * 512:(hb + 1) * 512], bank(pb + hb), s0, fnb[:, hb * 512:(hb + 1) * 512], ALU.mult, ALU.mult,
                    [("ps", pb + hb), ("ssq", st, 0)] + fnk, [("stage", st)])
            dst = yp_d[b * 128:(b + 1) * 128, :] if b < 4 else ys_d[(b - 4) * 128:(b - 3) * 128, :]
            out_ops.append(DMA("sp", dst, stage[:, st, :], [("stage", st)], [], "yout%d" % st))


    try:
        for l in range(DEPTH):
            emit_layer(l)
        stage_gate(50)
        emit_epilogue()
    except StopBuild:
        pass

    P.emit(final_wait_ops=out_ops)
    es.close()
    return nc


_CACHE = {}


def _prep_shared(inp):
    qp, ap = _qperm(), _aperm()
    w_in = np.asarray(inp["w_in"], np.float32)
    colperm = np.concatenate([qp, np.arange(512, 768), 768 + ap, np.arange(1280, 2304)])
    w_in_p = np.ascontiguousarray(w_in[:, :, colperm])
    w_out = np.asarray(inp["w_out"], np.float32)
    rowperm = np.concatenate([ap, np.arange(512, 1024)])
    w_out_p = np.ascontiguousarray(w_out[:, rowperm, :])
    attn_norm_p = np.asarray(inp["attn_norm"], np.float32)[:, ap]
    return w_in_p, w_out_p, attn_norm_p


def kernel(x_prompt, x_sample, cache_k, cache_v, c, c_ctx, norm_w, w_ada, b_ada, w_in, sink,
           attn_norm, pool_norm, w_pool, pool_scale, w_out, final_norm):
    f32 = np.float32
    inp = dict(w_in=w_in, w_out=w_out, attn_norm=attn_norm)
    w_in_p, w_out_p, attn_norm_p = _prep_shared(inp)
    x_prompt = np.asarray(x_prompt, f32)
    x_sample = np.asarray(x_sample, f32)
    cache_k = np.asarray(cache_k, f32)
    cache_v = np.asarray(cache_v, f32)
    c = np.asarray(c, f32)
    c_ctx = np.asarray(c_ctx, f32)
    w_ada = np.ascontiguousarray(np.asarray(w_ada, f32))
    w_pool = np.ascontiguousarray(np.asarray(w_pool, f32))
    b_ada = np.asarray(b_ada, f32)
    norm_w = np.asarray(norm_w, f32)
    pool_norm = np.asarray(pool_norm, f32)
    pool_scale = np.asarray(pool_scale, f32)
    sink = np.asarray(sink, f32)
    final_norm = np.asarray(final_norm, f32)

    bf = ml_dtypes.bfloat16
    ident = np.eye(128, dtype=f32)
    consts = {}
    for rev in (False, True):
        cos, sin = _rope_tables(rev)
        pd, pe = _pool_tables(rev)
        consts[rev] = dict(cosT=cos.astype(bf), sinT=sin.astype(bf), pm_diag=pd.astype(bf), pm_edge=pe.astype(bf))
    mask2 = np.ascontiguousarray(_masks()[:, :, 0, :]).astype(bf)
    permm = _perm_matrix().astype(bf)

    in_maps = []
    for i in range(NCORES):
        b, half = i // 2, i % 2
        rev = half == 1
        if not rev:
            xs = x_sample[b, 0:1536]
        else:
            xs = x_sample[b, ::-1][0:1536]
        vecs = np.zeros((256, 128), f32)
        vecs[0:96] = b_ada.reshape(DEPTH * 24, 128)
        vecs[96:128] = norm_w.reshape(DEPTH * 8, 128)
        vecs[128:144] = attn_norm_p.reshape(DEPTH * 4, 128)
        vecs[144:160] = pool_norm.reshape(DEPTH * 4, 128)
        vecs[160:176] = pool_scale.reshape(DEPTH * 4, 128)
        vecs[176:184] = c_ctx.reshape(8, 128)
        vecs[184:192] = c[b].reshape(8, 128)
        m = dict(
            xp=np.ascontiguousarray(x_prompt[2 * i:2 * i + 2].reshape(512, D)),
            xs=np.ascontiguousarray(xs),
            ck=np.ascontiguousarray(cache_k[b].reshape(DEPTH, 256, 128)),
            cv=np.ascontiguousarray(cache_v[b].reshape(DEPTH, 256, 128)),
            w_ada=w_ada, w_in=w_in_p, w_out=w_out_p, w_pool=w_pool,
            vecs=vecs, sinkv=np.ascontiguousarray(sink.reshape(1, 32)),
            fnorm=np.ascontiguousarray(final_norm.reshape(1, D)),
            ident_f=ident, ident_b=ident.astype(bf), perm=permm, mask2=mask2,
            **consts[rev],
        )
        in_maps.append(m)

    if "nc" not in _CACHE:
        _CACHE["nc"] = build_program()
    nc = _CACHE["nc"]
    res = run_bass_kernel_spmd(nc, in_maps, core_ids=list(range(NCORES)))
    outs = res.results

    y_prompt = np.zeros((16, 256, D), f32)
    y_sample = np.zeros((4, 2048, D), f32)
    new_k = np.zeros((16, DEPTH, 256, 2, 64), f32)
    new_v = np.zeros((16, DEPTH, 256, 2, 64), f32)
    for i in range(NCORES):
        b, half = i // 2, i % 2
        r = outs[i]
        y_prompt[2 * i:2 * i + 2] = np.asarray(r["yp"]).reshape(2, 256, D)
        ys = np.asarray(r["ys"])
        if half == 0:
            y_sample[b, 0:1024] = ys
        else:
            y_sample[b, 1024:2048] = ys[::-1]
        nk = np.asarray(r["nk"]).reshape(DEPTH, 2, 256, 2, 64)
        nv = np.asarray(r["nv"]).reshape(DEPTH, 2, 256, 2, 64)
        new_k[2 * i:2 * i + 2] = nk.transpose(1, 0, 2, 3, 4)
        new_v[2 * i:2 * i + 2] = nv.transpose(1, 0, 2, 3, 4)
    if DEBUG_DUMP:
        kernel.dbg = [np.asarray(o["dbg"]) for o in outs]
    return (y_prompt, y_sample, new_k, new_v)
```

```python
from contextlib import ExitStack
import numpy as np
import ml_dtypes
import concourse.bass as bass
import concourse.mybir as mybir
from concourse.bass_utils import run_bass_kernel_spmd

F32 = mybir.dt.float32
BF16 = mybir.dt.bfloat16
AF = mybir.ActivationFunctionType
ALU = mybir.AluOpType

D = 1024
DEPTH = 4
NCORES = 8
EPS = 1e-6
NTOK = 2048
NBLK = 16
KT_COLS = NTOK + 256
NRING = 2
DEBUG_DUMP = False
STOP = 99


class StopBuild(Exception):
    pass


def stage_gate(n):
    if n > STOP:
        raise StopBuild()

ENGS = ("pe", "act", "dve", "pool", "sp")


class Op:
    __slots__ = ("eng", "fn", "deps", "signal", "tick", "dma_sem", "dma_val")

    def __init__(self, eng, fn):
        self.eng = eng
        self.fn = fn
        self.deps = []
        self.signal = False
        self.tick = None
        self.dma_sem = None
        self.dma_val = None


class Prog:
    def __init__(self, nc):
        self.nc = nc
        self.ops = {e: [] for e in ENGS}
        self.res = {}
        self.dma_tot = {}

    def op(self, eng, fn, reads=(), writes=(), accum=False, dma_sem=None):
        o = Op(eng, fn)
        deps = []
        for r in reads:
            st = self.res.get(r)
            if st is not None and st[0] is not None:
                deps.append(("raw", st[0]))
            if st is not None and isinstance(r, tuple) and r[0] == "ps":
                for rd in st[1]:
                    if rd.eng != eng:
                        deps.append(("raw", rd))
        for w in writes:
            st = self.res.get(w)
            if st is not None:
                if st[0] is not None and not accum:
                    deps.append(("waw", st[0]))
                for rd in st[1]:
                    deps.append(("war", rd))
        seen = set()
        for kind, d in deps:
            if d is o or id(d) in seen:
                continue
            if d.eng == eng and d.dma_sem is None and dma_sem is None:
                if eng == "pe":
                    continue
            seen.add(id(d))
            o.deps.append(d)
        for r in reads:
            self.res.setdefault(r, [None, []])[1].append(o)
        for w in writes:
            st = self.res.setdefault(w, [None, []])
            st[0] = o
            st[1] = []
        if dma_sem is not None:
            o.dma_sem = dma_sem
            self.dma_tot[dma_sem] = self.dma_tot.get(dma_sem, 0) + 16
            o.dma_val = self.dma_tot[dma_sem]
        self.ops[eng].append(o)
        return o

    def emit(self, final_wait_ops=()):
        nc = self.nc
        for e in ENGS:
            for o in self.ops[e]:
                for d in o.deps:
                    d.signal = True
        for o in final_wait_ops:
            o.signal = True
        for e in ENGS:
            t = 0
            for o in self.ops[e]:
                if o.dma_sem is None and o.signal:
                    t += 1
                    o.tick = t
        with ExitStack() as es:
            sems = {}
            for e in ENGS:
                sems[e] = es.enter_context(nc.semaphore("s_" + e))
            for k in self.dma_tot:
                sems[("dma", k)] = es.enter_context(nc.semaphore("d_" + str(k)))
            block = es.enter_context(nc.Block())

            def run(eng_name, e):
                waited = {}
                for o in self.ops[eng_name]:
                    need = {}
                    for d in o.deps:
                        if d.dma_sem is not None:
                            key, val = ("dma", d.dma_sem), d.dma_val
                        else:
                            key, val = d.eng, d.tick
                        if need.get(key, 0) < val:
                            need[key] = val
                    for key, val in need.items():
                        if waited.get(key, 0) < val:
                            e.wait_ge(sems[key], val)
                            waited[key] = val
                    inst = o.fn(e)
                    if o.dma_sem is not None:
                        inst.then_inc(sems[("dma", o.dma_sem)], 16)
                    elif o.signal:
                        inst.then_inc(sems[eng_name], 1)
                if eng_name == "sp":
                    need = {}
                    for o in final_wait_ops:
                        if o.dma_sem is not None:
                            key, val = ("dma", o.dma_sem), o.dma_val
                        else:
                            key, val = o.eng, o.tick
                        need[key] = max(need.get(key, 0), val)
                    for key, val in need.items():
                        e.wait_ge(sems[key], val)

            @block.tensor
            def _(e):
                run("pe", e)

            @block.scalar
            def _(e):
                run("act", e)

            @block.vector
            def _(e):
                run("dve", e)

            @block.gpsimd
            def _(e):
                run("pool", e)

            @block.sync
            def _(e):
                run("sp", e)


def _aperm():
    idx = []
    for ac in range(4):
        g, j = ac // 2, ac % 2
        for h in (4 * g + j, 4 * g + 2 + j):
            idx += [h * 64 + d for d in range(64)]
    return np.array(idx)


def _qperm():
    idx = []
    for c in range(4):
        for h in (c, 4 + c):
            idx += [h * 64 + d for d in range(64)]
    return np.array(idx)


def _pool_op(L, w, n):
    M = np.zeros((n, n + 16), np.float64)
    for t in range(n):
        lo = min(max(t - w // 2, 0), L)
        hi = min(max(t + w // 2, 0), L)
        for s in range(lo, hi):
            if s < n + 16:
                M[t, s] += 1.0 / (hi - lo)
        M[t, t] -= 1.0
    return M


def _pool_tables(reverse):
    pd = np.zeros((128, 4, 4, 128), np.float32)
    pe = np.zeros((128, 4, 4, 16), np.float32)
    for g, w in enumerate((2, 4, 8, 16)):
        M = np.zeros((256, 256))
        for t in range(256):
            lo = min(max(t - w // 2, 0), 256)
            hi = min(max(t + w // 2, 0), 256)
            M[t, lo:hi] += 1.0 / (hi - lo)
            M[t, t] -= 1.0
        pd[:, 0, g, :] = M[0:128, 0:128].T
        pd[:, 1, g, :] = M[128:256, 128:256].T
        pe[:, 0, g, :] = M[128:144, 0:128].T
        pe[:, 1, g, :] = M[112:128, 128:256].T
        L = 2048
        Mg = np.zeros((L, L))
        for t in range(L):
            lo = min(max(t - w // 2, 0), L)
            hi = min(max(t + w // 2, 0), L)
            Mg[t, lo:hi] += 1.0 / (hi - lo)
            Mg[t, t] -= 1.0
        Ml = Mg[::-1, ::-1] if reverse else Mg
        pd[:, 2, g, :] = Ml[0:128, 0:128].T
        pd[:, 3, g, :] = Ml[128:256, 128:256].T
        pe[:, 2, g, :] = Ml[128:144, 0:128].T
        pe[:, 3, g, :] = Ml[240:256, 256:384].T
    return pd, pe


def _rope_tables(reverse):
    j = np.arange(1536)
    t = (2047 - j) if reverse else j
    row = (t // 64).astype(np.float32)
    col = (t % 64).astype(np.float32)
    inv = (10000.0 ** (-(np.arange(0, 32, 2, dtype=np.float32) / 32))).astype(np.float32)
    cos = np.zeros((128, 1536), np.float32)
    sin = np.zeros((128, 1536), np.float32)
    for p in range(128):
        d = p % 64
        pos = row if d < 32 else col
        ang = (pos * inv[d % 16]).astype(np.float32)
        cos[p] = np.cos(ang)
        sin[p] = np.sin(ang)
    return cos, sin


def _perm_matrix():
    pm = np.zeros((128, 128), np.float32)
    for jx in range(128):
        if jx % 32 < 16:
            pm[jx + 16, jx] = -1.0
        else:
            pm[jx - 16, jx] = 1.0
    return pm


def _masks():
    k = np.arange(128)[:, None]
    q = np.arange(128)[None, :]
    m = np.zeros((128, 2, 4, 128), np.float32)
    m[:, 0] = (k >= q)[:, None, :]
    m[:, 1] = (k <= q)[:, None, :]
    return m


def build_program():
    nc = bass.Bass("TRN2", target_bir_lowering=False)

    def din(name, shape, dt=F32):
        return nc.dram_tensor(name, list(shape), dt, kind="ExternalInput").ap()

    def dout(name, shape, dt=F32):
        return nc.dram_tensor(name, list(shape), dt, kind="ExternalOutput").ap()

    xp_d = din("xp", [512, D])
    xs_d = din("xs", [1536, D])
    ck_d = din("ck", [DEPTH, 256, 128])
    cv_d = din("cv", [DEPTH, 256, 128])
    wada_d = din("w_ada", [DEPTH, D, 3 * D])
    win_d = din("w_in", [DEPTH, D, 2304])
    wout_d = din("w_out", [DEPTH, D, D])
    wpool_d = din("w_pool", [DEPTH, 4, 128, 128])
    vecs_d = din("vecs", [256, 128])
    sink_d = din("sinkv", [1, 32])
    fn_d = din("fnorm", [1, D])
    identf_d = din("ident_f", [128, 128])
    identb_d = din("ident_b", [128, 128], BF16)
    perm_d = din("perm", [128, 128], BF16)
    cos_d = din("cosT", [128, 1536], BF16)
    sin_d = din("sinT", [128, 1536], BF16)
    mask_d = din("mask2", [128, 2, 128], BF16)
    pmd_d = din("pm_diag", [128, 4, 4, 128], BF16)
    pme_d = din("pm_edge", [128, 4, 4, 16], BF16)

    yp_d = dout("yp", [512, D])
    ys_d = dout("ys", [1024, D])
    nk_d = dout("nk", [DEPTH, 512, 128])
    nv_d = dout("nv", [DEPTH, 512, 128])
    if DEBUG_DUMP:
        dbg_d = dout("dbg", [DEPTH, 128, 8, NTOK])

    es = ExitStack()

    def sb(name, shape, dt=F32):
        return es.enter_context(nc.sbuf_tensor(name, list(shape), dt))

    psum = es.enter_context(nc.psum_tensor("psum", [128, 4096], F32))

    def bank(b, n=512, off=0):
        return psum[:, b * 512 + off: b * 512 + off + n]

    xT = sb("xT", [128, 8, NTOK])
    hT = sb("hT", [128, 8, NTOK], BF16)
    qT = sb("qT", [128, 4, NTOK], BF16)
    zT = sb("zT", [128, 4, NTOK], BF16)
    kT = sb("kT", [128, KT_COLS], BF16)
    NVA = 64 + 128 * 36
    vaug = sb("vaug", [128, NVA], BF16)
    ring = sb("ring", [128, NRING, 8, 512], BF16)
    AR = sb("arena", [128, 8192], BF16)
    zf = AR[:, 0:4096].bitcast(F32).rearrange("p (g t) -> p g t", g=4)
    stage = AR[:, 0:4096].bitcast(F32).rearrange("p (s t) -> p s t", s=2)
    pooledT = AR[:, 4096:6144].rearrange("p (g t) -> p g t", g=4)
    utm = AR[:, 6144:8192].rearrange("p (s t) -> p s t", s=4)
    PT = AR[:, 0:3072].rearrange("p (s t) -> p s t", s=3)
    atf = AR[:, 3072:5120].bitcast(F32).rearrange("p (a c t) -> p a c t", a=2, c=4)
    lnt = AR[:, 5120:7168].bitcast(F32).rearrange("p (a t) -> p a t", a=2)
    sqa = AR[:, 7168:7680].rearrange("p (c t) -> p c t", c=4)
    rb2 = AR[:, 7680:7936].bitcast(F32)
    vec_in = AR[:, 7168:7680].bitcast(F32).rearrange("p (s t) -> p s t", s=2)
    sq = sb("sq", [128, 4, 512], BF16)
    rb = sb("rb", [128, 512])
    tmpf = sb("tmpf", [128, 3, 512])
    q16 = sb("q16", [128, 2, 512], BF16)
    kvf = sb("kvf", [128, 256])
    ckt = sb("ckt", [128, 2, 128], BF16)
    cosT = sb("cosT_sb", [128, 1536], BF16)
    sinT = sb("sinT_sb", [128, 1536], BF16)
    mask2 = sb("mask2_sb", [128, 2, 128], BF16)
    pmd = sb("pmd_sb", [128, 4, 4, 128], BF16)
    pme = sb("pme_sb", [128, 4, 4, 16], BF16)
    wp = sb("wp_sb", [128, 2, 4, 128], BF16)
    identf = sb("identf_sb", [128, 128])
    identb = sb("identb_sb", [128, 128], BF16)
    onesb = sb("onesb", [128, 128], BF16)
    perm = sb("perm_sb", [128, 128], BF16)
    vec = sb("vec", [128, 256])
    esink = sb("esink", [128, 32])
    sT = sb("sT", [128, 8, 2], BF16)
    mods = sb("mods", [128, 2, 24, 2])
    gmul = sb("gmul", [128, 2, 8, 2])
    ssq = sb("ssq", [128, 8])
    selr = sb("selr", [1, 128], BF16)
    esb = sb("esb", [128, 2, 512], BF16)
    fnb = hT[:, 0, :].bitcast(F32)

    P = Prog(nc)
    cnt = {"dma": 0, "ring": 0, "ps_norm": 0, "evac": 0, "rope": 0}

    def MM(out, lhsT, rhs, start, stop, reads, writes, accum=False):
        return P.op("pe", lambda e: e.matmul(out, lhsT=lhsT, rhs=rhs, start=start, stop=stop),
                    reads=reads, writes=writes, accum=accum)

    def TR(out, in_, ident, reads, writes):
        return P.op("pe", lambda e: e.transpose(out=out, in_=in_, identity=ident), reads=reads, writes=writes)

    def ACT(out, in_, func, reads, writes, bias=None, scale=None, accum_out=None):
        kw = {}
        if bias is not None:
            kw["bias"] = bias
        if scale is not None:
            kw["scale"] = scale
        if accum_out is not None:
            kw["accum_out"] = accum_out
        return P.op("act", lambda e: e.activation(out=out, in_=in_, func=func, **kw), reads=reads, writes=writes)

    def ACOPY(out, in_, reads, writes):
        return P.op("act", lambda e: e.copy(out=out, in_=in_), reads=reads, writes=writes)

    def VCOPY(out, in_, reads, writes):
        return P.op("dve", lambda e: e.tensor_copy(out=out, in_=in_), reads=reads, writes=writes)

    def TT(out, in0, in1, op, reads, writes):
        return P.op("dve", lambda e: e.tensor_tensor(out=out, in0=in0, in1=in1, op=op), reads=reads, writes=writes)

    def STT(out, in0, scalar, in1, op0, op1, reads, writes):
        return P.op("dve", lambda e: e.scalar_tensor_tensor(out=out, in0=in0, scalar=scalar, in1=in1, op0=op0, op1=op1),
                    reads=reads, writes=writes)

    def TS(out, in0, scalar1, op0, reads, writes):
        return P.op("dve", lambda e: e.tensor_scalar(out=out, in0=in0, scalar1=scalar1, scalar2=None, op0=op0),
                    reads=reads, writes=writes)

    def DMA(eng, out, in_, reads, writes, sem):
        return P.op(eng, lambda e: e.dma_start(out=out, in_=in_), reads=reads, writes=writes, dma_sem=sem)

    def newsem(prefix):
        cnt["dma"] += 1
        return "%s%d" % (prefix, cnt["dma"])

    def c_bada(l):
        return vec[:, l * 24:(l + 1) * 24]

    def c_normw(l):
        return vec[:, 96 + l * 8: 96 + l * 8 + 8]

    def c_an(l, j):
        return vec[:, 128 + l * 4 + j: 128 + l * 4 + j + 1]

    def c_pn(l, j):
        return vec[:, 144 + l * 4 + j: 144 + l * 4 + j + 1]

    def c_psc(l, j):
        return vec[:, 160 + l * 4 + j: 160 + l * 4 + j + 1]

    def kx(cs, bs):
        return [("x", c, b) for c in cs for b in bs]

    def kh(cs, bs):
        return [("h", c, b) for c in cs for b in bs]

    def kq(cs, bs):
        return [("q", c, b) for c in cs for b in bs]

    def kz(cs, bs):
        return [("z", c, b) for c in cs for b in bs]

    def kk(bs):
        return [("k", b) for b in bs]

    def kv(bs):
        return [("v", b) for b in bs]

    def tok(b0, b1):
        return slice(b0 * 128, b1 * 128)

    DMA("sp", identf[:], identf_d, [], ["identf"], newsem("c"))
    DMA("sp", identb[:], identb_d, [], ["identb"], newsem("c"))
    DMA("sp", perm[:], perm_d, [], ["perm"], newsem("c"))
    DMA("sp", vec_in[:, 0, :], vecs_d[0:128, :], [], ["vec_in0"], newsem("c"))
    DMA("sp", vec_in[:, 1, :], vecs_d[128:256, :], [], ["vec_in1"], newsem("c"))
    DMA("sp", esink[:], sink_d.broadcast_to([128, 32]), [], ["esink"], newsem("c"))
    DMA("sp", cosT[:], cos_d, [], ["cos"], newsem("c"))
    DMA("sp", sinT[:], sin_d, [], ["sin"], newsem("c"))
    DMA("sp", mask2[:], mask_d, [], ["mask2"], newsem("c"))
    DMA("sp", pmd[:], pmd_d, [], ["pmd"], newsem("c"))
    DMA("sp", pme[:], pme_d, [], ["pme"], newsem("c"))
    P.op("dve", lambda e: e.memset(onesb[:], 1.0), writes=["onesb"])
    P.op("dve", lambda e: e.memset(selr[:, 0:64], 0.0), writes=["selr"])
    P.op("dve", lambda e: e.memset(selr[:, 64:128], 1.0), writes=["selr"])
    P.op("dve", lambda e: e.memset(vaug[:], 1.0), writes=kv(range(18)))

    for i in range(2):
        TR(bank(7, 128, i * 128), vec_in[:, i, :], identf[:], ["vec_in%d" % i, "identf"], [("ps", 7)])
    ACOPY(vec[:], bank(7, 256), [("ps", 7)], ["vec"])
    ACT(sT[:, :, 0], vec[:, 176:184], AF.Silu, ["vec"], ["sT"])
    ACT(sT[:, :, 1], vec[:, 184:192], AF.Silu, ["vec"], ["sT"])
    ACT(esink[:], esink[:], AF.Exp, ["esink"], ["esink"])

    def ring_load(src_ap, ncols, slot):
        src = src_ap.rearrange("(k p) c -> p k c", p=128)
        DMA("pool", ring[:, slot, :, 0:ncols], src, [], [("ring", slot)], "ring%d" % slot)
        return slot

    MODB = 3

    def emit_ada_granule(l, gi, slot):
        ring_load(wada_d[l][:, gi * 512:(gi + 1) * 512], 512, slot)
        pm = l % 2
        for jc in range(4):
            for k in range(8):
                MM(bank(7, 2, jc * 2), ring[:, slot, k, jc * 128:(jc + 1) * 128], sT[:, k, :], k == 0, k == 7,
                   [("ring", slot), "sT"], [("ps", 7)], accum=(k > 0))
        pv = bank(7, 8).rearrange("p (j v) -> p j v", v=2)
        for v in range(2):
            TT(mods[:, pm, gi * 4:(gi + 1) * 4, v], pv[:, :, v], vec[:, l * 24 + gi * 4: l * 24 + gi * 4 + 4], ALU.add,
               [("ps", 7), "vec"], [("mods", pm)])

    def emit_mods_finish(l):
        pm = l % 2
        for v in range(2):
            STT(gmul[:, pm, :, v], mods[:, pm, 8:16, v], 1.0, c_normw(l), ALU.add, ALU.mult,
                [("mods", pm), "vec"], [("gmul", pm)])

    def load_x_block(b):
        st = b % 2
        src = xp_d[b * 128:(b + 1) * 128, :] if b < 4 else xs_d[(b - 4) * 128:(b - 3) * 128, :]
        DMA("sp", stage[:, st, :], src, [], [("stage", st)], "xin%d" % st)
        pb = (b % 2) * 2
        for c in range(8):
            TR(bank(pb + c // 4, 128, (c % 4) * 128), stage[:, st, c * 128:(c + 1) * 128], identf[:],
               [("stage", st), "identf"], [("ps", pb + c // 4)])
        src_ps = psum[:, pb * 512: pb * 512 + 1024].rearrange("p (c t) -> p c t", c=8)
        if b % 2 == 0:
            ACOPY(xT[:, :, tok(b, b + 1)], src_ps, [("ps", pb), ("ps", pb + 1)], kx(range(8), [b]))
        else:
            VCOPY(xT[:, :, tok(b, b + 1)], src_ps, [("ps", pb), ("ps", pb + 1)], kx(range(8), [b]))

    xb = 0
    for gi in range(6):
        emit_ada_granule(0, gi, gi % 2)
        for _ in range(3 if gi < 4 else 2):
            if xb < NBLK:
                load_x_block(xb)
                xb += 1
    while xb < NBLK:
        load_x_block(xb)
        xb += 1
    emit_mods_finish(0)

    def units_of(nsamp):
        us = [(0, 4)]
        b = 4
        while b < 4 + nsamp:
            us.append((b, min(b + 4, 4 + nsamp)))
            b += 4
        return us

    def rsqrt_from_stats(ps_ap, dst, scale_div, rd_keys, key):
        ACT(dst, ps_ap, AF.Ln, rd_keys, [key], bias=EPS, scale=1.0 / scale_div)
        ACT(dst, dst, AF.Exp, [key], [key], scale=-0.5)

    def rope_evac(ps_b, n, lt0, dst_ap, dst_keys):
        i = cnt["rope"] % 2
        rbk = 4 + (cnt["rope"] % 4)
        cnt["rope"] += 1
        ACOPY(q16[:, i, 0:n], bank(ps_b, n), [("ps", ps_b)], [("q16", i)])
        MM(bank(rbk, n), perm[:], q16[:, i, 0:n], True, True, [("q16", i), "perm"], [("ps", rbk)])
        TT(tmpf[:, i, 0:n], bank(ps_b, n), cosT[:, lt0:lt0 + n], ALU.mult, [("ps", ps_b), "cos"], [("tmpf", i)])
        TT(tmpf[:, 2, 0:n], bank(rbk, n), sinT[:, lt0:lt0 + n], ALU.mult, [("ps", rbk), "sin"], [("tmpf", 2)])
        TT(dst_ap, tmpf[:, i, 0:n], tmpf[:, 2, 0:n], ALU.add, [("tmpf", i), ("tmpf", 2)], dst_keys)

    def f_slab(slot, col0, units, kchunks_fn, banks, evac_fn):
        for ui, (b0, b1) in enumerate(units):
            n = (b1 - b0) * 128
            pb = banks[ui]
            for k in range(8):
                rhs_ap, rkeys = kchunks_fn(k, b0, b1)
                MM(bank(pb, n), ring[:, slot, k, col0:col0 + 128], rhs_ap, k == 0, k == 7,
                   [("ring", slot)] + rkeys, [("ps", pb)], accum=(k > 0))
        for ui, (b0, b1) in enumerate(units):
            evac_fn(ui, b0, b1, banks[ui])

    def h_chunk(k, b0, b1):
        return hT[:, k, tok(b0, b1)], kh([k], range(b0, b1))

    def a_chunk(k, b0, b1):
        if k < 4:
            return qT[:, k, tok(b0, b1)], kq([k], range(b0, b1))
        return zT[:, k - 4, tok(b0, b1)], kz([k - 4], range(b0, b1))

    def vsel(b0):
        return 0 if b0 < 4 else 1

    out_ops = []

    def emit_layer(l):
        pm = l % 2
        ni = 12 - l
        nq = 11 - l
        units_in = units_of(ni)
        units_q = units_of(nq)
        blocks_in = list(range(4)) + list(range(4, 4 + ni))
        blocks_q = list(range(4)) + list(range(4, 4 + nq))

        DMA("pool", wp[:, pm], wpool_d[l].rearrange("g c d -> c g d"), [], [("wp", pm)], "wp%d" % pm)

        stage_gate(3 + 10 * l)
        def norm_A(b0, b1, pb):
            n = (b1 - b0) * 128
            for hf in range(2):
                ACT(sq[:, :, 0:n], xT[:, hf * 4:hf * 4 + 4, tok(b0, b1)], AF.Square,
                    kx(range(hf * 4, hf * 4 + 4), range(b0, b1)), ["sq"])
                for c4 in range(4):
                    c = hf * 4 + c4
                    MM(bank(pb, n), onesb[:], sq[:, c4, 0:n], c == 0, c == 7, ["sq", "onesb"], [("ps", pb)], accum=(c > 0))

        def norm_B(b0, b1, pb):
            n = (b1 - b0) * 128
            v = vsel(b0)
            rsqrt_from_stats(bank(pb, n), rb[:, 0:n], float(D), [("ps", pb)], "rb")
            for c in range(8):
                i = c % 2
                STT(tmpf[:, i, 0:n], xT[:, c, tok(b0, b1)], gmul[:, pm, c, v:v + 1], rb[:, 0:n], ALU.mult, ALU.mult,
                    kx([c], range(b0, b1)) + [("gmul", pm), "rb"], [("tmpf", i)])
                if c % 2 == 0:
                    ACT(hT[:, c, tok(b0, b1)], tmpf[:, i, 0:n], AF.Identity, [("tmpf", i), ("mods", pm)],
                        kh([c], range(b0, b1)), bias=mods[:, pm, c, v:v + 1], scale=1.0)
                else:
                    TS(hT[:, c, tok(b0, b1)], tmpf[:, i, 0:n], mods[:, pm, c, v:v + 1], ALU.add,
                       [("tmpf", i), ("mods", pm)], kh([c], range(b0, b1)))

        norm_sched = {}
        nU = len(units_in)

        def nA(u):
            norm_A(units_in[u][0], units_in[u][1], 6 + u % 2)

        def nB(u):
            norm_B(units_in[u][0], units_in[u][1], 6 + u % 2)

        nA(0)
        if nU > 1:
            nA(1)
        nB(0)
        for u in range(nU):
            lst = []
            if u + 2 < nU:
                lst.append(lambda u=u: nA(u + 2))
            if u + 1 < nU:
                lst.append(lambda u=u: nB(u + 1))
            norm_sched[units_in[u][0]] = lst

        stage_gate(4 + 10 * l)
        slot_u = ring_load(win_d[l][:, 1280:1792], 512, 0)
        targets = set(blocks_q)
        pooled_in_unit = {}

        def unit_index_q(tb):
            for ui, (b0, b1) in enumerate(units_q):
                if b0 <= tb < b1:
                    return ui
            return None

        pend = []

        def pool_stage2a(ui):
            b0, b1 = units_q[ui]
            n = (b1 - b0) * 128
            pk = [("pooledT", i) for i in range(b1 - b0)]
            for g in range(4):
                bk = 4 + g % 2
                MM(bank(bk, n), wp[:, pm, g, :], pooledT[:, g, 0:n], True, True, [("wp", pm)] + pk, [("ps", bk)])
                TS(zf[:, g, 0:n], bank(bk, n), c_psc(l, g), ALU.mult, [("ps", bk), "vec"], [("zf", g)])
            pend.append([1, lambda: pool_stage2b(ui)])

        def pool_stage2b(ui):
            b0, b1 = units_q[ui]
            n = (b1 - b0) * 128
            TT(sq[:, :, 0:n], zf[:, :, 0:n], zf[:, :, 0:n], ALU.mult, [("zf", g) for g in range(4)], ["sq"])
            for g in range(4):
                MM(bank(4, n), onesb[:], sq[:, g, 0:n], g == 0, g == 3, ["sq", "onesb"], [("ps", 4)], accum=(g > 0))
            rsqrt_from_stats(bank(4, n), rb[:, 0:n], 512.0, [("ps", 4)], "rb")
            TT(zT[:, :, tok(b0, b1)], zf[:, :, 0:n], rb[:, 0:n].unsqueeze(1).broadcast_to([128, 4, n]), ALU.mult,
               [("zf", g) for g in range(4)] + ["rb"], kz(range(4), range(b0, b1)))

        pcount = {"i": 0}

        def pool_target(tb):
            ui = unit_index_q(tb)
            b0, b1 = units_q[ui]
            if tb < 4:
                dtype_i = 0 if tb % 2 == 0 else 1
                prev_b = tb - 1 if tb % 2 == 1 else None
                next_b = tb + 1 if tb % 2 == 0 else None
                et_prev, et_next = 0, 1
            else:
                dtype_i = 2 if tb == 4 else 3
                prev_b = tb - 1 if tb > 4 else None
                next_b = tb + 1
                et_prev, et_next = 2, 3
            PB = 2 + pcount["i"] % 2
            pcount["i"] += 1
            for g in range(4):
                gs = slice(g * 128, (g + 1) * 128)
                MM(bank(PB, 128, g * 128), utm[:, tb % 4, gs], pmd[:, dtype_i, g, :], True,
                   (prev_b is None and next_b is None), [("utm", tb % 4), "pmd"], [("ps", PB)], accum=(g > 0))
                if prev_b is not None:
                    MM(bank(PB, 16, g * 128), utm[:, prev_b % 4, gs], pme[:, et_prev, g, :], False, next_b is None,
                       [("utm", prev_b % 4), "pme"], [("ps", PB)], accum=True)
                if next_b is not None:
                    MM(bank(PB, 16, g * 128 + 112), utm[:, next_b % 4, gs], pme[:, et_next, g, :], False, True,
                       [("utm", next_b % 4), "pme"], [("ps", PB)], accum=True)
            off = (tb - b0) * 128
            VCOPY(pooledT[:, :, off:off + 128], bank(PB).rearrange("p (g t) -> p g t", g=4), [("ps", PB)],
                  [("pooledT", tb - b0)])
            pooled_in_unit[ui] = pooled_in_unit.get(ui, 0) + 1
            if pooled_in_unit[ui] == b1 - b0:
                pend.append([1, lambda: pool_stage2a(ui)])

        def run_pending(flush=False):
            while True:
                due = [p for p in pend if p[0] <= 0 or flush]
                if not due:
                    break
                p = due[0]
                pend.remove(p)
                p[1]()
            for p in pend:
                p[0] -= 1

        for bi, b in enumerate(blocks_in):
            pb = bi % 2
            for k in range(8):
                MM(bank(pb), hT[:, k, tok(b, b + 1)], ring[:, slot_u, k, :], k == 0, k == 7,
                   [("ring", slot_u)] + kh([k], [b]), [("ps", pb)], accum=(k > 0))
            ACOPY(utm[:, b % 4, :], bank(pb), [("ps", pb)], [("utm", b % 4)])
            for fn in norm_sched.get(b, []):
                fn()
            run_pending()
            if b < 4:
                if b % 2 == 1:
                    pend.append([0, lambda b=b: (pool_target(b - 1), pool_target(b))])
            elif b - 1 >= 4 and (b - 1) in targets:
                pend.append([0, lambda b=b: pool_target(b - 1)])
        stage_gate(5.1 + 10 * l)
        slot_kv = ring_load(win_d[l][:, 512:768], 256, 1)
        DMA("pool", ckt[:], ck_d[l].rearrange("(b p) f -> p b f", p=128), [], ["ckt"], "ckt")
        for cb in range(2):
            e0 = 2 * (16 + cb)
            dstv = vaug[:, 64 + 128 * e0: 64 + 128 * e0 + 256].rearrange("p (g x) -> p g x", g=2)[:, :, 0:64]
            srcv = cv_d[l][cb * 128:(cb + 1) * 128, :].rearrange("p (g d) -> p g d", g=2)
            DMA("pool", dstv, srcv, [], kv([16 + cb]), "cv%d" % cb)

        def kv_block(bi, b):
            pb = bi % 2
            if b < 4:
                for k in range(8):
                    MM(bank(pb, 256), hT[:, k, tok(b, b + 1)], ring[:, slot_kv, k, 0:256], k == 0, k == 7,
                       [("ring", slot_kv)] + kh([k], [b]), [("ps", pb)], accum=(k > 0))
                if b % 2 == 0:
                    kbuf, kkey = kvf[:], "kvf"
                else:
                    kbuf, kkey = tmpf[:, 2, 0:256], ("tmpf", 2)
                ACOPY(kbuf, bank(pb, 256), [("ps", pb)], [kkey])
                out_ops.append(DMA("sp", nk_d[l][b * 128:(b + 1) * 128, :], kbuf[:, 0:128], [kkey], [], "okv%d" % (b % 2)))
                out_ops.append(DMA("sp", nv_d[l][b * 128:(b + 1) * 128, :], kbuf[:, 128:256], [kkey], [], "okv%d" % (b % 2)))
                vsrc = bank(pb, 128, 128).rearrange("p (g x) -> p g x", g=2)
            else:
                for k in range(8):
                    MM(bank(pb, 128), hT[:, k, tok(b, b + 1)], ring[:, slot_kv, k, 128:256], k == 0, k == 7,
                       [("ring", slot_kv)] + kh([k], [b]), [("ps", pb)], accum=(k > 0))
                vsrc = bank(pb, 128).rearrange("p (g x) -> p g x", g=2)
            e0 = 2 * b
            dstv = vaug[:, 64 + 128 * e0: 64 + 128 * e0 + 256].rearrange("p (g x) -> p g x", g=2)[:, :, 0:64]
            VCOPY(dstv, vsrc, [("ps", pb)], kv([b]))

        for bi, b in enumerate(blocks_in[:4]):
            kv_block(bi, b)
        run_pending(flush=True)
        ctb = bank(7, 128).bitcast(BF16)
        for cb in range(2):
            TR(ctb[:, cb * 128:(cb + 1) * 128], ckt[:, cb, :], identb[:], ["ckt", "identb"], [("ps", 7)])
        ACOPY(kT[:, NTOK:NTOK + 256], ctb, [("ps", 7)], kk([16, 17]))
        for bi, b in enumerate(blocks_in):
            if bi >= 4:
                kv_block(bi, b)

        stage_gate(5.4 + 10 * l)

        def evac_k(ui, b0, b1, pb):
            n = (b1 - b0) * 128
            if b0 < 4:
                ACOPY(kT[:, tok(b0, b1)], bank(pb, n), [("ps", pb)], kk(range(b0, b1)))
            else:
                rope_evac(pb, n, (b0 - 4) * 128, kT[:, tok(b0, b1)], kk(range(b0, b1)))
        f_slab(slot_kv, 0, units_in, h_chunk, [0, 1, 2, 3], evac_k)

        stage_gate(5 + 10 * l)
        slot_gp = ring_load(win_d[l][:, 1792:2304], 512, 0)
        for j in range(4):
            banks = [0, 1, 2, 3] if j % 2 == 0 else [4, 5, 6, 7]

            def evac_gp(ui, b0, b1, pb, j=j):
                n = (b1 - b0) * 128
                i = cnt["evac"] % 2
                cnt["evac"] += 1
                ACT(q16[:, i, 0:n], bank(pb, n), AF.Silu, [("ps", pb)], [("q16", i)])
                STT(zT[:, j, tok(b0, b1)], zT[:, j, tok(b0, b1)], c_pn(l, j), q16[:, i, 0:n], ALU.mult, ALU.mult,
                    kz([j], range(b0, b1)) + [("q16", i), "vec"], kz([j], range(b0, b1)))
            f_slab(slot_gp, j * 128, units_q, h_chunk, banks, evac_gp)

        stage_gate(7 + 10 * l)
        slot_q = ring_load(win_d[l][:, 0:512], 512, 1)
        for c in range(4):
            def evac_q(ui, b0, b1, pb, c=c):
                n = (b1 - b0) * 128
                if b0 < 4:
                    ACOPY(qT[:, c, tok(b0, b1)], bank(pb, n), [("ps", pb)], kq([c], range(b0, b1)))
                else:
                    rope_evac(pb, n, (b0 - 4) * 128, qT[:, c, tok(b0, b1)], kq([c], range(b0, b1)))
            f_slab(slot_q, c * 128, units_q, h_chunk, [0, 1, 2, 3], evac_q)
            if c == 1 and l + 1 < DEPTH:
                emit_ada_granule(l + 1, 0, 0)

        stage_gate(8 + 10 * l)
        VCOPY(esb[:].rearrange("p g (h q) -> p (g h) q", h=4), esink[:, l * 8:(l + 1) * 8].unsqueeze(2).broadcast_to([128, 8, 128]),
              ["esink"], ["esb"])
        ada_pending = [1] if l + 1 < DEPTH else []
        ptasks = []
        for qi, qb in enumerate(blocks_q):
            if qb < 4:
                s0 = (qb // 2) * 2
                keys = [(s0, None), (s0 + 1, None)]
            else:
                keys = []
                if qb > 4:
                    keys.append((qb - 1, 0))
                keys.append((qb, None))
                keys.append((qb + 1, 1))
                keys += [(16, None), (17, None)]
            ptasks.append(dict(qi=qi, qb=qb, keys=keys, ob=((4, 5) if qi % 2 == 0 else (6, 7)), ab=qi % 2))
        psteps = [(ti, ki) for ti, t in enumerate(ptasks) for ki in range(len(t["keys"]))]
        last_ps = {}
        for si_, (ti_, ki_) in enumerate(psteps):
            last_ps[ti_] = si_
        deferred = []

        def emit_qk(si):
            ti, ki = psteps[si]
            t = ptasks[ti]
            qb = t["qb"]
            kb, mtype = t["keys"][ki]
            sp = si % 2
            pt = si % 3
            for g in range(2):
                lo, hi = 64 * g, 64 * g + 64
                MM(bank(2 * sp + g), kT[lo:hi, kb * 128:(kb + 1) * 128], qT[lo:hi, :, tok(qb, qb + 1)], True, True,
                   kk([kb]) + kq(range(4), [qb]), [("ps", 2 * sp + g)])
            ACT(PT[:, pt, :], psum[:, 2 * sp * 512:(2 * sp + 2) * 512], AF.Exp, [("ps", 2 * sp), ("ps", 2 * sp + 1)],
                [("PT", pt)], scale=0.125)
            if mtype is not None:
                ptv = PT[:, pt, :].rearrange("p (h q) -> p h q", h=8)
                TT(ptv, ptv, mask2[:, mtype, :].unsqueeze(1).broadcast_to([128, 8, 128]), ALU.mult,
                   [("PT", pt), "mask2"], [("PT", pt)])

        def emit_unit_rmsnorm(ub0, ub1):
            n = (ub1 - ub0) * 128
            uk = kq(range(4), range(ub0, ub1))
            TT(sq[:, :, 0:n], qT[:, :, tok(ub0, ub1)], qT[:, :, tok(ub0, ub1)], ALU.mult, uk, ["sq"])
            for c in range(4):
                MM(bank(3, n), onesb[:], sq[:, c, 0:n], c == 0, c == 3, ["sq", "onesb"], [("ps", 3)], accum=(c > 0))
            rsqrt_from_stats(bank(3, n), rb[:, 0:n], 512.0, [("ps", 3)], "rb")
            TT(qT[:, :, tok(ub0, ub1)], qT[:, :, tok(ub0, ub1)], rb[:, 0:n].unsqueeze(1).broadcast_to([128, 4, n]), ALU.mult,
               uk + ["rb"], uk)

        def emit_pv(si):
            ti, ki = psteps[si]
            t = ptasks[ti]
            qb, ab = t["qb"], t["ab"]
            kb, mtype = t["keys"][ki]
            pt = si % 3
            first, last = (ki == 0), (ki == len(t["keys"]) - 1)
            for g in range(2):
                ob = t["ob"][g]
                en = 2 * kb + g
                MM(bank(ob), vaug[:, 64 + 128 * en: 64 + 128 * en + 128], PT[:, pt, g * 512:(g + 1) * 512], first, last,
                   kv([kb]) + [("PT", pt)], [("ps", ob)], accum=(not first))
            if not last:
                return
            for g in range(2):
                ob = t["ob"][g]
                TT(lnt[64:128, g, :], bank(ob)[64:128, :], esb[64:128, g, :], ALU.add, [("ps", ob), "esb"], [("lnt", g)])

            def finish2(t=t, qb=qb):
                ACT(lnt[64:128, :, :], lnt[64:128, :, :], AF.Ln, [("lnt", 0), ("lnt", 1)], [("lnt", 0), ("lnt", 1)])
                ACT(lnt[64:128, :, :], lnt[64:128, :, :], AF.Exp, [("lnt", 0), ("lnt", 1)], [("lnt", 0), ("lnt", 1)], scale=-1.0)
                for g in range(2):
                    ob = t["ob"][g]
                    TT(qT[0:64, 2 * g:2 * g + 2, tok(qb, qb + 1)], bank(ob, 256)[0:64, :].rearrange("p (j t) -> p j t", j=2),
                       lnt[64:128, g, 0:256].rearrange("p (j t) -> p j t", j=2), ALU.mult,
                       [("ps", ob), ("lnt", g)], kq([2 * g, 2 * g + 1], [qb]))
                    TT(qT[64:128, 2 * g:2 * g + 2, tok(qb, qb + 1)], bank(ob, 256, 256)[0:64, :].rearrange("p (j t) -> p j t", j=2),
                       lnt[64:128, g, 256:512].rearrange("p (j t) -> p j t", j=2), ALU.mult,
                       [("ps", ob), ("lnt", g)], kq([2 * g, 2 * g + 1], [qb]))
            deferred.append((si + 1, finish2))
            for (ub0, ub1) in units_q:
                if qb == ub1 - 1:
                    deferred.append((si + 3, lambda ub0=ub0, ub1=ub1: emit_unit_rmsnorm(ub0, ub1)))
            if ada_pending and t["qi"] >= 3:
                gi = ada_pending.pop(0)
                deferred.append((si + 2, lambda gi=gi: emit_ada_granule(l + 1, gi, 1)))

        nst = len(psteps)
        PIPE = 2
        for si in range(nst + PIPE):
            if si < nst:
                emit_qk(si)
            if si >= PIPE:
                cur = si - PIPE
                emit_pv(cur)
                for d in [d for d in deferred if d[0] <= cur]:
                    deferred.remove(d)
                    d[1]()
        while ada_pending:
            emit_ada_granule(l + 1, ada_pending.pop(0), 1)

        stage_gate(9 + 10 * l)
        slot_ga = ring_load(win_d[l][:, 768:1280], 512, 0)

        def ga_mm(j, ulist, banks):
            for ui, (b0, b1) in ulist:
                n = (b1 - b0) * 128
                pb = banks[ui]
                for k in range(8):
                    MM(bank(pb, n), ring[:, slot_ga, k, j * 128:(j + 1) * 128], hT[:, k, tok(b0, b1)], k == 0, k == 7,
                       [("ring", slot_ga)] + kh([k], range(b0, b1)), [("ps", pb)], accum=(k > 0))

        def evac_ga(ui, b0, b1, pb, j):
            n = (b1 - b0) * 128
            i = cnt["evac"] % 2
            cnt["evac"] += 1
            ACT(q16[:, i, 0:n], bank(pb, n), AF.Silu, [("ps", pb)], [("q16", i)])
            STT(qT[:, j, tok(b0, b1)], qT[:, j, tok(b0, b1)], c_an(l, j), q16[:, i, 0:n], ALU.mult, ALU.mult,
                kq([j], range(b0, b1)) + [("q16", i), "vec"], kq([j], range(b0, b1)))

        ulist = list(enumerate(units_q))
        ga_mm(0, ulist[:3], [0, 1, 2, 3])
        for d in list(deferred):
            d[1]()
        deferred.clear()
        if l + 1 < DEPTH:
            emit_ada_granule(l + 1, 2, 1)
        ga_mm(0, ulist[3:], [0, 1, 2, 3])
        for ui, (b0, b1) in ulist:
            evac_ga(ui, b0, b1, [0, 1, 2, 3][ui], 0)
        for j in range(1, 4):
            banks = [0, 1, 2, 3] if j % 2 == 0 else [4, 5, 6, 7]
            f_slab(slot_ga, j * 128, units_q, h_chunk, banks, lambda ui, b0, b1, pb, j=j: evac_ga(ui, b0, b1, pb, j))

        stage_gate(10 + 10 * l)
        for half in range(2):
            slot_o = ring_load(wout_d[l][:, half * 512:(half + 1) * 512], 512, 1 - half)
            for cc in range(4):
                c = half * 4 + cc
                banks = [0, 1, 2, 3] if cc % 2 == 0 else [4, 5, 6, 7]
                if l + 1 < DEPTH and cc == 2:
                    emit_ada_granule(l + 1, 3 + half, half)

                def evac_o(ui, b0, b1, pb, c=c):
                    n = (b1 - b0) * 128
                    v = vsel(b0)
                    STT(xT[:, c, tok(b0, b1)], bank(pb, n), mods[:, pm, 16 + c, v:v + 1], xT[:, c, tok(b0, b1)],
                        ALU.mult, ALU.add, [("ps", pb), ("mods", pm)] + kx([c], range(b0, b1)), kx([c], range(b0, b1)))
                f_slab(slot_o, cc * 128, units_q, a_chunk, banks, evac_o)
        if l + 1 < DEPTH:
            emit_ada_granule(l + 1, 5, 1)
            emit_mods_finish(l + 1)

        if DEBUG_DUMP:
            out_ops.append(DMA("sp", dbg_d[l], xT[:], kx(range(8), range(NBLK)), [], "dbg"))

    def emit_epilogue():
        DMA("sp", fnb, fn_d.broadcast_to([128, D]), [], kh([0], range(NBLK)), "fnb")
        fnk = kh([0], range(NBLK))
        own = list(range(4)) + list(range(4, 12))
        stage4 = AR[:, 0:8192].bitcast(F32).rearrange("p (s t) -> p s t", s=4)
        for i, b in enumerate(own):
            pb = (i % 4) * 2
            st = i % 4
            for c in range(8):
                TR(bank(pb + c // 4, 128, (c % 4) * 128), xT[:, c, tok(b, b + 1)], identf[:], kx([c], [b]) + ["identf"],
                   [("ps", pb + c // 4)])
            for hb in range(2):
                ACT(stage4[:, st, hb * 512:(hb + 1) * 512], bank(pb + hb), AF.Square, [("ps", pb + hb)],
                    [("stage", st), ("ssq", st, hb)], accum_out=ssq[:, 2 * st + hb: 2 * st + hb + 1])
            s0 = ssq[:, 2 * st:2 * st + 1]
            TT(s0, s0, ssq[:, 2 * st + 1:2 * st + 2], ALU.add, [("ssq", st, 0), ("ssq", st, 1)], [("ssq", st, 0)])
            ACT(s0, s0, AF.Ln, [("ssq", st, 0)], [("ssq", st, 0)], bias=EPS, scale=1.0 / D)
            ACT(s0, s0, AF.Exp, [("ssq", st, 0)], [("ssq", st, 0)], scale=-0.5)
            for hb in range(2):
                STT(stage4[:, st, hb * 512:(hb + 1) * 512], bank(pb + hb), s0, fnb[:, hb * 512:(hb + 1) * 512], ALU.mult, ALU.mult,
                    [("ps", pb + hb), ("ssq", st, 0)] + fnk, [("stage", st)])
            dst = yp_d[b * 128:(b + 1) * 128, :] if b < 4 else ys_d[(b - 4) * 128:(b - 3) * 128, :]
            out_ops.append(DMA("sp", dst, stage4[:, st, :], [("stage", st)], [], "yout%d" % st))


    try:
        for l in range(DEPTH):
            emit_layer(l)
        stage_gate(50)
        emit_epilogue()
    except StopBuild:
        pass

    P.emit(final_wait_ops=out_ops)
    es.close()
    return nc


_CACHE = {}


def _prep_shared(inp):
    qp, ap = _qperm(), _aperm()
    w_in = np.asarray(inp["w_in"], np.float32)
    colperm = np.concatenate([qp, np.arange(512, 768), 768 + ap, np.arange(1280, 2304)])
    w_in_p = np.ascontiguousarray(w_in[:, :, colperm])
    w_out = np.asarray(inp["w_out"], np.float32)
    rowperm = np.concatenate([ap, np.arange(512, 1024)])
    w_out_p = np.ascontiguousarray(w_out[:, rowperm, :])
    attn_norm_p = np.asarray(inp["attn_norm"], np.float32)[:, ap]
    return w_in_p, w_out_p, attn_norm_p


def kernel(x_prompt, x_sample, cache_k, cache_v, c, c_ctx, norm_w, w_ada, b_ada, w_in, sink,
           attn_norm, pool_norm, w_pool, pool_scale, w_out, final_norm):
    f32 = np.float32
    inp = dict(w_in=w_in, w_out=w_out, attn_norm=attn_norm)
    w_in_p, w_out_p, attn_norm_p = _prep_shared(inp)
    x_prompt = np.asarray(x_prompt, f32)
    x_sample = np.asarray(x_sample, f32)
    cache_k = np.asarray(cache_k, f32)
    cache_v = np.asarray(cache_v, f32)
    c = np.asarray(c, f32)
    c_ctx = np.asarray(c_ctx, f32)
    w_ada = np.ascontiguousarray(np.asarray(w_ada, f32))
    w_pool = np.ascontiguousarray(np.asarray(w_pool, f32))
    b_ada = np.asarray(b_ada, f32)
    norm_w = np.asarray(norm_w, f32)
    pool_norm = np.asarray(pool_norm, f32)
    pool_scale = np.asarray(pool_scale, f32)
    sink = np.asarray(sink, f32)
    final_norm = np.asarray(final_norm, f32)

    bf = ml_dtypes.bfloat16
    ident = np.eye(128, dtype=f32)
    consts = {}
    for rev in (False, True):
        cos, sin = _rope_tables(rev)
        pd, pe = _pool_tables(rev)
        consts[rev] = dict(cosT=cos.astype(bf), sinT=sin.astype(bf), pm_diag=pd.astype(bf), pm_edge=pe.astype(bf))
    mask2 = np.ascontiguousarray(_masks()[:, :, 0, :]).astype(bf)
    permm = _perm_matrix().astype(bf)

    in_maps = []
    for i in range(NCORES):
        b, half = i // 2, i % 2
        rev = half == 1
        if not rev:
            xs = x_sample[b, 0:1536]
        else:
            xs = x_sample[b, ::-1][0:1536]
        vecs = np.zeros((256, 128), f32)
        vecs[0:96] = b_ada.reshape(DEPTH * 24, 128)
        vecs[96:128] = norm_w.reshape(DEPTH * 8, 128)
        vecs[128:144] = attn_norm_p.reshape(DEPTH * 4, 128)
        vecs[144:160] = pool_norm.reshape(DEPTH * 4, 128)
        vecs[160:176] = pool_scale.reshape(DEPTH * 4, 128)
        vecs[176:184] = c_ctx.reshape(8, 128)
        vecs[184:192] = c[b].reshape(8, 128)
        m = dict(
            xp=np.ascontiguousarray(x_prompt[2 * i:2 * i + 2].reshape(512, D)),
            xs=np.ascontiguousarray(xs),
            ck=np.ascontiguousarray(cache_k[b].reshape(DEPTH, 256, 128)),
            cv=np.ascontiguousarray(cache_v[b].reshape(DEPTH, 256, 128)),
            w_ada=w_ada, w_in=w_in_p, w_out=w_out_p, w_pool=w_pool,
            vecs=vecs, sinkv=np.ascontiguousarray(sink.reshape(1, 32)),
            fnorm=np.ascontiguousarray(final_norm.reshape(1, D)),
            ident_f=ident, ident_b=ident.astype(bf), perm=permm, mask2=mask2,
            **consts[rev],
        )
        in_maps.append(m)

    if "nc" not in _CACHE:
        _CACHE["nc"] = build_program()
    nc = _CACHE["nc"]
    res = run_bass_kernel_spmd(nc, in_maps, core_ids=list(range(NCORES)))
    outs = res.results

    y_prompt = np.zeros((16, 256, D), f32)
    y_sample = np.zeros((4, 2048, D), f32)
    new_k = np.zeros((16, DEPTH, 256, 2, 64), f32)
    new_v = np.zeros((16, DEPTH, 256, 2, 64), f32)
    for i in range(NCORES):
        b, half = i // 2, i % 2
        r = outs[i]
        y_prompt[2 * i:2 * i + 2] = np.asarray(r["yp"]).reshape(2, 256, D)
        ys = np.asarray(r["ys"])
        if half == 0:
            y_sample[b, 0:1024] = ys
        else:
            y_sample[b, 1024:2048] = ys[::-1]
        nk = np.asarray(r["nk"]).reshape(DEPTH, 2, 256, 2, 64)
        nv = np.asarray(r["nv"]).reshape(DEPTH, 2, 256, 2, 64)
        new_k[2 * i:2 * i + 2] = nk.transpose(1, 0, 2, 3, 4)
        new_v[2 * i:2 * i + 2] = nv.transpose(1, 0, 2, 3, 4)
    if DEBUG_DUMP:
        kernel.dbg = [np.asarray(o["dbg"]) for o in outs]
    return (y_prompt, y_sample, new_k, new_v)
```

```python
from contextlib import ExitStack
import numpy as np
import ml_dtypes
import concourse.bass as bass
import concourse.mybir as mybir
from concourse.bass_utils import run_bass_kernel_spmd

F32 = mybir.dt.float32
BF16 = mybir.dt.bfloat16
AF = mybir.ActivationFunctionType
ALU = mybir.AluOpType

D = 1024
DEPTH = 4
NCORES = 8
EPS = 1e-6
NTOK = 2048
NBLK = 16
KT_COLS = NTOK + 256
NRING = 2
DEBUG_DUMP = False
STOP = 99


class StopBuild(Exception):
    pass


def stage_gate(n):
    if n > STOP:
        raise StopBuild()

ENGS = ("pe", "act", "dve", "pool", "sp")


class Op:
    __slots__ = ("eng", "fn", "deps", "signal", "tick", "dma_sem", "dma_val")

    def __init__(self, eng, fn):
        self.eng = eng
        self.fn = fn
        self.deps = []
        self.signal = False
        self.tick = None
        self.dma_sem = None
        self.dma_val = None


class Prog:
    def __init__(self, nc):
        self.nc = nc
        self.ops = {e: [] for e in ENGS}
        self.res = {}
        self.dma_tot = {}

    def op(self, eng, fn, reads=(), writes=(), accum=False, dma_sem=None):
        o = Op(eng, fn)
        deps = []
        for r in reads:
            st = self.res.get(r)
            if st is not None and st[0] is not None:
                deps.append(("raw", st[0]))
            if st is not None and isinstance(r, tuple) and r[0] == "ps":
                for rd in st[1]:
                    if rd.eng != eng:
                        deps.append(("raw", rd))
        for w in writes:
            st = self.res.get(w)
            if st is not None:
                if st[0] is not None and not accum:
                    deps.append(("waw", st[0]))
                for rd in st[1]:
                    deps.append(("war", rd))
        seen = set()
        for kind, d in deps:
            if d is o or id(d) in seen:
                continue
            if d.eng == eng and d.dma_sem is None and dma_sem is None:
                if eng == "pe":
                    continue
            seen.add(id(d))
            o.deps.append(d)
        for r in reads:
            self.res.setdefault(r, [None, []])[1].append(o)
        for w in writes:
            st = self.res.setdefault(w, [None, []])
            st[0] = o
            st[1] = []
        if dma_sem is not None:
            o.dma_sem = dma_sem
            self.dma_tot[dma_sem] = self.dma_tot.get(dma_sem, 0) + 16
            o.dma_val = self.dma_tot[dma_sem]
        self.ops[eng].append(o)
        return o

    def emit(self, final_wait_ops=()):
        nc = self.nc
        for e in ENGS:
            for o in self.ops[e]:
                for d in o.deps:
                    d.signal = True
        for o in final_wait_ops:
            o.signal = True
        for e in ENGS:
            t = 0
            for o in self.ops[e]:
                if o.dma_sem is None and o.signal:
                    t += 1
                    o.tick = t
        with ExitStack() as es:
            sems = {}
            for e in ENGS:
                sems[e] = es.enter_context(nc.semaphore("s_" + e))
            for k in self.dma_tot:
                sems[("dma", k)] = es.enter_context(nc.semaphore("d_" + str(k)))
            block = es.enter_context(nc.Block())

            def run(eng_name, e):
                waited = {}
                for o in self.ops[eng_name]:
                    need = {}
                    for d in o.deps:
                        if d.dma_sem is not None:
                            key, val = ("dma", d.dma_sem), d.dma_val
                        else:
                            key, val = d.eng, d.tick
                        if need.get(key, 0) < val:
                            need[key] = val
                    for key, val in need.items():
                        if waited.get(key, 0) < val:
                            e.wait_ge(sems[key], val)
                            waited[key] = val
                    inst = o.fn(e)
                    if o.dma_sem is not None:
                        inst.then_inc(sems[("dma", o.dma_sem)], 16)
                    elif o.signal:
                        inst.then_inc(sems[eng_name], 1)
                if eng_name == "sp":
                    need = {}
                    for o in final_wait_ops:
                        if o.dma_sem is not None:
                            key, val = ("dma", o.dma_sem), o.dma_val
                        else:
                            key, val = o.eng, o.tick
                        need[key] = max(need.get(key, 0), val)
                    for key, val in need.items():
                        e.wait_ge(sems[key], val)

            @block.tensor
            def _(e):
                run("pe", e)

            @block.scalar
            def _(e):
                run("act", e)

            @block.vector
            def _(e):
                run("dve", e)

            @block.gpsimd
            def _(e):
                run("pool", e)

            @block.sync
            def _(e):
                run("sp", e)


def _aperm():
    idx = []
    for ac in range(4):
        g, j = ac // 2, ac % 2
        for h in (4 * g + j, 4 * g + 2 + j):
            idx += [h * 64 + d for d in range(64)]
    return np.array(idx)


def _qperm():
    idx = []
    for c in range(4):
        for h in (c, 4 + c):
            idx += [h * 64 + d for d in range(64)]
    return np.array(idx)


def _pool_op(L, w, n):
    M = np.zeros((n, n + 16), np.float64)
    for t in range(n):
        lo = min(max(t - w // 2, 0), L)
        hi = min(max(t + w // 2, 0), L)
        for s in range(lo, hi):
            if s < n + 16:
                M[t, s] += 1.0 / (hi - lo)
        M[t, t] -= 1.0
    return M


def _pool_tables(reverse):
    pd = np.zeros((128, 4, 4, 128), np.float32)
    pe = np.zeros((128, 4, 4, 16), np.float32)
    for g, w in enumerate((2, 4, 8, 16)):
        M = np.zeros((256, 256))
        for t in range(256):
            lo = min(max(t - w // 2, 0), 256)
            hi = min(max(t + w // 2, 0), 256)
            M[t, lo:hi] += 1.0 / (hi - lo)
            M[t, t] -= 1.0
        pd[:, 0, g, :] = M[0:128, 0:128].T
        pd[:, 1, g, :] = M[128:256, 128:256].T
        pe[:, 0, g, :] = M[128:144, 0:128].T
        pe[:, 1, g, :] = M[112:128, 128:256].T
        L = 2048
        Mg = np.zeros((L, L))
        for t in range(L):
            lo = min(max(t - w // 2, 0), L)
            hi = min(max(t + w // 2, 0), L)
            Mg[t, lo:hi] += 1.0 / (hi - lo)
            Mg[t, t] -= 1.0
        Ml = Mg[::-1, ::-1] if reverse else Mg
        pd[:, 2, g, :] = Ml[0:128, 0:128].T
        pd[:, 3, g, :] = Ml[128:256, 128:256].T
        pe[:, 2, g, :] = Ml[128:144, 0:128].T
        pe[:, 3, g, :] = Ml[240:256, 256:384].T
    return pd, pe


def _rope_tables(reverse):
    j = np.arange(1536)
    t = (2047 - j) if reverse else j
    row = (t // 64).astype(np.float32)
    col = (t % 64).astype(np.float32)
    inv = (10000.0 ** (-(np.arange(0, 32, 2, dtype=np.float32) / 32))).astype(np.float32)
    cos = np.zeros((128, 1536), np.float32)
    sin = np.zeros((128, 1536), np.float32)
    for p in range(128):
        d = p % 64
        pos = row if d < 32 else col
        ang = (pos * inv[d % 16]).astype(np.float32)
        cos[p] = np.cos(ang)
        sin[p] = np.sin(ang)
    return cos, sin


def _perm_matrix():
    pm = np.zeros((128, 128), np.float32)
    for jx in range(128):
        if jx % 32 < 16:
            pm[jx + 16, jx] = -1.0
        else:
            pm[jx - 16, jx] = 1.0
    return pm


def _masks():
    k = np.arange(128)[:, None]
    q = np.arange(128)[None, :]
    m = np.zeros((128, 2, 4, 128), np.float32)
    m[:, 0] = (k >= q)[:, None, :]
    m[:, 1] = (k <= q)[:, None, :]
    return m


def build_program():
    nc = bass.Bass("TRN2", target_bir_lowering=False)

    def din(name, shape, dt=F32):
        return nc.dram_tensor(name, list(shape), dt, kind="ExternalInput").ap()

    def dout(name, shape, dt=F32):
        return nc.dram_tensor(name, list(shape), dt, kind="ExternalOutput").ap()

    xp_d = din("xp", [512, D])
    xs_d = din("xs", [1536, D])
    ck_d = din("ck", [DEPTH, 256, 128])
    cv_d = din("cv", [DEPTH, 256, 128])
    wada_d = din("w_ada", [DEPTH, D, 3 * D])
    win_d = din("w_in", [DEPTH, D, 2304])
    wout_d = din("w_out", [DEPTH, D, D])
    wpool_d = din("w_pool", [DEPTH, 4, 128, 128])
    vecs_d = din("vecs", [256, 128])
    sink_d = din("sinkv", [1, 32])
    fn_d = din("fnorm", [1, D])
    identf_d = din("ident_f", [128, 128])
    identb_d = din("ident_b", [128, 128], BF16)
    perm_d = din("perm", [128, 128], BF16)
    cos_d = din("cosT", [128, 1536], BF16)
    sin_d = din("sinT", [128, 1536], BF16)
    mask_d = din("mask2", [128, 2, 128], BF16)
    pmd_d = din("pm_diag", [128, 4, 4, 128], BF16)
    pme_d = din("pm_edge", [128, 4, 4, 16], BF16)

    yp_d = dout("yp", [512, D])
    ys_d = dout("ys", [1024, D])
    nk_d = dout("nk", [DEPTH, 512, 128])
    nv_d = dout("nv", [DEPTH, 512, 128])
    if DEBUG_DUMP:
        dbg_d = dout("dbg", [DEPTH, 128, 8, NTOK])

    es = ExitStack()

    def sb(name, shape, dt=F32):
        return es.enter_context(nc.sbuf_tensor(name, list(shape), dt))

    psum = es.enter_context(nc.psum_tensor("psum", [128, 4096], F32))

    def bank(b, n=512, off=0):
        return psum[:, b * 512 + off: b * 512 + off + n]

    xT = sb("xT", [128, 8, NTOK])
    hT = sb("hT", [128, 8, NTOK], BF16)
    qT = sb("qT", [128, 4, NTOK], BF16)
    zT = sb("zT", [128, 4, NTOK], BF16)
    kT = sb("kT", [128, KT_COLS], BF16)
    NVA = 64 + 128 * 36
    vaug = sb("vaug", [128, NVA], BF16)
    ring = sb("ring", [128, NRING, 8, 512], BF16)
    AR = sb("arena", [128, 8192], BF16)
    zf = AR[:, 0:4096].bitcast(F32).rearrange("p (g t) -> p g t", g=4)
    stage = AR[:, 0:4096].bitcast(F32).rearrange("p (s t) -> p s t", s=2)
    pooledT = AR[:, 4096:6144].rearrange("p (g t) -> p g t", g=4)
    utm = AR[:, 6144:8192].rearrange("p (s t) -> p s t", s=4)
    PT = AR[:, 0:3072].rearrange("p (s t) -> p s t", s=3)
    atf = AR[:, 3072:5120].bitcast(F32).rearrange("p (a c t) -> p a c t", a=2, c=4)
    lnt = AR[:, 5120:7168].bitcast(F32).rearrange("p (a t) -> p a t", a=2)
    sqa = AR[:, 7168:7680].rearrange("p (c t) -> p c t", c=4)
    rb2 = AR[:, 7680:7936].bitcast(F32)
    vec_in = AR[:, 7168:7680].bitcast(F32).rearrange("p (s t) -> p s t", s=2)
    sq = sb("sq", [128, 4, 512], BF16)
    rb = sb("rb", [128, 512])
    tmpf = sb("tmpf", [128, 3, 512])
    q16 = sb("q16", [128, 2, 512], BF16)
    kvf = sb("kvf", [128, 256])
    ckt = sb("ckt", [128, 2, 128], BF16)
    cosT = sb("cosT_sb", [128, 1536], BF16)
    sinT = sb("sinT_sb", [128, 1536], BF16)
    mask2 = sb("mask2_sb", [128, 2, 128], BF16)
    pmd = sb("pmd_sb", [128, 4, 4, 128], BF16)
    pme = sb("pme_sb", [128, 4, 4, 16], BF16)
    wp = sb("wp_sb", [128, 2, 4, 128], BF16)
    identf = sb("identf_sb", [128, 128])
    identb = sb("identb_sb", [128, 128], BF16)
    onesb = sb("onesb", [128, 128], BF16)
    perm = sb("perm_sb", [128, 128], BF16)
    vec = sb("vec", [128, 256])
    esink = sb("esink", [128, 32])
    sT = sb("sT", [128, 8, 2], BF16)
    mods = sb("mods", [128, 2, 24, 2])
    gmul = sb("gmul", [128, 2, 8, 2])
    ssq = sb("ssq", [128, 8])
    selr = sb("selr", [1, 128], BF16)
    esb = sb("esb", [128, 2, 512], BF16)
    fnb = hT[:, 0, :].bitcast(F32)

    P = Prog(nc)
    cnt = {"dma": 0, "ring": 0, "ps_norm": 0, "evac": 0, "rope": 0}

    def MM(out, lhsT, rhs, start, stop, reads, writes, accum=False):
        return P.op("pe", lambda e: e.matmul(out, lhsT=lhsT, rhs=rhs, start=start, stop=stop),
                    reads=reads, writes=writes, accum=accum)

    def TR(out, in_, ident, reads, writes):
        return P.op("pe", lambda e: e.transpose(out=out, in_=in_, identity=ident), reads=reads, writes=writes)

    def ACT(out, in_, func, reads, writes, bias=None, scale=None, accum_out=None):
        kw = {}
        if bias is not None:
            kw["bias"] = bias
        if scale is not None:
            kw["scale"] = scale
        if accum_out is not None:
            kw["accum_out"] = accum_out
        return P.op("act", lambda e: e.activation(out=out, in_=in_, func=func, **kw), reads=reads, writes=writes)

    def ACOPY(out, in_, reads, writes):
        return P.op("act", lambda e: e.copy(out=out, in_=in_), reads=reads, writes=writes)

    def VCOPY(out, in_, reads, writes):
        return P.op("dve", lambda e: e.tensor_copy(out=out, in_=in_), reads=reads, writes=writes)

    def TT(out, in0, in1, op, reads, writes):
        return P.op("dve", lambda e: e.tensor_tensor(out=out, in0=in0, in1=in1, op=op), reads=reads, writes=writes)

    def STT(out, in0, scalar, in1, op0, op1, reads, writes):
        return P.op("dve", lambda e: e.scalar_tensor_tensor(out=out, in0=in0, scalar=scalar, in1=in1, op0=op0, op1=op1),
                    reads=reads, writes=writes)

    def TS(out, in0, scalar1, op0, reads, writes):
        return P.op("dve", lambda e: e.tensor_scalar(out=out, in0=in0, scalar1=scalar1, scalar2=None, op0=op0),
                    reads=reads, writes=writes)

    def DMA(eng, out, in_, reads, writes, sem):
        return P.op(eng, lambda e: e.dma_start(out=out, in_=in_), reads=reads, writes=writes, dma_sem=sem)

    def newsem(prefix):
        cnt["dma"] += 1
        return "%s%d" % (prefix, cnt["dma"])

    def c_bada(l):
        return vec[:, l * 24:(l + 1) * 24]

    def c_normw(l):
        return vec[:, 96 + l * 8: 96 + l * 8 + 8]

    def c_an(l, j):
        return vec[:, 128 + l * 4 + j: 128 + l * 4 + j + 1]

    def c_pn(l, j):
        return vec[:, 144 + l * 4 + j: 144 + l * 4 + j + 1]

    def c_psc(l, j):
        return vec[:, 160 + l * 4 + j: 160 + l * 4 + j + 1]

    def kx(cs, bs):
        return [("x", c, b) for c in cs for b in bs]

    def kh(cs, bs):
        return [("h", c, b) for c in cs for b in bs]

    def kq(cs, bs):
        return [("q", c, b) for c in cs for b in bs]

    def kz(cs, bs):
        return [("z", c, b) for c in cs for b in bs]

    def kk(bs):
        return [("k", b) for b in bs]

    def kv(bs):
        return [("v", b) for b in bs]

    def tok(b0, b1):
        return slice(b0 * 128, b1 * 128)

    DMA("sp", identf[:], identf_d, [], ["identf"], newsem("c"))
    DMA("sp", identb[:], identb_d, [], ["identb"], newsem("c"))
    DMA("sp", perm[:], perm_d, [], ["perm"], newsem("c"))
    DMA("sp", vec_in[:, 0, :], vecs_d[0:128, :], [], ["vec_in0"], newsem("c"))
    DMA("sp", vec_in[:, 1, :], vecs_d[128:256, :], [], ["vec_in1"], newsem("c"))
    DMA("sp", esink[:], sink_d.broadcast_to([128, 32]), [], ["esink"], newsem("c"))
    DMA("sp", cosT[:], cos_d, [], ["cos"], newsem("c"))
    DMA("sp", sinT[:], sin_d, [], ["sin"], newsem("c"))
    DMA("sp", mask2[:], mask_d, [], ["mask2"], newsem("c"))
    DMA("sp", pmd[:], pmd_d, [], ["pmd"], newsem("c"))
    DMA("sp", pme[:], pme_d, [], ["pme"], newsem("c"))
    P.op("dve", lambda e: e.memset(onesb[:], 1.0), writes=["onesb"])
    P.op("dve", lambda e: e.memset(selr[:, 0:64], 0.0), writes=["selr"])
    P.op("dve", lambda e: e.memset(selr[:, 64:128], 1.0), writes=["selr"])
    P.op("dve", lambda e: e.memset(vaug[:], 1.0), writes=kv(range(18)))

    for i in range(2):
        TR(bank(7, 128, i * 128), vec_in[:, i, :], identf[:], ["vec_in%d" % i, "identf"], [("ps", 7)])
    ACOPY(vec[:], bank(7, 256), [("ps", 7)], ["vec"])
    ACT(sT[:, :, 0], vec[:, 176:184], AF.Silu, ["vec"], ["sT"])
    ACT(sT[:, :, 1], vec[:, 184:192], AF.Silu, ["vec"], ["sT"])
    ACT(esink[:], esink[:], AF.Exp, ["esink"], ["esink"])

    def ring_load(src_ap, ncols, slot):
        src = src_ap.rearrange("(k p) c -> p k c", p=128)
        DMA("pool", ring[:, slot, :, 0:ncols], src, [], [("ring", slot)], "ring%d" % slot)
        return slot

    MODB = 3

    def emit_ada_granule(l, gi, slot):
        ring_load(wada_d[l][:, gi * 512:(gi + 1) * 512], 512, slot)
        pm = l % 2
        for jc in range(4):
            for k in range(8):
                MM(bank(7, 2, jc * 2), ring[:, slot, k, jc * 128:(jc + 1) * 128], sT[:, k, :], k == 0, k == 7,
                   [("ring", slot), "sT"], [("ps", 7)], accum=(k > 0))
        pv = bank(7, 8).rearrange("p (j v) -> p j v", v=2)
        for v in range(2):
            TT(mods[:, pm, gi * 4:(gi + 1) * 4, v], pv[:, :, v], vec[:, l * 24 + gi * 4: l * 24 + gi * 4 + 4], ALU.add,
               [("ps", 7), "vec"], [("mods", pm)])

    def emit_mods_finish(l):
        pm = l % 2
        for v in range(2):
            STT(gmul[:, pm, :, v], mods[:, pm, 8:16, v], 1.0, c_normw(l), ALU.add, ALU.mult,
                [("mods", pm), "vec"], [("gmul", pm)])

    stage3 = AR[:, 0:6144].bitcast(F32).rearrange("p (s t) -> p s t", s=3)

    def load_x_block(b):
        st = b % 3
        src = xp_d[b * 128:(b + 1) * 128, :] if b < 4 else xs_d[(b - 4) * 128:(b - 3) * 128, :]
        DMA("sp", stage3[:, st, :], src, [], [("stage", st)], "xin%d" % st)
        pb = (b % 3) * 2
        for c in range(8):
            TR(bank(pb + c // 4, 128, (c % 4) * 128), stage3[:, st, c * 128:(c + 1) * 128], identf[:],
               [("stage", st), "identf"], [("ps", pb + c // 4)])
        src_ps = psum[:, pb * 512: pb * 512 + 1024].rearrange("p (c t) -> p c t", c=8)
        if b % 2 == 0:
            ACOPY(xT[:, :, tok(b, b + 1)], src_ps, [("ps", pb), ("ps", pb + 1)], kx(range(8), [b]))
        else:
            VCOPY(xT[:, :, tok(b, b + 1)], src_ps, [("ps", pb), ("ps", pb + 1)], kx(range(8), [b]))

    xb = 0
    for gi in range(6):
        emit_ada_granule(0, gi, gi % 2)
        for _ in range(3 if gi < 4 else 2):
            if xb < NBLK:
                load_x_block(xb)
                xb += 1
    while xb < NBLK:
        load_x_block(xb)
        xb += 1
    emit_mods_finish(0)

    def units_of(nsamp):
        us = [(0, 4)]
        b = 4
        while b < 4 + nsamp:
            us.append((b, min(b + 4, 4 + nsamp)))
            b += 4
        return us

    def rsqrt_from_stats(ps_ap, dst, scale_div, rd_keys, key):
        ACT(dst, ps_ap, AF.Ln, rd_keys, [key], bias=EPS, scale=1.0 / scale_div)
        ACT(dst, dst, AF.Exp, [key], [key], scale=-0.5)

    def rope_evac(ps_b, n, lt0, dst_ap, dst_keys):
        i = cnt["rope"] % 2
        rbk = 4 + (cnt["rope"] % 4)
        cnt["rope"] += 1
        ACOPY(q16[:, i, 0:n], bank(ps_b, n), [("ps", ps_b)], [("q16", i)])
        MM(bank(rbk, n), perm[:], q16[:, i, 0:n], True, True, [("q16", i), "perm"], [("ps", rbk)])
        TT(tmpf[:, i, 0:n], bank(ps_b, n), cosT[:, lt0:lt0 + n], ALU.mult, [("ps", ps_b), "cos"], [("tmpf", i)])
        TT(tmpf[:, 2, 0:n], bank(rbk, n), sinT[:, lt0:lt0 + n], ALU.mult, [("ps", rbk), "sin"], [("tmpf", 2)])
        TT(dst_ap, tmpf[:, i, 0:n], tmpf[:, 2, 0:n], ALU.add, [("tmpf", i), ("tmpf", 2)], dst_keys)

    def f_slab(slot, col0, units, kchunks_fn, banks, evac_fn):
        for ui, (b0, b1) in enumerate(units):
            n = (b1 - b0) * 128
            pb = banks[ui]
            for k in range(8):
                rhs_ap, rkeys = kchunks_fn(k, b0, b1)
                MM(bank(pb, n), ring[:, slot, k, col0:col0 + 128], rhs_ap, k == 0, k == 7,
                   [("ring", slot)] + rkeys, [("ps", pb)], accum=(k > 0))
        for ui, (b0, b1) in enumerate(units):
            evac_fn(ui, b0, b1, banks[ui])

    def h_chunk(k, b0, b1):
        return hT[:, k, tok(b0, b1)], kh([k], range(b0, b1))

    def a_chunk(k, b0, b1):
        if k < 4:
            return qT[:, k, tok(b0, b1)], kq([k], range(b0, b1))
        return zT[:, k - 4, tok(b0, b1)], kz([k - 4], range(b0, b1))

    def vsel(b0):
        return 0 if b0 < 4 else 1

    out_ops = []

    def emit_layer(l):
        pm = l % 2
        ni = 12 - l
        nq = 11 - l
        units_in = units_of(ni)
        units_q = units_of(nq)
        blocks_in = list(range(4)) + list(range(4, 4 + ni))
        blocks_q = list(range(4)) + list(range(4, 4 + nq))

        DMA("pool", wp[:, pm], wpool_d[l].rearrange("g c d -> c g d"), [], [("wp", pm)], "wp%d" % pm)

        stage_gate(3 + 10 * l)
        def norm_A(b0, b1, pb):
            n = (b1 - b0) * 128
            for hf in range(2):
                ACT(sq[:, :, 0:n], xT[:, hf * 4:hf * 4 + 4, tok(b0, b1)], AF.Square,
                    kx(range(hf * 4, hf * 4 + 4), range(b0, b1)), ["sq"])
                for c4 in range(4):
                    c = hf * 4 + c4
                    MM(bank(pb, n), onesb[:], sq[:, c4, 0:n], c == 0, c == 7, ["sq", "onesb"], [("ps", pb)], accum=(c > 0))

        def norm_B(b0, b1, pb):
            n = (b1 - b0) * 128
            v = vsel(b0)
            rsqrt_from_stats(bank(pb, n), rb[:, 0:n], float(D), [("ps", pb)], "rb")
            for c in range(8):
                i = c % 2
                STT(tmpf[:, i, 0:n], xT[:, c, tok(b0, b1)], gmul[:, pm, c, v:v + 1], rb[:, 0:n], ALU.mult, ALU.mult,
                    kx([c], range(b0, b1)) + [("gmul", pm), "rb"], [("tmpf", i)])
                if c % 2 == 0:
                    ACT(hT[:, c, tok(b0, b1)], tmpf[:, i, 0:n], AF.Identity, [("tmpf", i), ("mods", pm)],
                        kh([c], range(b0, b1)), bias=mods[:, pm, c, v:v + 1], scale=1.0)
                else:
                    TS(hT[:, c, tok(b0, b1)], tmpf[:, i, 0:n], mods[:, pm, c, v:v + 1], ALU.add,
                       [("tmpf", i), ("mods", pm)], kh([c], range(b0, b1)))

        norm_sched = {}
        nU = len(units_in)

        def nA(u):
            norm_A(units_in[u][0], units_in[u][1], 6 + u % 2)

        def nB(u):
            norm_B(units_in[u][0], units_in[u][1], 6 + u % 2)

        nA(0)
        if nU > 1:
            nA(1)
        nB(0)
        for u in range(nU):
            lst = []
            if u + 2 < nU:
                lst.append(lambda u=u: nA(u + 2))
            if u + 1 < nU:
                lst.append(lambda u=u: nB(u + 1))
            norm_sched[units_in[u][0]] = lst

        stage_gate(4 + 10 * l)
        slot_u = ring_load(win_d[l][:, 1280:1792], 512, 0)
        targets = set(blocks_q)
        pooled_in_unit = {}

        def unit_index_q(tb):
            for ui, (b0, b1) in enumerate(units_q):
                if b0 <= tb < b1:
                    return ui
            return None

        pend = []

        def pool_stage2a(ui):
            b0, b1 = units_q[ui]
            n = (b1 - b0) * 128
            pk = [("pooledT", i) for i in range(b1 - b0)]
            for g in range(4):
                bk = 4 + g % 2
                MM(bank(bk, n), wp[:, pm, g, :], pooledT[:, g, 0:n], True, True, [("wp", pm)] + pk, [("ps", bk)])
                TS(zf[:, g, 0:n], bank(bk, n), c_psc(l, g), ALU.mult, [("ps", bk), "vec"], [("zf", g)])
            pend.append([1, lambda: pool_stage2b(ui)])

        def pool_stage2b(ui):
            b0, b1 = units_q[ui]
            n = (b1 - b0) * 128
            TT(sq[:, :, 0:n], zf[:, :, 0:n], zf[:, :, 0:n], ALU.mult, [("zf", g) for g in range(4)], ["sq"])
            for g in range(4):
                MM(bank(4, n), onesb[:], sq[:, g, 0:n], g == 0, g == 3, ["sq", "onesb"], [("ps", 4)], accum=(g > 0))
            rsqrt_from_stats(bank(4, n), rb[:, 0:n], 512.0, [("ps", 4)], "rb")
            TT(zT[:, :, tok(b0, b1)], zf[:, :, 0:n], rb[:, 0:n].unsqueeze(1).broadcast_to([128, 4, n]), ALU.mult,
               [("zf", g) for g in range(4)] + ["rb"], kz(range(4), range(b0, b1)))

        pcount = {"i": 0}

        def pool_target(tb):
            ui = unit_index_q(tb)
            b0, b1 = units_q[ui]
            if tb < 4:
                dtype_i = 0 if tb % 2 == 0 else 1
                prev_b = tb - 1 if tb % 2 == 1 else None
                next_b = tb + 1 if tb % 2 == 0 else None
                et_prev, et_next = 0, 1
            else:
                dtype_i = 2 if tb == 4 else 3
                prev_b = tb - 1 if tb > 4 else None
                next_b = tb + 1
                et_prev, et_next = 2, 3
            PB = 2 + pcount["i"] % 2
            pcount["i"] += 1
            for g in range(4):
                gs = slice(g * 128, (g + 1) * 128)
                MM(bank(PB, 128, g * 128), utm[:, tb % 4, gs], pmd[:, dtype_i, g, :], True,
                   (prev_b is None and next_b is None), [("utm", tb % 4), "pmd"], [("ps", PB)], accum=(g > 0))
                if prev_b is not None:
                    MM(bank(PB, 16, g * 128), utm[:, prev_b % 4, gs], pme[:, et_prev, g, :], False, next_b is None,
                       [("utm", prev_b % 4), "pme"], [("ps", PB)], accum=True)
                if next_b is not None:
                    MM(bank(PB, 16, g * 128 + 112), utm[:, next_b % 4, gs], pme[:, et_next, g, :], False, True,
                       [("utm", next_b % 4), "pme"], [("ps", PB)], accum=True)
            off = (tb - b0) * 128
            VCOPY(pooledT[:, :, off:off + 128], bank(PB).rearrange("p (g t) -> p g t", g=4), [("ps", PB)],
                  [("pooledT", tb - b0)])
            pooled_in_unit[ui] = pooled_in_unit.get(ui, 0) + 1
            if pooled_in_unit[ui] == b1 - b0:
                pend.append([1, lambda: pool_stage2a(ui)])

        def run_pending(flush=False):
            while True:
                due = [p for p in pend if p[0] <= 0 or flush]
                if not due:
                    break
                p = due[0]
                pend.remove(p)
                p[1]()
            for p in pend:
                p[0] -= 1

        for bi, b in enumerate(blocks_in):
            pb = bi % 2
            for k in range(8):
                MM(bank(pb), hT[:, k, tok(b, b + 1)], ring[:, slot_u, k, :], k == 0, k == 7,
                   [("ring", slot_u)] + kh([k], [b]), [("ps", pb)], accum=(k > 0))
            ACOPY(utm[:, b % 4, :], bank(pb), [("ps", pb)], [("utm", b % 4)])
            for fn in norm_sched.get(b, []):
                fn()
            run_pending()
            if b < 4:
                if b % 2 == 1:
                    pend.append([0, lambda b=b: (pool_target(b - 1), pool_target(b))])
            elif b - 1 >= 4 and (b - 1) in targets:
                pend.append([0, lambda b=b: pool_target(b - 1)])
        stage_gate(5.1 + 10 * l)
        slot_kv = ring_load(win_d[l][:, 512:768], 256, 1)
        DMA("pool", ckt[:], ck_d[l].rearrange("(b p) f -> p b f", p=128), [], ["ckt"], "ckt")
        for cb in range(2):
            e0 = 2 * (16 + cb)
            dstv = vaug[:, 64 + 128 * e0: 64 + 128 * e0 + 256].rearrange("p (g x) -> p g x", g=2)[:, :, 0:64]
            srcv = cv_d[l][cb * 128:(cb + 1) * 128, :].rearrange("p (g d) -> p g d", g=2)
            DMA("pool", dstv, srcv, [], kv([16 + cb]), "cv%d" % cb)

        def kv_block(bi, b):
            pb = bi % 2
            if b < 4:
                for k in range(8):
                    MM(bank(pb, 256), hT[:, k, tok(b, b + 1)], ring[:, slot_kv, k, 0:256], k == 0, k == 7,
                       [("ring", slot_kv)] + kh([k], [b]), [("ps", pb)], accum=(k > 0))
                if b % 2 == 0:
                    kbuf, kkey = kvf[:], "kvf"
                else:
                    kbuf, kkey = tmpf[:, 2, 0:256], ("tmpf", 2)
                ACOPY(kbuf, bank(pb, 256), [("ps", pb)], [kkey])
                out_ops.append(DMA("sp", nk_d[l][b * 128:(b + 1) * 128, :], kbuf[:, 0:128], [kkey], [], "okv%d" % (b % 2)))
                out_ops.append(DMA("sp", nv_d[l][b * 128:(b + 1) * 128, :], kbuf[:, 128:256], [kkey], [], "okv%d" % (b % 2)))
                vsrc = bank(pb, 128, 128).rearrange("p (g x) -> p g x", g=2)
            else:
                for k in range(8):
                    MM(bank(pb, 128), hT[:, k, tok(b, b + 1)], ring[:, slot_kv, k, 128:256], k == 0, k == 7,
                       [("ring", slot_kv)] + kh([k], [b]), [("ps", pb)], accum=(k > 0))
                vsrc = bank(pb, 128).rearrange("p (g x) -> p g x", g=2)
            e0 = 2 * b
            dstv = vaug[:, 64 + 128 * e0: 64 + 128 * e0 + 256].rearrange("p (g x) -> p g x", g=2)[:, :, 0:64]
            VCOPY(dstv, vsrc, [("ps", pb)], kv([b]))

        for bi, b in enumerate(blocks_in[:4]):
            kv_block(bi, b)
        run_pending(flush=True)
        ctb = bank(7, 128).bitcast(BF16)
        for cb in range(2):
            TR(ctb[:, cb * 128:(cb + 1) * 128], ckt[:, cb, :], identb[:], ["ckt", "identb"], [("ps", 7)])
        ACOPY(kT[:, NTOK:NTOK + 256], ctb, [("ps", 7)], kk([16, 17]))
        for bi, b in enumerate(blocks_in):
            if bi >= 4:
                kv_block(bi, b)

        stage_gate(5.4 + 10 * l)

        def evac_k(ui, b0, b1, pb):
            n = (b1 - b0) * 128
            if b0 < 4:
                ACOPY(kT[:, tok(b0, b1)], bank(pb, n), [("ps", pb)], kk(range(b0, b1)))
            else:
                rope_evac(pb, n, (b0 - 4) * 128, kT[:, tok(b0, b1)], kk(range(b0, b1)))
        f_slab(slot_kv, 0, units_in, h_chunk, [0, 1, 2, 3], evac_k)

        stage_gate(5 + 10 * l)
        slot_gp = ring_load(win_d[l][:, 1792:2304], 512, 0)
        for j in range(4):
            banks = [0, 1, 2, 3] if j % 2 == 0 else [4, 5, 6, 7]

            def evac_gp(ui, b0, b1, pb, j=j):
                n = (b1 - b0) * 128
                i = cnt["evac"] % 2
                cnt["evac"] += 1
                ACT(q16[:, i, 0:n], bank(pb, n), AF.Silu, [("ps", pb)], [("q16", i)])
                STT(zT[:, j, tok(b0, b1)], zT[:, j, tok(b0, b1)], c_pn(l, j), q16[:, i, 0:n], ALU.mult, ALU.mult,
                    kz([j], range(b0, b1)) + [("q16", i), "vec"], kz([j], range(b0, b1)))
            f_slab(slot_gp, j * 128, units_q, h_chunk, banks, evac_gp)

        stage_gate(7 + 10 * l)
        slot_q = ring_load(win_d[l][:, 0:512], 512, 1)
        for c in range(4):
            def evac_q(ui, b0, b1, pb, c=c):
                n = (b1 - b0) * 128
                if b0 < 4:
                    ACOPY(qT[:, c, tok(b0, b1)], bank(pb, n), [("ps", pb)], kq([c], range(b0, b1)))
                else:
                    rope_evac(pb, n, (b0 - 4) * 128, qT[:, c, tok(b0, b1)], kq([c], range(b0, b1)))
            f_slab(slot_q, c * 128, units_q, h_chunk, [0, 1, 2, 3], evac_q)
            if c == 1 and l + 1 < DEPTH:
                emit_ada_granule(l + 1, 0, 0)

        stage_gate(8 + 10 * l)
        VCOPY(esb[:].rearrange("p g (h q) -> p (g h) q", h=4), esink[:, l * 8:(l + 1) * 8].unsqueeze(2).broadcast_to([128, 8, 128]),
              ["esink"], ["esb"])
        ada_pending = [1] if l + 1 < DEPTH else []
        ptasks = []
        for qi, qb in enumerate(blocks_q):
            if qb < 4:
                s0 = (qb // 2) * 2
                keys = [(s0, None), (s0 + 1, None)]
            else:
                keys = []
                if qb > 4:
                    keys.append((qb - 1, 0))
                keys.append((qb, None))
                keys.append((qb + 1, 1))
                keys += [(16, None), (17, None)]
            ptasks.append(dict(qi=qi, qb=qb, keys=keys, ob=((4, 5) if qi % 2 == 0 else (6, 7)), ab=qi % 2))
        psteps = [(ti, ki) for ti, t in enumerate(ptasks) for ki in range(len(t["keys"]))]
        last_ps = {}
        for si_, (ti_, ki_) in enumerate(psteps):
            last_ps[ti_] = si_
        deferred = []

        def emit_qk(si):
            ti, ki = psteps[si]
            t = ptasks[ti]
            qb = t["qb"]
            kb, mtype = t["keys"][ki]
            sp = si % 2
            pt = si % 3
            for g in range(2):
                lo, hi = 64 * g, 64 * g + 64
                MM(bank(2 * sp + g), kT[lo:hi, kb * 128:(kb + 1) * 128], qT[lo:hi, :, tok(qb, qb + 1)], True, True,
                   kk([kb]) + kq(range(4), [qb]), [("ps", 2 * sp + g)])
            ACT(PT[:, pt, :], psum[:, 2 * sp * 512:(2 * sp + 2) * 512], AF.Exp, [("ps", 2 * sp), ("ps", 2 * sp + 1)],
                [("PT", pt)], scale=0.125)
            if mtype is not None:
                ptv = PT[:, pt, :].rearrange("p (h q) -> p h q", h=8)
                TT(ptv, ptv, mask2[:, mtype, :].unsqueeze(1).broadcast_to([128, 8, 128]), ALU.mult,
                   [("PT", pt), "mask2"], [("PT", pt)])

        def emit_unit_rmsnorm(ub0, ub1):
            n = (ub1 - ub0) * 128
            uk = kq(range(4), range(ub0, ub1))
            TT(sq[:, :, 0:n], qT[:, :, tok(ub0, ub1)], qT[:, :, tok(ub0, ub1)], ALU.mult, uk, ["sq"])
            for c in range(4):
                MM(bank(3, n), onesb[:], sq[:, c, 0:n], c == 0, c == 3, ["sq", "onesb"], [("ps", 3)], accum=(c > 0))
            rsqrt_from_stats(bank(3, n), rb[:, 0:n], 512.0, [("ps", 3)], "rb")
            TT(qT[:, :, tok(ub0, ub1)], qT[:, :, tok(ub0, ub1)], rb[:, 0:n].unsqueeze(1).broadcast_to([128, 4, n]), ALU.mult,
               uk + ["rb"], uk)

        def emit_pv(si):
            ti, ki = psteps[si]
            t = ptasks[ti]
            qb, ab = t["qb"], t["ab"]
            kb, mtype = t["keys"][ki]
            pt = si % 3
            first, last = (ki == 0), (ki == len(t["keys"]) - 1)
            for g in range(2):
                ob = t["ob"][g]
                en = 2 * kb + g
                MM(bank(ob), vaug[:, 64 + 128 * en: 64 + 128 * en + 128], PT[:, pt, g * 512:(g + 1) * 512], first, last,
                   kv([kb]) + [("PT", pt)], [("ps", ob)], accum=(not first))
            if not last:
                return
            for g in range(2):
                ob = t["ob"][g]
                TT(lnt[64:128, g, :], bank(ob)[64:128, :], esb[64:128, g, :], ALU.add, [("ps", ob), "esb"], [("lnt", g)])

            def finish2(t=t, qb=qb):
                ACT(lnt[64:128, :, :], lnt[64:128, :, :], AF.Ln, [("lnt", 0), ("lnt", 1)], [("lnt", 0), ("lnt", 1)])
                ACT(lnt[64:128, :, :], lnt[64:128, :, :], AF.Exp, [("lnt", 0), ("lnt", 1)], [("lnt", 0), ("lnt", 1)], scale=-1.0)
                for g in range(2):
                    ob = t["ob"][g]
                    TT(qT[0:64, 2 * g:2 * g + 2, tok(qb, qb + 1)], bank(ob, 256)[0:64, :].rearrange("p (j t) -> p j t", j=2),
                       lnt[64:128, g, 0:256].rearrange("p (j t) -> p j t", j=2), ALU.mult,
                       [("ps", ob), ("lnt", g)], kq([2 * g, 2 * g + 1], [qb]))
                    TT(qT[64:128, 2 * g:2 * g + 2, tok(qb, qb + 1)], bank(ob, 256, 256)[0:64, :].rearrange("p (j t) -> p j t", j=2),
                       lnt[64:128, g, 256:512].rearrange("p (j t) -> p j t", j=2), ALU.mult,
                       [("ps", ob), ("lnt", g)], kq([2 * g, 2 * g + 1], [qb]))
            deferred.append((si + 1, finish2))
            for (ub0, ub1) in units_q:
                if qb == ub1 - 1:
                    deferred.append((si + 3, lambda ub0=ub0, ub1=ub1: emit_unit_rmsnorm(ub0, ub1)))
            if ada_pending and t["qi"] >= 3:
                gi = ada_pending.pop(0)
                deferred.append((si + 2, lambda gi=gi: emit_ada_granule(l + 1, gi, 1)))

        nst = len(psteps)
        PIPE = 2
        for si in range(nst + PIPE):
            if si < nst:
                emit_qk(si)
            if si >= PIPE:
                cur = si - PIPE
                emit_pv(cur)
                for d in [d for d in deferred if d[0] <= cur]:
                    deferred.remove(d)
                    d[1]()
        while ada_pending:
            emit_ada_granule(l + 1, ada_pending.pop(0), 1)

        stage_gate(9 + 10 * l)
        slot_ga = ring_load(win_d[l][:, 768:1280], 512, 0)

        def ga_mm(j, ulist, banks):
            for ui, (b0, b1) in ulist:
                n = (b1 - b0) * 128
                pb = banks[ui]
                for k in range(8):
                    MM(bank(pb, n), ring[:, slot_ga, k, j * 128:(j + 1) * 128], hT[:, k, tok(b0, b1)], k == 0, k == 7,
                       [("ring", slot_ga)] + kh([k], range(b0, b1)), [("ps", pb)], accum=(k > 0))

        def evac_ga(ui, b0, b1, pb, j):
            n = (b1 - b0) * 128
            i = cnt["evac"] % 2
            cnt["evac"] += 1
            ACT(q16[:, i, 0:n], bank(pb, n), AF.Silu, [("ps", pb)], [("q16", i)])
            STT(qT[:, j, tok(b0, b1)], qT[:, j, tok(b0, b1)], c_an(l, j), q16[:, i, 0:n], ALU.mult, ALU.mult,
                kq([j], range(b0, b1)) + [("q16", i), "vec"], kq([j], range(b0, b1)))

        ulist = list(enumerate(units_q))
        ga_mm(0, ulist[:3], [0, 1, 2, 3])
        for d in list(deferred):
            d[1]()
        deferred.clear()
        if l + 1 < DEPTH:
            emit_ada_granule(l + 1, 2, 1)
        ga_mm(0, ulist[3:], [0, 1, 2, 3])
        for ui, (b0, b1) in ulist:
            evac_ga(ui, b0, b1, [0, 1, 2, 3][ui], 0)
        for j in range(1, 4):
            banks = [0, 1, 2, 3] if j % 2 == 0 else [4, 5, 6, 7]
            f_slab(slot_ga, j * 128, units_q, h_chunk, banks, lambda ui, b0, b1, pb, j=j: evac_ga(ui, b0, b1, pb, j))

        stage_gate(10 + 10 * l)
        for half in range(2):
            slot_o = ring_load(wout_d[l][:, half * 512:(half + 1) * 512], 512, 1 - half)
            for cc in range(4):
                c = half * 4 + cc
                banks = [0, 1, 2, 3] if cc % 2 == 0 else [4, 5, 6, 7]
                if l + 1 < DEPTH and cc == 2:
                    emit_ada_granule(l + 1, 3 + half, half)

                def evac_o(ui, b0, b1, pb, c=c):
                    n = (b1 - b0) * 128
                    v = vsel(b0)
                    STT(xT[:, c, tok(b0, b1)], bank(pb, n), mods[:, pm, 16 + c, v:v + 1], xT[:, c, tok(b0, b1)],
                        ALU.mult, ALU.add, [("ps", pb), ("mods", pm)] + kx([c], range(b0, b1)), kx([c], range(b0, b1)))
                f_slab(slot_o, cc * 128, units_q, a_chunk, banks, evac_o)
        if l + 1 < DEPTH:
            emit_ada_granule(l + 1, 5, 1)
            emit_mods_finish(l + 1)

        if DEBUG_DUMP:
            out_ops.append(DMA("sp", dbg_d[l], xT[:], kx(range(8), range(NBLK)), [], "dbg"))

    def emit_epilogue():
        DMA("sp", fnb, fn_d.broadcast_to([128, D]), [], kh([0], range(NBLK)), "fnb")
        fnk = kh([0], range(NBLK))
        own = list(range(4)) + list(range(4, 12))
        stage4 = AR[:, 0:8192].bitcast(F32).rearrange("p (s t) -> p s t", s=4)
        for i, b in enumerate(own):
            pb = (i % 4) * 2
            st = i % 4
            for c in range(8):
                TR(bank(pb + c // 4, 128, (c % 4) * 128), xT[:, c, tok(b, b + 1)], identf[:], kx([c], [b]) + ["identf"],
                   [("ps", pb + c // 4)])
            for hb in range(2):
                ACT(stage4[:, st, hb * 512:(hb + 1) * 512], bank(pb + hb), AF.Square, [("ps", pb + hb)],
                    [("stage", st), ("ssq", st, hb)], accum_out=ssq[:, 2 * st + hb: 2 * st + hb + 1])
            s0 = ssq[:, 2 * st:2 * st + 1]
            TT(s0, s0, ssq[:, 2 * st + 1:2 * st + 2], ALU.add, [("ssq", st, 0), ("ssq", st, 1)], [("ssq", st, 0)])
            ACT(s0, s0, AF.Ln, [("ssq", st, 0)], [("ssq", st, 0)], bias=EPS, scale=1.0 / D)
            ACT(s0, s0, AF.Exp, [("ssq", st, 0)], [("ssq", st, 0)], scale=-0.5)
            for hb in range(2):
                STT(stage4[:, st, hb * 512:(hb + 1) * 512], bank(pb + hb), s0, fnb[:, hb * 512:(hb + 1) * 512], ALU.mult, ALU.mult,
                    [("ps", pb + hb), ("ssq", st, 0)] + fnk, [("stage", st)])
            dst = yp_d[b * 128:(b + 1) * 128, :] if b < 4 else ys_d[(b - 4) * 128:(b - 3) * 128, :]
            out_ops.append(DMA("sp", dst, stage4[:, st, :], [("stage", st)], [], "yout%d" % st))


    try:
        for l in range(DEPTH):
            emit_layer(l)
        stage_gate(50)
        emit_epilogue()
    except StopBuild:
        pass

    P.emit(final_wait_ops=out_ops)
    es.close()
    return nc


_CACHE = {}


def _prep_shared(inp):
    qp, ap = _qperm(), _aperm()
    w_in = np.asarray(inp["w_in"], np.float32)
    colperm = np.concatenate([qp, np.arange(512, 768), 768 + ap, np.arange(1280, 2304)])
    w_in_p = np.ascontiguousarray(w_in[:, :, colperm])
    w_out = np.asarray(inp["w_out"], np.float32)
    rowperm = np.concatenate([ap, np.arange(512, 1024)])
    w_out_p = np.ascontiguousarray(w_out[:, rowperm, :])
    attn_norm_p = np.asarray(inp["attn_norm"], np.float32)[:, ap]
    return w_in_p, w_out_p, attn_norm_p


def kernel(x_prompt, x_sample, cache_k, cache_v, c, c_ctx, norm_w, w_ada, b_ada, w_in, sink,
           attn_norm, pool_norm, w_pool, pool_scale, w_out, final_norm):
    f32 = np.float32
    inp = dict(w_in=w_in, w_out=w_out, attn_norm=attn_norm)
    w_in_p, w_out_p, attn_norm_p = _prep_shared(inp)
    x_prompt = np.asarray(x_prompt, f32)
    x_sample = np.asarray(x_sample, f32)
    cache_k = np.asarray(cache_k, f32)
    cache_v = np.asarray(cache_v, f32)
    c = np.asarray(c, f32)
    c_ctx = np.asarray(c_ctx, f32)
    w_ada = np.ascontiguousarray(np.asarray(w_ada, f32))
    w_pool = np.ascontiguousarray(np.asarray(w_pool, f32))
    b_ada = np.asarray(b_ada, f32)
    norm_w = np.asarray(norm_w, f32)
    pool_norm = np.asarray(pool_norm, f32)
    pool_scale = np.asarray(pool_scale, f32)
    sink = np.asarray(sink, f32)
    final_norm = np.asarray(final_norm, f32)

    bf = ml_dtypes.bfloat16
    ident = np.eye(128, dtype=f32)
    consts = {}
    for rev in (False, True):
        cos, sin = _rope_tables(rev)
        pd, pe = _pool_tables(rev)
        consts[rev] = dict(cosT=cos.astype(bf), sinT=sin.astype(bf), pm_diag=pd.astype(bf), pm_edge=pe.astype(bf))
    mask2 = np.ascontiguousarray(_masks()[:, :, 0, :]).astype(bf)
    permm = _perm_matrix().astype(bf)

    in_maps = []
    for i in range(NCORES):
        b, half = i // 2, i % 2
        rev = half == 1
        if not rev:
            xs = x_sample[b, 0:1536]
        else:
            xs = x_sample[b, ::-1][0:1536]
        vecs = np.zeros((256, 128), f32)
        vecs[0:96] = b_ada.reshape(DEPTH * 24, 128)
        vecs[96:128] = norm_w.reshape(DEPTH * 8, 128)
        vecs[128:144] = attn_norm_p.reshape(DEPTH * 4, 128)
        vecs[144:160] = pool_norm.reshape(DEPTH * 4, 128)
        vecs[160:176] = pool_scale.reshape(DEPTH * 4, 128)
        vecs[176:184] = c_ctx.reshape(8, 128)
        vecs[184:192] = c[b].reshape(8, 128)
        m = dict(
            xp=np.ascontiguousarray(x_prompt[2 * i:2 * i + 2].reshape(512, D)),
            xs=np.ascontiguousarray(xs),
            ck=np.ascontiguousarray(cache_k[b].reshape(DEPTH, 256, 128)),
            cv=np.ascontiguousarray(cache_v[b].reshape(DEPTH, 256, 128)),
            w_ada=w_ada, w_in=w_in_p, w_out=w_out_p, w_pool=w_pool,
            vecs=vecs, sinkv=np.ascontiguousarray(sink.reshape(1, 32)),
            fnorm=np.ascontiguousarray(final_norm.reshape(1, D)),
            ident_f=ident, ident_b=ident.astype(bf), perm=permm, mask2=mask2,
            **consts[rev],
        )
        in_maps.append(m)

    if "nc" not in _CACHE:
        _CACHE["nc"] = build_program()
    nc = _CACHE["nc"]
    res = run_bass_kernel_spmd(nc, in_maps, core_ids=list(range(NCORES)))
    outs = res.results

    y_prompt = np.zeros((16, 256, D), f32)
    y_sample = np.zeros((4, 2048, D), f32)
    new_k = np.zeros((16, DEPTH, 256, 2, 64), f32)
    new_v = np.zeros((16, DEPTH, 256, 2, 64), f32)
    for i in range(NCORES):
        b, half = i // 2, i % 2
        r = outs[i]
        y_prompt[2 * i:2 * i + 2] = np.asarray(r["yp"]).reshape(2, 256, D)
        ys = np.asarray(r["ys"])
        if half == 0:
            y_sample[b, 0:1024] = ys
        else:
            y_sample[b, 1024:2048] = ys[::-1]
        nk = np.asarray(r["nk"]).reshape(DEPTH, 2, 256, 2, 64)
        nv = np.asarray(r["nv"]).reshape(DEPTH, 2, 256, 2, 64)
        new_k[2 * i:2 * i + 2] = nk.transpose(1, 0, 2, 3, 4)
        new_v[2 * i:2 * i + 2] = nv.transpose(1, 0, 2, 3, 4)
    if DEBUG_DUMP:
        kernel.dbg = [np.asarray(o["dbg"]) for o in outs]
    return (y_prompt, y_sample, new_k, new_v)
```
